# Optimizing a Trainium2 kernel written in Bass

```python
import math
import jax, jax.numpy as jnp
from jax import lax
import numpy as np

D_MODEL = 2048
BATCH = 1
SEQ = 16384
DEPTH = 1
DEC_BATCH = 16
DEC_SEQ = 16
PAST_LEN = 2048

CHUNK = 64
Q_BLOCK = 128
N_HEADS = 8
QK_HALF = 64
V_HEAD = 2 * QK_HALF
ATTN_QK_WIDTH = N_HEADS * 2 * QK_HALF
ATTN_V_WIDTH = N_HEADS * V_HEAD
POOL_WINDOWS = (2, 4, 8, 16)
POOL_GROUPS = len(POOL_WINDOWS)
POOL_WIDTH = D_MODEL // 2
POOL_GROUP_DIM = POOL_WIDTH // POOL_GROUPS
POOL_HIST = max(POOL_WINDOWS) - 1
IN_COLS = 2 * ATTN_QK_WIDTH + ATTN_V_WIDTH + POOL_WIDTH + 2 * D_MODEL
CONV_WIDTH = 3
D_FF = 5504
EPS = 1e-6
NEG_INF = -1e30

kernel_name = "hybrid_diffattn_pool_convglu_stream_step"


def rms_norm(x, g):
    xf = x.astype(jnp.float32)
    y = xf * lax.rsqrt(jnp.mean(xf * xf, axis=-1, keepdims=True) + EPS)
    return (y * g.astype(jnp.float32)).astype(x.dtype)


def diff_attn(q, k, v, q_pos, k_pos, lam):
    s = jnp.einsum('bqhcd,bkhcd->bhcqk', q.astype(jnp.float32), k.astype(jnp.float32)) * (QK_HALF ** -0.5)
    mask = (k_pos[None, :] // CHUNK) <= (q_pos[:, None] // CHUNK)
    s = jnp.where(mask, s, NEG_INF)
    p = jax.nn.softmax(s, axis=-1)
    a = p[:, :, 0] - lam * p[:, :, 1]
    return jnp.einsum('bhqk,bkhd->bqhd', a, v.astype(jnp.float32))


def diff_attn_blocked(q, k, v, lam):
    B, S = q.shape[0], q.shape[1]
    nb = S // Q_BLOCK
    qb = q.reshape(B, nb, Q_BLOCK, N_HEADS, 2, QK_HALF).transpose(1, 0, 2, 3, 4, 5)
    qpos = jnp.arange(S).reshape(nb, Q_BLOCK)
    kpos = jnp.arange(S)

    def one(args):
        qblk, qp = args
        return diff_attn(qblk, k, v, qp, kpos, lam)

    o = lax.map(one, (qb, qpos))
    return o.transpose(1, 0, 2, 3, 4).reshape(B, S, N_HEADS, V_HEAD)


def pool_mix(u_ext, start, w_pool, pool_scale):
    B, T_ext = u_ext.shape[0], u_ext.shape[1]
    T = T_ext - POOL_HIST
    uf = u_ext.astype(jnp.float32).reshape(B, T_ext, POOL_GROUPS, POOL_GROUP_DIM)
    cs = jnp.concatenate([jnp.zeros((B, 1, POOL_GROUPS, POOL_GROUP_DIM), jnp.float32),
                          jnp.cumsum(uf, axis=1)], axis=1)
    cur = uf[:, POOL_HIST:]
    pos = start + jnp.arange(T)
    outs = []
    for g, w in enumerate(POOL_WINDOWS):
        win = cs[:, POOL_HIST + 1:POOL_HIST + 1 + T, g] - cs[:, POOL_HIST + 1 - w:POOL_HIST + 1 - w + T, g]
        cnt = jnp.minimum(w, pos + 1).astype(jnp.float32)[None, :, None]
        outs.append(win / cnt - cur[:, :, g])
    p = jnp.stack(outs, axis=2)
    y = jnp.einsum('btgc,gcd->btgd', p, w_pool.astype(jnp.float32)).reshape(B, T, POOL_WIDTH)
    return (y * pool_scale.astype(jnp.float32)).astype(u_ext.dtype)


def layer(x, start, k_hist, v_hist, pool_hist, conv_hist, lam_init,
          attn_norm_g, w_in, q_norm_g, k_norm_g, lambda_q1, lambda_k1, lambda_q2, lambda_k2,
          subln_g, w_pool, pool_scale, w_up_attn, w_up_pool, w_out,
          ffn_norm_g, w_ffn_in, conv_w, conv_b, w_ffn_out):
    B, T = x.shape[0], x.shape[1]
    h = rms_norm(x, attn_norm_g)
    proj = h @ w_in
    o0 = ATTN_QK_WIDTH
    o1 = 2 * ATTN_QK_WIDTH
    o2 = o1 + ATTN_V_WIDTH
    o3 = o2 + POOL_WIDTH
    o4 = o3 + D_MODEL
    q = rms_norm(proj[..., :o0].reshape(B, T, N_HEADS, 2, QK_HALF), q_norm_g)
    k = rms_norm(proj[..., o0:o1].reshape(B, T, N_HEADS, 2, QK_HALF), k_norm_g)
    v = proj[..., o1:o2].reshape(B, T, N_HEADS, V_HEAD)
    u = proj[..., o2:o3]
    g_a = proj[..., o3:o4]
    g_b = proj[..., o4:]

    lam = (jnp.exp(jnp.sum(lambda_q1.astype(jnp.float32) * lambda_k1.astype(jnp.float32)))
           - jnp.exp(jnp.sum(lambda_q2.astype(jnp.float32) * lambda_k2.astype(jnp.float32)))
           + lam_init)
    if k_hist is None:
        o = diff_attn_blocked(q, k, v, lam)
    else:
        P = k_hist.shape[1]
        keys = jnp.concatenate([k_hist.reshape(B, P, N_HEADS, 2, QK_HALF).astype(k.dtype), k], axis=1)
        vals = jnp.concatenate([v_hist.astype(v.dtype), v], axis=1)
        o = diff_attn(q, keys, vals, start + jnp.arange(T), jnp.arange(P + T), lam)
    o = o * lax.rsqrt(jnp.mean(o * o, axis=-1, keepdims=True) + EPS)
    o = (o * subln_g.astype(jnp.float32) * (1.0 - lam_init)).astype(x.dtype).reshape(B, T, ATTN_V_WIDTH)

    u_ext = jnp.concatenate([pool_hist.astype(u.dtype), u], axis=1)
    pb = pool_mix(u_ext, start, w_pool, pool_scale)

    m = jax.nn.sigmoid(g_a) * (o @ w_up_attn) + jax.nn.sigmoid(g_b) * (pb @ w_up_pool)
    x1 = x + m @ w_out

    h2 = rms_norm(x1, ffn_norm_g)
    gv = h2 @ w_ffn_in
    z, val = gv[..., :D_FF], gv[..., D_FF:]
    z_ext = jnp.concatenate([conv_hist.astype(z.dtype), z], axis=1)
    zc = conv_b + z_ext[:, 0:T] * conv_w[0] + z_ext[:, 1:T + 1] * conv_w[1] + z_ext[:, 2:T + 2] * conv_w[2]
    y = x1 + (jax.nn.silu(zc) * val) @ w_ffn_out

    k_rows = k.reshape(B, T, N_HEADS, 2 * QK_HALF)
    return y, k_rows, v, u_ext[:, -POOL_HIST:], z_ext[:, -(CONV_WIDTH - 1):]


def setup_inputs(seed: int = 0) -> dict:
    key = jax.random.key(seed)
    ks = jax.random.split(key, 32)
    f32 = jnp.float32
    nrm = lambda k, shape, s: jax.random.normal(k, shape, f32) * s
    D = D_MODEL
    return {
        "x_prompt": nrm(ks[0], (BATCH, SEQ, D), 1.0),
        "x_sample": nrm(ks[1], (DEC_BATCH, DEC_SEQ, D), 1.0),
        "cache_k": nrm(ks[2], (DEPTH, DEC_BATCH, PAST_LEN, N_HEADS, 2 * QK_HALF), 1.0),
        "cache_v": nrm(ks[3], (DEPTH, DEC_BATCH, PAST_LEN, N_HEADS, V_HEAD), 1.0),
        "state_pool": nrm(ks[4], (DEPTH, DEC_BATCH, POOL_HIST, POOL_WIDTH), 1.0),
        "state_conv": nrm(ks[5], (DEPTH, DEC_BATCH, CONV_WIDTH - 1, D_FF), 1.0),
        "attn_norm_g": 1.0 + nrm(ks[6], (DEPTH, D), 0.05),
        "w_in": nrm(ks[7], (DEPTH, D, IN_COLS), D ** -0.5),
        "q_norm_g": 1.0 + nrm(ks[8], (DEPTH, 2, QK_HALF), 0.05),
        "k_norm_g": 1.0 + nrm(ks[9], (DEPTH, 2, QK_HALF), 0.05),
        "lambda_q1": nrm(ks[10], (DEPTH, QK_HALF), 0.1),
        "lambda_k1": nrm(ks[11], (DEPTH, QK_HALF), 0.1),
        "lambda_q2": nrm(ks[12], (DEPTH, QK_HALF), 0.1),
        "lambda_k2": nrm(ks[13], (DEPTH, QK_HALF), 0.1),
        "subln_g": 1.0 + nrm(ks[14], (DEPTH, V_HEAD), 0.05),
        "w_pool": nrm(ks[15], (DEPTH, POOL_GROUPS, POOL_GROUP_DIM, POOL_GROUP_DIM), POOL_GROUP_DIM ** -0.5),
        "pool_scale": 1.0 + nrm(ks[16], (DEPTH, POOL_WIDTH), 0.05),
        "w_up_attn": nrm(ks[17], (DEPTH, ATTN_V_WIDTH, D), ATTN_V_WIDTH ** -0.5),
        "w_up_pool": nrm(ks[18], (DEPTH, POOL_WIDTH, D), POOL_WIDTH ** -0.5),
        "w_out": nrm(ks[19], (DEPTH, D, D), D ** -0.5),
        "ffn_norm_g": 1.0 + nrm(ks[20], (DEPTH, D), 0.05),
        "w_ffn_in": nrm(ks[21], (DEPTH, D, 2 * D_FF), D ** -0.5),
        "conv_w": nrm(ks[22], (DEPTH, CONV_WIDTH, D_FF), CONV_WIDTH ** -0.5),
        "conv_b": nrm(ks[23], (DEPTH, D_FF), 0.02),
        "w_ffn_out": nrm(ks[24], (DEPTH, D_FF, D), D_FF ** -0.5),
    }


def reference(x_prompt, x_sample, cache_k, cache_v, state_pool, state_conv,
              attn_norm_g, w_in, q_norm_g, k_norm_g, lambda_q1, lambda_k1, lambda_q2, lambda_k2,
              subln_g, w_pool, pool_scale, w_up_attn, w_up_pool, w_out,
              ffn_norm_g, w_ffn_in, conv_w, conv_b, w_ffn_out):
    y_p, y_s = x_prompt, x_sample
    Bp = x_prompt.shape[0]
    past = cache_k.shape[2]
    kp_l, vp_l, pp_l, cp_l, ks_l, vs_l, ps_l, cs_l = [], [], [], [], [], [], [], []
    for l in range(DEPTH):
        lam_init = 0.8 - 0.6 * math.exp(-0.3 * l)
        w = (attn_norm_g[l], w_in[l], q_norm_g[l], k_norm_g[l], lambda_q1[l], lambda_k1[l],
             lambda_q2[l], lambda_k2[l], subln_g[l], w_pool[l], pool_scale[l], w_up_attn[l],
             w_up_pool[l], w_out[l], ffn_norm_g[l], w_ffn_in[l], conv_w[l], conv_b[l], w_ffn_out[l])
        pool0 = jnp.zeros((Bp, POOL_HIST, POOL_WIDTH), y_p.dtype)
        conv0 = jnp.zeros((Bp, CONV_WIDTH - 1, D_FF), y_p.dtype)
        y_p, kp, vp, pp, cp = layer(y_p, 0, None, None, pool0, conv0, lam_init, *w)
        y_s, kk, vv, ps, cs = layer(y_s, past, cache_k[l], cache_v[l], state_pool[l], state_conv[l], lam_init, *w)
        kp_l.append(kp); vp_l.append(vp); pp_l.append(pp); cp_l.append(cp)
        ks_l.append(kk); vs_l.append(vv); ps_l.append(ps); cs_l.append(cs)
    k_prompt = jnp.stack(kp_l)
    v_prompt = jnp.stack(vp_l)
    pool_prompt = jnp.stack(pp_l)
    conv_prompt = jnp.stack(cp_l)
    k_sample = jnp.stack(ks_l)
    v_sample = jnp.stack(vs_l)
    pool_sample = jnp.stack(ps_l)
    conv_sample = jnp.stack(cs_l)
    return (y_p, y_s, k_prompt, v_prompt, pool_prompt, conv_prompt, k_sample, v_sample, pool_sample, conv_sample)
```

```python
import os
import numpy as np
from contextlib import ExitStack
import concourse.bass as bass
import concourse.mybir as mybir
from concourse.bass_utils import run_bass_kernel_spmd

F32 = mybir.dt.float32
BF16 = mybir.dt.bfloat16
AF = mybir.ActivationFunctionType
ALU = mybir.AluOpType
AX = mybir.AxisListType

D = 2048
H = 8
DFF = 5504
KF = 43
NPT = 128
NSL = 129
OWN0 = 112
NG = 8
EPS = 1e-6
LAM_INIT = 0.2
NCORE = 8
SM = 96
KSTOP = float(os.environ.get("KSTOP", "99"))
KTILES = int(os.environ.get("KTILES", "%d" % 129))
KSUB = float(os.environ.get("KSUB", "99"))
KSMALL = int(os.environ.get("KSMALL", "1"))
KGROUPS = int(os.environ.get("KGROUPS", "8"))
KATT = os.environ.get("KATT", "")


class _Tok:
    __slots__ = ("sem", "val", "eng")

    def __init__(self, sem, val, eng):
        self.sem = sem
        self.val = val
        self.eng = eng


class Sched:
    NDS = 40

    def __init__(self, nc):
        self.nc = nc
        self.eng = {"pe": nc.tensor, "act": nc.scalar, "dve": nc.vector,
                    "pool": nc.gpsimd, "sp": nc.sync}
        self.sem = {e: nc.alloc_semaphore("s_" + e) for e in ("pe", "act", "dve", "pool")}
        self.cnt = {e: 0 for e in self.sem}
        self.cur = {e: _Tok(self.sem[e], None, e) for e in self.sem}
        self.dsem = [nc.alloc_semaphore("d%d" % i) for i in range(self.NDS)]
        self.dcnt = [0] * self.NDS
        self.dnext = 0
        self.waited = {e: {} for e in self.eng}
        self.lastw = {}
        self.readers = {}
        self.nwait = 0
        self.nins = 0

    def _wait(self, e, tok):
        if tok is None:
            return
        if tok.val is None:
            if tok.eng == e:
                return
            raise RuntimeError("wait on unresolved token of %s from %s" % (tok.eng, e))
        w = self.waited[e]
        sid = id(tok.sem)
        if w.get(sid, 0) >= tok.val:
            return
        self.eng[e].wait_ge(tok.sem, tok.val)
        self.nwait += 1
        w[sid] = tok.val

    def _deps(self, e, reads, writes):
        toks = []
        for k in reads:
            toks.append(self.lastw.get(k))
        for k in writes:
            toks.append(self.lastw.get(k))
            toks.extend(self.readers.get(k, {}).values())
        best = {}
        for t in toks:
            if t is None:
                continue
            if t.eng == "pe" and e == "pe":
                continue
            if t.eng == e and t.val is not None and t.val <= self.cnt[e] - 3:
                continue
            if t.val is None:
                self._wait(e, t)
                continue
            sid = id(t.sem)
            if sid not in best or best[sid].val < t.val:
                best[sid] = t
        for t in best.values():
            self._wait(e, t)

    def _record(self, tok, reads, writes):
        for k in reads:
            self.readers.setdefault(k, {})[id(tok.sem)] = tok
        for k in writes:
            self.lastw[k] = tok
            self.readers[k] = {}

    def op(self, e, fn, reads=(), writes=(), signal=True):
        self._deps(e, reads, writes)
        ins = fn(self.eng[e])
        self.nins += 1
        tok = self.cur[e]
        self._record(tok, reads, writes)
        if signal:
            ins.then_inc(self.sem[e], 1)
            self.cnt[e] += 1
            tok.val = self.cnt[e]
            self.cur[e] = _Tok(self.sem[e], None, e)
        return ins

    def dma(self, q, out, in_, reads=(), writes=(), **kw):
        self._deps(q, reads, writes)
        i = self.dnext
        self.dnext = (i + 1) % self.NDS
        if self.dcnt[i]:
            self._wait(q, _Tok(self.dsem[i], self.dcnt[i], "dma"))
        ins = self.eng[q].dma_start(out=out, in_=in_, **kw)
        self.dcnt[i] += 16
        ins.then_inc(self.dsem[i], 16)
        self.nins += 1
        self._record(_Tok(self.dsem[i], self.dcnt[i], "dma"), reads, writes)
        return ins

    def sync_all(self, engines):
        for e in engines:
            for i in range(self.NDS):
                if self.dcnt[i]:
                    self._wait(e, _Tok(self.dsem[i], self.dcnt[i], "dma"))
            for x in self.sem:
                if x != e and self.cnt[x]:
                    assert self.cur[x].val is None
                    self._wait(e, _Tok(self.sem[x], self.cnt[x], x))

    def barrier(self):
        self.sync_all(["pe", "act", "dve", "pool", "sp"])
        self.lastw = {}
        self.readers = {}


def build():
    nc = bass.Bass("TRN2", target_bir_lowering=False)
    S = Sched(nc)

    def din(name, shape):
        return nc.dram_tensor(name, list(shape), F32, kind="ExternalInput").ap()

    def dout(name, shape):
        return nc.dram_tensor(name, list(shape), F32, kind="ExternalOutput").ap()

    def dscr(name, shape, dt=BF16):
        return nc.dram_tensor(name, list(shape), dt).ap()

    xs = din("xs", [NSL * 128, D])
    kvalid = din("kvalid", [128, NSL])
    ck = din("ck", [2, 2048, 1024])
    cv = din("cv", [2, 2048, 1024])
    sp_in = din("sp_in", [2, 15, 1024])
    sc_in = din("sc_in", [2, 2, DFF])
    invcnt = din("invcnt", [1, 64])
    hvalid = din("hvalid", [128, 1])
    ident = din("ident", [128, 128])
    g_attn = din("g_attn", [1, D])
    g_ffn = din("g_ffn", [1, D])
    gq_t = din("gq_t", [1, 1024])
    gk_t = din("gk_t", [1, 1024])
    sg_t = din("sg_t", [1, 1024])
    lq1 = din("lq1", [1, 64])
    lk1 = din("lk1", [1, 64])
    lq2 = din("lq2", [1, 64])
    lk2 = din("lk2", [1, 64])
    psc_t = din("psc_t", [128, 8])
    cw_t = din("cw_t", [128, KF * 3])
    cb_t = din("cb_t", [128, KF])
    w_in = din("w_in", [D, 8192])
    w_pool = din("w_pool", [1024, 256])
    w_ua = din("w_ua", [1024, D])
    w_up = din("w_up", [1024, D])
    w_out = din("w_out", [D, D])
    w_fi = din("w_fi", [D, 2 * DFF])
    w_fo = din("w_fo", [DFF, D])

    y_p = dout("y_p", [2048, D])
    y_s = dout("y_s", [2, 16, D])
    k_p = dout("k_p", [2048, 1024])
    v_p = dout("v_p", [2048, 1024])
    pool_p = dout("pool_p", [15, 1024])
    conv_p = dout("conv_p", [2, DFF])
    k_s = dout("k_s", [2, 16, 1024])
    v_s = dout("v_s", [2, 16, 1024])
    pool_s = dout("pool_s", [2, 15, 1024])
    conv_s = dout("conv_s", [2, 2, DFF])

    wbp_in = dscr("wbp_in", [16, 128, 16, 512])
    wb_pool = dscr("wb_pool", [1024, 256])
    wbp_ua = dscr("wbp_ua", [4, 128, 8, 512])
    wbp_up = dscr("wbp_up", [4, 128, 8, 512])
    wbp_out = dscr("wbp_out", [4, 128, 16, 512])
    wbp_fz = dscr("wbp_fz", [11, 128, 16, 512])
    wbp_fv = dscr("wbp_fv", [11, 128, 16, 512])
    wbp_fo = dscr("wbp_fo", [4, 128, KF, 512])
    KTs = dscr("KTs", [H, 128, NSL * 128])
    VXs = dscr("VXs", [H, 128, NSL, 130])
    KTc = dscr("KTc", [2, H, 128, 2048])
    VXc = dscr("VXc", [2, H, 128, 16, 130])

    pk = nc.alloc_psum_tensor("pk", [128, 1024], F32)
    pv = nc.alloc_psum_tensor("pv", [128, 1024], F32)
    pf4 = nc.alloc_psum_tensor("pf4", [128, 512], F32)
    pf5 = nc.alloc_psum_tensor("pf5", [128, 512], F32)
    pbf0 = nc.alloc_psum_tensor("pbf0", [128, 8, 128], BF16)
    pbf1 = nc.alloc_psum_tensor("pbf1", [128, 8, 128], BF16)
    B = [pk[:, 0:512], pk[:, 512:1024], pv[:, 0:512], pv[:, 512:1024], pf4[:], pf5[:]]
    BK = ["B0", "B1", "B2", "B3", "B4", "B5"]
    pbf0f = pbf0[:].rearrange("p a b -> p (a b)").bitcast(F32)
    pbf1f = pbf1[:].rearrange("p a b -> p (a b)").bitcast(F32)
    STB = [(pk, ("B0", "B1")), (pv, ("B2", "B3"))]
    ACC = [(pf4[:], "B4"), (pf5[:], "B5"), (pbf0f, "pbf0"), (pbf1f, "pbf1")]
    BSET = [[(B[0], "B0"), (B[1], "B1"), (B[2], "B2"), (B[3], "B3")], ACC]
    ZR = [(B[0], "B0"), (B[1], "B1"), (pf4[:], "B4"), (pf5[:], "B5")]
    VR = [(B[2], "B2"), (B[3], "B3"), (pbf0f, "pbf0"), (pbf1f, "pbf1")]

    def A(fn, r, w):
        return S.op("act", fn, r, w)

    def V(fn, r, w):
        return S.op("dve", fn, r, w)

    def G(fn, r, w):
        return S.op("pool", fn, r, w)

    def P(fn, r, w, sig=True):
        return S.op("pe", fn, r, w, signal=sig)

    casts = []

    def add_cast(dst, src, rows, c_lo, c_hi):
        for r0 in range(0, rows, 128):
            r1 = min(rows, r0 + 128)
            for c0 in range(c_lo, c_hi, 2048):
                c1 = min(c_hi, c0 + 2048)
                casts.append((dst[r0:r1, c0:c1], src[r0:r1, c0:c1]))

    def add_cast_p(dst4, src2, rows, c_lo, c_hi, piece0):
        for kt in range(rows // 128):
            r0 = kt * 128
            c0 = c_lo
            while c0 < c_hi:
                c1 = min(c_hi, c0 + 2048)
                nfull = (c1 - c0) // 512
                pj = piece0 + (c0 - c_lo) // 512
                if nfull:
                    casts.append((dst4[pj:pj + nfull, :, kt, :].rearrange("j p c -> p j c"),
                                  src2[r0:r0 + 128, c0:c0 + nfull * 512].rearrange("p (j c) -> p j c", c=512)))
                rem = (c1 - c0) - nfull * 512
                if rem:
                    casts.append((dst4[pj + nfull, :, kt, 0:rem], src2[r0:r0 + 128, c0 + nfull * 512:c1]))
                c0 = c1

    add_cast_p(wbp_in, w_in, D, 0, 1024, 0)
    add_cast_p(wbp_in, w_in, D, 3072, 8192, 6)
    add_cast(wb_pool, w_pool, 1024, 0, 256)
    add_cast_p(wbp_ua, w_ua, 1024, 0, D, 0)
    add_cast_p(wbp_up, w_up, 1024, 0, D, 0)
    add_cast_p(wbp_out, w_out, D, 0, D, 0)
    add_cast_p(wbp_fz, w_fi, D, 0, DFF, 0)
    add_cast_p(wbp_fv, w_fi, D, DFF, 2 * DFF, 0)
    add_cast_p(wbp_fo, w_fo, DFF, 0, D, 0)
    cast_pos = [0]

    def emit_casts(n):
        for _ in range(n):
            if cast_pos[0] < len(casts):
                d, s = casts[cast_pos[0]]
                cast_pos[0] += 1
                S.dma("pool", d, s, writes=["wcast"])

    with ExitStack() as top:
        def sb(name, shape, dt, st=top):
            return st.enter_context(nc.sbuf_tensor(name, list(shape), dt))

        idf = sb("idf", [128, 128], F32)
        idb = sb("idb", [128, 128], BF16)
        gA = sb("gA", [128, D], F32)
        kval = sb("kval", [128, NSL], F32)
        ss = sb("ss", [128, 4], F32)
        rs = sb("rs", [128, 4], F32)
        ssk = sb("ssk", [128, 16], F32)
        rk = sb("rk", [128, 16], F32)
        lamc = sb("lamc", [128, 8], F32)
        S.dma("sp", idf[:], ident[:, :], writes=["idf"])
        V(lambda e: e.tensor_copy(idb[:], idf[:]), ["idf"], ["idb"])
        S.dma("sp", gA[:], g_attn.partition_broadcast(128), writes=["gA"])
        S.dma("sp", kval[:], kvalid[:, :], writes=["kval"])

        def rstd_cols(src_ap, n, width, junk_ap, col, rkeys, jkey):
            A(lambda e: e.activation(out=junk_ap, in_=src_ap, func=AF.Square, accum_out=ss[0:n, col:col + 1]),
              rkeys, [jkey, "ss%d" % col])
            A(lambda e: e.activation(out=rs[0:n, col:col + 1], in_=ss[0:n, col:col + 1], func=AF.Ln,
                                     scale=1.0 / width, bias=EPS), ["ss%d" % col], ["rs%d" % col])
            A(lambda e: e.activation(out=rs[0:n, col:col + 1], in_=rs[0:n, col:col + 1], func=AF.Exp,
                                     scale=-0.5), ["rs%d" % col], ["rs%d" % col])

        def headnorm(raw, n, sqbuf, outf, gtile, rkey, sqkey, okey, gkey):
            A(lambda e: e.activation(out=sqbuf[0:n, 0:1024], in_=raw, func=AF.Square), [rkey], [sqkey])
            V(lambda e: e.reduce_sum(out=ssk[0:n, :], in_=sqbuf[0:n, 0:1024].rearrange("p (g d) -> p g d", d=64),
                                     axis=AX.X), [sqkey], ["ssk"])
            A(lambda e: e.activation(out=rk[0:n, :], in_=ssk[0:n, :], func=AF.Ln, scale=1.0 / 64, bias=EPS),
              ["ssk"], ["rk"])
            A(lambda e: e.activation(out=rk[0:n, :], in_=rk[0:n, :], func=AF.Exp, scale=-0.5), ["rk"], ["rk"])
            V(lambda e: e.tensor_tensor(out=outf.rearrange("p (g d) -> p g d", d=64),
                                        in0=raw.rearrange("p (g d) -> p g d", d=64),
                                        in1=rk[0:n, :].unsqueeze(2).to_broadcast([n, 16, 64]), op=ALU.mult),
              [rkey, "rk"], [okey])
            V(lambda e: e.tensor_tensor(out=outf, in0=outf, in1=gtile[0:n, :], op=ALU.mult), [okey, gkey], [okey])

        with ExitStack() as p1:
            def sb1(name, shape, dt):
                return sb(name, shape, dt, p1)

            wkv = sb1("wkv", [128, 16, 2048], BF16)
            gk = sb1("gk", [128, 1024], F32)
            xt = [sb1("xt%d" % i, [128, D], F32) for i in range(3)]
            tmp = sb1("tmp", [128, D], F32)
            hb = [sb1("hb%d" % i, [128, D], BF16) for i in range(2)]
            hT = [sb1("hT%d" % i, [128, 16, 128], BF16) for i in range(2)]
            kraw = sb1("kraw", [128, 1024], F32)
            kf = [sb1("kf%d" % i, [128, 1024], F32) for i in range(2)]
            kb = sb1("kb", [128, 1024], BF16)
            vf = [sb1("vf%d" % i, [128, 1024], F32) for i in range(2)]
            vx = [sb1("vx%d" % i, [128, 8, 130], BF16) for i in range(2)]
            ktt = [sb1("ktt%d" % i, [128, 8, 128], BF16) for i in range(2)]

            S.dma("sp", gk[:], gk_t.partition_broadcast(128), writes=["gk"])
            for b_ in range(2):
                V(lambda e, b_=b_: e.memset(vx[b_][:], 0.0), [], ["vx%d" % b_])
            for kt in range(16):
                S.dma("pool", wkv[:, kt, :], w_in[kt * 128:(kt + 1) * 128, 1024:3072], writes=["wkv"])

            def ktrans(b, dst_ap):
                for hh in range(H):
                    P(lambda e, hh=hh: e.transpose(pbf1[:, hh, :], kb[:, hh * 128:(hh + 1) * 128], idb[:]),
                      ["kb", "idb"], ["pbf1"], sig=(hh == H - 1))
                V(lambda e: e.tensor_copy(ktt[b][:], pbf1[:]), ["pbf1"], ["ktt%d" % b])
                S.dma("sp", dst_ap, ktt[b][:], reads=["ktt%d" % b], writes=["KTs"])

            sqb = sb1("sqb", [128, 1024], F32)

            def tiles_iter():
                return [t for t in range(NSL) if not (t >= KTILES and t < NSL - 2)]

            def xload(s):
                b3 = s % 3
                S.dma("sp", xt[b3][:], xs[s * 128:(s + 1) * 128, :], writes=["xt%d" % b3])

            def hchain(s):
                b = s % 2
                b3 = s % 3
                xk, hk = "xt%d" % b3, "hb%d" % b
                rstd_cols(xt[b3][:], 128, D, hb[b][:], 0, [xk], hk)
                A(lambda e: e.activation(out=tmp[:], in_=xt[b3][:], func=AF.Copy, scale=rs[:, 0:1]),
                  [xk, "rs0"], ["tmp"])
                G(lambda e: e.tensor_tensor(out=hb[b][:], in0=tmp[:], in1=gA[:], op=ALU.mult),
                  ["tmp", "gA"], [hk])

            def hT_make(s):
                b = s % 2
                hk, hTk = "hb%d" % b, "hT%d" % b
                for half in range(2):
                    for j in range(8):
                        kt = half * 8 + j
                        P(lambda e, kt=kt, j=j: e.transpose(pbf0[:, j, :], hb[b][:, kt * 128:(kt + 1) * 128], idb[:]),
                          [hk, "idb"], ["pbf0"], sig=(j == 7))
                    V(lambda e, half=half: e.tensor_copy(hT[b][:, half * 8:(half + 1) * 8, :], pbf0[:]),
                      ["pbf0"], [hTk])

            def mm(s):
                b = s % 2
                hTk = "hT%d" % b
                for cc in range(4):
                    dst = pk if cc < 2 else pv
                    c0 = (cc % 2) * 512
                    for kt in range(16):
                        P(lambda e, kt=kt, cc=cc, dst=dst, c0=c0: e.matmul(
                            dst[:, c0:c0 + 512], lhsT=hT[b][:, kt, :], rhs=wkv[:, kt, cc * 512:(cc + 1) * 512],
                            start=(kt == 0), stop=(kt == 15)),
                          [hTk, "wkv"], ["pk" if cc < 2 else "pv"], sig=(kt == 15))

            def evac(s):
                b = s % 2
                V(lambda e: e.tensor_copy(kraw[:], pk[:]), ["pk"], ["kraw"])
                A(lambda e: e.copy(out=vf[b][:], in_=pv[:]), ["pv"], ["vf%d" % b])

            def tail(s):
                b = s % 2
                headnorm(kraw[:], 128, sqb, kf[b][:], gk, "kraw", "sqb", "kf%d" % b, "gk")
                V(lambda e: e.tensor_copy(kb[:], kf[b][:]), ["kf%d" % b], ["kb"])
                A(lambda e: e.copy(out=vx[b][:, :, 0:128], in_=vf[b][:].rearrange("p (h d) -> p h d", d=128)),
                  ["vf%d" % b], ["vx%d" % b])
                V(lambda e: e.tensor_copy(vx[b][:, :, 128:129],
                                          kval[:, s:s + 1].unsqueeze(1).to_broadcast([128, 8, 1])),
                  ["kval"], ["vx%d" % b])
                S.dma("sp", VXs[:, :, s, :].rearrange("h p e -> p h e"), vx[b][:],
                      reads=["vx%d" % b], writes=["VXs"])
                if OWN0 <= s < NPT:
                    r0 = (s - OWN0) * 128
                    S.dma("sp", k_p[r0:r0 + 128, :], kf[b][:], reads=["kf%d" % b], writes=["k_p"])
                    S.dma("sp", v_p[r0:r0 + 128, :], vf[b][:], reads=["vf%d" % b], writes=["v_p"])
                if s == NPT:
                    for j in range(2):
                        S.dma("sp", k_s[j], kf[b][32 * j:32 * j + 16, :], reads=["kf%d" % b], writes=["k_s"])
                        S.dma("sp", v_s[j], vf[b][32 * j:32 * j + 16, :], reads=["vf%d" % b], writes=["v_s"])

            tl = tiles_iter()
            xload(tl[0])
            xload(tl[1])
            hchain(tl[0])
            hT_make(tl[0])
            for idx, s in enumerate(tl):
                nxt = tl[idx + 1] if idx + 1 < len(tl) else None
                if idx + 2 < len(tl):
                    xload(tl[idx + 2])
                if nxt is not None:
                    hchain(nxt)
                emit_casts(2)
                mm(s)
                evac(s)
                if nxt is not None:
                    hT_make(nxt)
                if idx > 0:
                    sp_ = tl[idx - 1]
                    ktrans(sp_ % 2, KTs[:, :, sp_ * 128:(sp_ + 1) * 128].rearrange("h p t -> p h t"))
                tail(s)
            sp_ = tl[-1]
            ktrans(sp_ % 2, KTs[:, :, sp_ * 128:(sp_ + 1) * 128].rearrange("h p t -> p h t"))

            for j in range(2 if KSTOP >= 2 else 0):
                for t in range(16):
                    i = j * 16 + t
                    b = i % 2
                    emit_casts(2)
                    S.dma("sp", xt[b][:, 0:1024], ck[j, t * 128:(t + 1) * 128, :], writes=["xt%d" % b])
                    S.dma("sp", xt[b][:, 1024:2048], cv[j, t * 128:(t + 1) * 128, :], writes=["xt%d" % b])
                    for hh in range(H):
                        P(lambda e, hh=hh: e.transpose(pk[:, hh * 128:(hh + 1) * 128],
                                                       xt[b][:, hh * 128:(hh + 1) * 128], idf[:]),
                          ["xt%d" % b, "idf"], ["pk"], sig=(hh == H - 1))
                    V(lambda e: e.tensor_copy(ktt[b][:], pk[:].rearrange("p (h t) -> p h t", t=128)),
                      ["pk"], ["ktt%d" % b])
                    S.dma("sp", KTc[j, :, :, t * 128:(t + 1) * 128].rearrange("h p t -> p h t"), ktt[b][:],
                          reads=["ktt%d" % b], writes=["KTc"])
                    G(lambda e: e.tensor_copy(vx[b][:, :, 0:128],
                                              xt[b][:, 1024:2048].rearrange("p (h d) -> p h d", d=128)),
                      ["xt%d" % b], ["vx%d" % b])
                    V(lambda e: e.tensor_copy(vx[b][:, :, 128:129],
                                              kval[:, NPT:NPT + 1].unsqueeze(1).to_broadcast([128, 8, 1])),
                      ["kval"], ["vx%d" % b])
                    S.dma("sp", VXc[j, :, :, t, :].rearrange("h p e -> p h e"), vx[b][:],
                          reads=["vx%d" % b], writes=["VXc"])
            emit_casts(len(casts))
            S.barrier()

        with ExitStack() as p2:
            def sb2(name, shape, dt):
                return sb(name, shape, dt, p2)

            NC_ = 256
            gF = sb2("gF", [128, D], F32)
            gq = sb2("gq", [128, 1024], F32)
            sg = sb2("sg", [128, 1024], F32)
            invc = sb2("invc", [128, 4, 16], F32)
            hv = sb2("hv", [128, 1], F32)
            psc = sb2("psc", [128, 8], F32)
            cw = sb2("cw", [128, KF, 3], F32)
            cb = sb2("cb", [128, KF], F32)
            lt = [sb2("lt%d" % i, [128, 64], F32) for i in range(4)]
            wpl = sb2("wpl", [128, 8, 256], BF16)
            xg = sb2("xg", [128, 2, D], F32)
            hT = sb2("hTm", [128, 16, NC_], BF16)
            tmpf = sb2("tmpf", [128, D], F32)
            hb = sb2("hbm", [128, D], BF16)
            qraw = sb2("qraw", [128, 1024], F32)
            qf = sb2("qf", [128, 1024], F32)
            qb = sb2("qb", [128, 1024], BF16)
            RA = sb2("RA", [128, 5504], F32)
            RB = sb2("RB", [128, 7728], F32)
            ring = [sb2("ring%d" % i, [128, 8192], BF16) for i in range(3)]
            zext = sb2("zext", [128, 3, 258], F32)
            zextB = sb2("zextB", [128, 3, 258], F32)
            zcB = sb2("zcB", [128, NC_], F32)
            szlB = sb2("szlB", [128, NC_], F32)
            zc = sb2("zc", [128, NC_], F32)
            szl = sb2("szl", [128, NC_], F32)
            zh = sb2("zh", [128, KF, 2], F32)
            sch = sb2("sch", [128, KF, 2, 2], F32)
            uh = sb2("uh", [128, 8, 15], F32)
            uexs = sb2("uexs", [128, 8, 3, 32], F32)
            zsave = sb2("zsave", [128, KF, 2, 2], F32)
            l12 = sb2("l12", [128, 4], F32)
            ofin = sb2("ofin", [128, 128], F32)
            t1f = sb2("t1f", [128, 128], F32)
            stg = tmpf[:, 0:1024]
            yst = [sb2("yst%d" % i, [128, 512], F32) for i in range(2)]

            RAb = RA[:].bitcast(BF16)
            RBb = RB[:].bitcast(BF16)
            aT = RAb[:, 0:KF * NC_].rearrange("p (k n) -> p k n", n=NC_)
            QT = RAb[:, 0:16 * NC_].rearrange("p (c k n) -> p c k n", c=2, n=NC_)
            uext = RA[:, 2048:4216].rearrange("p (k n) -> p k n", n=271)
            pa = tmpf[:, 0:542].rearrange("p (k n) -> p k n", n=271)
            pb_ = tmpf[:, 542:1084].rearrange("p (k n) -> p k n", n=271)
            pT = qf[:].bitcast(BF16).rearrange("p (k n) -> p k n", n=NC_)
            pbT = qraw[:].bitcast(BF16).rearrange("p (k n) -> p k n", n=NC_)
            ob = RAb[:, 8552:8552 + 2048].rearrange("p (t f) -> p t f", f=1024)
            KTr = [RBb[:, i * 2048:(i + 1) * 2048] for i in range(3)]
            VXr = [RBb[:, 6144 + i * 2080:6144 + (i + 1) * 2080].rearrange("p (t e) -> p t e", e=130)
                   for i in range(3)]
            PTr = [RBb[:, 12384 + i * 1024:12384 + (i + 1) * 1024].rearrange("p (j c q) -> p j c q", c=2, q=256)
                   for i in range(3)]
            oT = RBb[:, 0:2048].rearrange("p (k n) -> p k n", n=NC_)
            mT = RBb[:, 2048:2048 + 4096].rearrange("p (k n) -> p k n", n=NC_)

            S.dma("sp", gF[:], g_ffn.partition_broadcast(128), writes=["gF"])
            S.dma("sp", gq[:], gq_t.partition_broadcast(128), writes=["gq"])
            S.dma("sp", sg[:], sg_t.partition_broadcast(128), writes=["sg"])
            S.dma("sp", invc[:].rearrange("p a b -> p (a b)"), invcnt.partition_broadcast(128), writes=["invc"])
            S.dma("sp", hv[:], hvalid[:, :], writes=["hv"])
            S.dma("sp", psc[:], psc_t[:, :], writes=["psc"])
            S.dma("sp", cw[:].rearrange("p a b -> p (a b)"), cw_t[:, :], writes=["cw"])
            S.dma("sp", cb[:], cb_t[:, :], writes=["cb"])
            S.dma("sp", wpl[:], wb_pool.rearrange("(k p) n -> p k n", p=128), reads=["wcast"], writes=["wpl"])
            for i, lv in enumerate((lq1, lk1, lq2, lk2)):
                S.dma("sp", lt[i][:], lv.partition_broadcast(128), writes=["lt%d" % i])
            V(lambda e: e.tensor_scalar(out=sg[:], in0=sg[:], scalar1=1.0 - LAM_INIT, scalar2=None, op0=ALU.mult),
              ["sg"], ["sg"])
            for i in range(2):
                V(lambda e, i=i: e.tensor_tensor(out=lt[2 * i][:], in0=lt[2 * i][:], in1=lt[2 * i + 1][:], op=ALU.mult),
                  ["lt%d" % (2 * i), "lt%d" % (2 * i + 1)], ["lt%d" % (2 * i)])
                V(lambda e, i=i: e.reduce_sum(out=lamc[:, i:i + 1], in_=lt[2 * i][:], axis=AX.X),
                  ["lt%d" % (2 * i)], ["lamc"])
            A(lambda e: e.activation(out=lamc[:, 0:2], in_=lamc[:, 0:2], func=AF.Exp), ["lamc"], ["lamc"])
            V(lambda e: e.tensor_tensor(out=lamc[:, 2:3], in0=lamc[:, 1:2], in1=lamc[:, 0:1], op=ALU.subtract),
              ["lamc"], ["lamc"])
            V(lambda e: e.tensor_scalar(out=lamc[:, 2:3], in0=lamc[:, 2:3], scalar1=-LAM_INIT, scalar2=None,
                                        op0=ALU.add), ["lamc"], ["lamc"])
            G(lambda e: e.memset(uh[:], 0.0), [], ["uh"])
            G(lambda e: e.memset(zh[:], 0.0), [], ["zh"])
            G(lambda e: e.memset(uexs[:], 0.0), [], ["uexs"])
            G(lambda e: e.memset(zext[:], 0.0), [], ["zext"])
            G(lambda e: e.memset(zextB[:], 0.0), [], ["zextB"])

            ring_i = [0]

            def wload(src_ap, kt_n, ncols):
                i = ring_i[0] % 3
                ring_i[0] += 1
                view = ring[i][:, 0:kt_n * ncols].rearrange("p (k n) -> p k n", n=ncols)
                S.dma("sp", view, src_ap, reads=["wcast"], writes=["ring%d" % i])
                return view, "ring%d" % i

            def wsrc(wb, r0, kt_n, c0, ncols):
                return wb[r0:r0 + kt_n * 128, c0:c0 + ncols].rearrange("(k p) n -> p k n", p=128)

            bank_i = {}

            def nbank(lo=0, n=4):
                c = bank_i.get((lo, n), 0)
                bank_i[(lo, n)] = c + 1
                i = lo + c % n
                return B[i], BK[i]

            def make_hT(toks, gtile, gkey, src_of):
                for ti, (nr, col0) in enumerate(toks):
                    src, skey = src_of(ti)
                    rstd_cols(src, nr, D, hb[0:nr, :], 0, [skey], "hbm")
                    A(lambda e: e.activation(out=tmpf[0:nr, :], in_=src, func=AF.Copy, scale=rs[0:nr, 0:1]),
                      [skey, "rs0"], ["tmpf"])
                    V(lambda e: e.tensor_tensor(out=hb[0:nr, :], in0=tmpf[0:nr, :], in1=gtile[0:nr, :], op=ALU.mult),
                      ["tmpf", gkey], ["hbm"])
                    for half in range(2):
                        for j in range(8):
                            kt = half * 8 + j
                            P(lambda e, kt=kt, j=j: e.transpose(pbf0[:, j, 0:nr], hb[0:nr, kt * 128:(kt + 1) * 128],
                                                                idb[0:nr, 0:nr]),
                              ["hbm", "idb"], ["pbf0"], sig=(j == 7))
                        V(lambda e, half=half: e.tensor_copy(hT[:, half * 8:(half + 1) * 8, col0:col0 + nr],
                                                             pbf0[:, :, 0:nr]), ["pbf0"], ["hT"])

            def finalize(acc0, acc1, pb0, nq, ti, hh, k0, k1):
                sl = slice(pb0, pb0 + nq)
                if KATT in ("st", "st0", "st0c0", "av", "nomask"):
                    return
                V(lambda e: e.tensor_copy(l12[sl, 0:1], acc0[:, 128:129]), [k0], ["l12"])
                V(lambda e: e.tensor_copy(l12[sl, 1:2], acc1[:, 128:129]), [k1], ["l12"])
                V(lambda e: e.tensor_scalar(out=l12[sl, 0:2], in0=l12[sl, 0:2], scalar1=1e-30, scalar2=None,
                                            op0=ALU.add), ["l12"], ["l12"])
                V(lambda e: e.reciprocal(l12[sl, 2:4], l12[sl, 0:2]), ["l12"], ["l12"])
                V(lambda e: e.tensor_tensor(out=l12[sl, 3:4], in0=l12[sl, 3:4], in1=lamc[sl, 2:3], op=ALU.mult),
                  ["l12", "lamc"], ["l12"])
                V(lambda e: e.tensor_scalar(out=t1f[sl, :], in0=acc0[:, 0:128], scalar1=l12[sl, 2:3], scalar2=None,
                                            op0=ALU.mult), [k0, "l12"], ["t1f"])
                V(lambda e: e.scalar_tensor_tensor(out=ofin[sl, :], in0=acc1[:, 0:128], scalar=l12[sl, 3:4],
                                                   in1=t1f[sl, :], op0=ALU.mult, op1=ALU.add),
                  [k1, "l12", "t1f"], ["ofin"])
                A(lambda e: e.activation(out=t1f[sl, :], in_=ofin[sl, :], func=AF.Square,
                                         accum_out=ss[sl, 1:2]), ["ofin"], ["t1f", "ss1"])
                A(lambda e: e.activation(out=rs[sl, 1:2], in_=ss[sl, 1:2], func=AF.Ln, scale=1.0 / 128, bias=EPS),
                  ["ss1"], ["rs1"])
                A(lambda e: e.activation(out=rs[sl, 1:2], in_=rs[sl, 1:2], func=AF.Exp, scale=-0.5),
                  ["rs1"], ["rs1"])
                A(lambda e: e.activation(out=t1f[sl, :], in_=ofin[sl, :], func=AF.Copy, scale=rs[sl, 1:2]),
                  ["ofin", "rs1"], ["t1f"])
                G(lambda e: e.tensor_tensor(out=ob[sl, ti, hh * 128:(hh + 1) * 128], in0=t1f[sl, :],
                                            in1=sg[sl, hh * 128:(hh + 1) * 128], op=ALU.mult),
                  ["t1f", "sg"], ["ob"])

            kv_i = [0]

            def kv_load(kt_src, vx_src, ntile, krows=128, kbase=0):
                i = kv_i[0] % 3
                kv_i[0] += 1
                if kt_src is not None:
                    S.dma("sp", KTr[i][:, 0:kt_src.shape[-1]], kt_src, reads=["KTs", "KTc"], writes=["KTr%d" % i])
                S.dma("sp", VXr[i][kbase:kbase + krows, 0:ntile, :], vx_src, reads=["VXs", "VXc"],
                      writes=["VXr%d" % i])
                return i

            pt_i = [0]

            def key_step(hh, tiles, nk, kb0, qlo, qhi, accs_per_tile, masks, stb):
                stt, stkeys = STB[stb]
                pi = pt_i[0] % 3
                pt_i[0] += 1
                PT = PTr[pi]
                ptk = "PT%d" % pi
                nt = len(tiles)
                for j, (i, tt) in enumerate(tiles):
                    for c in range(2):
                        o0 = j * 512 + c * 256
                        P(lambda e, c=c, i=i, tt=tt, o0=o0: e.matmul(
                            stt[kb0:kb0 + nk, o0 + qlo:o0 + qhi], lhsT=KTr[i][:, tt * 128:tt * 128 + nk],
                            rhs=QT[:, c, hh, qlo:qhi], start=True, stop=True),
                          ["KTr%d" % i, "QT"], [stkeys[j]], sig=(c == 1))
                if qlo == 0 and qhi == 256:
                    src = stt[kb0:kb0 + nk, 0:nt * 512]
                    dst = PT[kb0:kb0 + nk, 0:nt, :, :].rearrange("p j c q -> p (j c q)")
                else:
                    src = stt[kb0:kb0 + nk, 0:nt * 512].rearrange("p (j c q) -> p j c q", c=2, q=256)[:, :, :, qlo:qhi]
                    dst = PT[kb0:kb0 + nk, 0:nt, :, qlo:qhi]
                A(lambda e: e.activation(out=dst, in_=src, func=AF.Exp, scale=0.125),
                  [stkeys[j] for j in range(nt)], [ptk])
                for (q0,) in masks:
                    V(lambda e, q0=q0: e.memset(PT[64:128, 0, :, q0:q0 + 64], 0.0), [], [ptk])

                def av():
                    for j, (i, tt) in enumerate(tiles):
                        accs = accs_per_tile[j]
                        for ai, (c, acc, akey, qc0, qc1, fi, la) in enumerate(accs):
                            P(lambda e, c=c, acc=acc, qc0=qc0, qc1=qc1, fi=fi, la=la, i=i, tt=tt, j=j: e.matmul(
                                acc, lhsT=PT[kb0:kb0 + nk, j, c, qc0:qc1], rhs=VXr[i][kb0:kb0 + nk, tt, 0:129],
                                start=fi, stop=la),
                              [ptk, "VXr%d" % i], [akey], sig=(j == nt - 1 and ai == len(accs) - 1))
                return av

            pend_av = [None]

            def pipe_push(av):
                if pend_av[0] is not None:
                    pend_av[0]()
                pend_av[0] = av

            def pipe_flush():
                pipe_push(None)

            def attend_main(gi):
                j0 = 2 * gi
                T = OWN0 + j0 + 2
                nch = (T + 15) // 16
                TP = OWN0 + j0
                for hh in range(H):
                    loaded = {}

                    def ld(ch):
                        t0 = ch * 16
                        nt = min(16, T - t0)
                        loaded[ch] = kv_load(KTs[hh, :, t0 * 128:(t0 + nt) * 128], VXs[hh, :, t0:t0 + nt, :], nt)
                    ld(0)
                    if nch > 1:
                        ld(1)

                    def accs_for(t):
                        full = t <= OWN0 + j0
                        accs = []
                        for c in range(2):
                            for jj in range(2):
                                if jj == 0 and not full:
                                    continue
                                last_t = OWN0 + j0 + jj
                                accs.append((c, ACC[2 * c + jj][0][:, 0:129], ACC[2 * c + jj][1], jj * 128,
                                             (jj + 1) * 128, t == 0, t == last_t))
                        return accs
                    step = 0
                    t = 0
                    while t < T:
                        ch, tt = divmod(t, 16)
                        if tt == 2 and ch + 2 < nch:
                            ld(ch + 2)
                        if t < TP:
                            i = loaded[ch]
                            pipe_push(key_step(hh, [(i, tt), (i, tt + 1)], 128, 0, 0, 256,
                                               [accs_for(t), accs_for(t + 1)], [], step % 2))
                            t += 2
                        else:
                            i = loaded[ch]
                            full = t <= OWN0 + j0
                            masks = [(0,)] if t == OWN0 + j0 else [(128,)]
                            pipe_push(key_step(hh, [(i, tt)], 128, 0, 0 if full else 128, 256,
                                               [accs_for(t)], masks, step % 2))
                            t += 1
                        step += 1
                    pipe_flush()
                    for jj in range(2):
                        finalize(ACC[jj][0][:, 0:129], ACC[2 + jj][0][:, 0:129], 0, 128, jj, hh, ACC[jj][1],
                                 ACC[2 + jj][1])

            def attend_small():
                for hh in range(H):
                    for j in range(2):
                        i = kv_load(KTc[j, hh, :, :], VXc[j, hh, :, :, :], 16)
                        pb0 = 32 * j
                        a0 = ACC[0][0][pb0:pb0 + 16, 0:129]
                        a1 = ACC[1][0][pb0:pb0 + 16, 0:129]
                        for t in range(0, 16, 2):
                            accs = [[(0, a0, ACC[0][1], pb0, pb0 + 16, tq == 0, False),
                                     (1, a1, ACC[1][1], pb0, pb0 + 16, tq == 0, False)] for tq in (t, t + 1)]
                            pipe_push(key_step(hh, [(i, t), (i, t + 1)], 128, 0, pb0, pb0 + 16, accs, [],
                                               (t // 2) % 2))
                        pipe_flush()
                        i2 = kv_load(KTs[hh, :, NPT * 128 + pb0:NPT * 128 + pb0 + 16],
                                     VXs[hh, pb0:pb0 + 16, NPT:NPT + 1, :], 1, krows=16, kbase=pb0)
                        accs = [[(0, a0, ACC[0][1], pb0, pb0 + 16, False, True),
                                 (1, a1, ACC[1][1], pb0, pb0 + 16, False, True)]]
                        pipe_push(key_step(hh, [(i2, 0)], 16, pb0, pb0, pb0 + 16, accs, [], 0))
                        pipe_flush()
                        finalize(a0, a1, pb0, 16, 0, hh, ACC[0][1], ACC[1][1])
                    a0 = ACC[0][0][64:81, 0:129]
                    a1 = ACC[1][0][64:81, 0:129]
                    nch = OWN0 // 16
                    step = 0
                    for ch in range(nch):
                        i = kv_load(KTs[hh, :, ch * 2048:(ch + 1) * 2048], VXs[hh, :, ch * 16:(ch + 1) * 16, :], 16)
                        for tt in range(0, 16, 2):
                            t = ch * 16 + tt
                            accs = [[(0, a0, ACC[0][1], 64, 81, tq == 0, tq == OWN0 - 1),
                                     (1, a1, ACC[1][1], 64, 81, tq == 0, tq == OWN0 - 1)] for tq in (t, t + 1)]
                            pipe_push(key_step(hh, [(i, tt), (i, tt + 1)], 128, 0, 64, 81, accs, [], step % 2))
                            step += 1
                        pipe_flush()
                    finalize(a0, a1, 64, 17, 0, hh, ACC[0][1], ACC[1][1])

            def do_group(kind, gi):
                small = kind == "small"
                if small:
                    n = SM
                    toks = [(SM, 0)]
                    rows0 = [NPT * 128]
                    segs = [(0, 16), (32, 16), (64, 17)]
                else:
                    n = 256
                    toks = [(128, 0), (128, 128)]
                    rows0 = [(OWN0 + 2 * gi) * 128, (OWN0 + 2 * gi + 1) * 128]
                    segs = [(0, 256)]
                for ti, (nr, col0) in enumerate(toks):
                    S.dma("sp", xg[0:nr, ti, :], xs[rows0[ti]:rows0[ti] + nr, :], writes=["xg%d" % ti])
                make_hT(toks, gA, "gA", lambda ti: (xg[0:toks[ti][0], ti, :], "xg%d" % ti))
                V(lambda e: e.memset(QT[64:128, 0, :, :], 0.0), [], ["QT"])
                V(lambda e: e.memset(QT[0:64, 1, :, :], 0.0), [], ["QT"])
                wq = [wload(wbp_in[cc], 16, 512) for cc in range(2)]
                for ti, (nr, col0) in enumerate(toks):
                    for cc in range(2):
                        wv, wk = wq[cc]
                        bk, bkk = nbank()
                        for kt in range(16):
                            P(lambda e, kt=kt, wv=wv, bk=bk: e.matmul(
                                bk[0:nr, :], lhsT=hT[:, kt, col0:col0 + nr], rhs=wv[:, kt, :],
                                start=(kt == 0), stop=(kt == 15)), ["hT", wk], [bkk], sig=(kt == 15))
                        V(lambda e, bk=bk, cc=cc: e.tensor_copy(qraw[0:nr, cc * 512:(cc + 1) * 512], bk[0:nr, :]),
                          [bkk], ["qraw"])
                    headnorm(qraw[0:nr, :], nr, tmpf, qf[0:nr, :], gq, "qraw", "tmpf", "qf", "gq")
                    G(lambda e: e.tensor_copy(qb[0:nr, :], qf[0:nr, :]), ["qf"], ["qb"])
                    for hh in range(H):
                        P(lambda e, hh=hh: e.transpose(pbf1[:, hh, 0:nr], qb[0:nr, hh * 128:(hh + 1) * 128],
                                                       idb[0:nr, 0:nr]), ["qb", "idb"], ["pbf1"], sig=(hh == H - 1))
                    V(lambda e: e.tensor_copy(QT[0:64, 0, :, col0:col0 + nr], pbf1[0:64, :, 0:nr]), ["pbf1"], ["QT"])
                    V(lambda e: e.tensor_copy(QT[64:128, 1, :, col0:col0 + nr], pbf1[64:128, :, 0:nr]),
                      ["pbf1"], ["QT"])
                for pc in range(2):
                    wv, wk = wload(wbp_in[6 + pc], 16, 512)
                    for mi in range(4):
                        m = pc * 4 + mi
                        bk, bkk = nbank()
                        for kt in range(16):
                            P(lambda e, kt=kt, wv=wv, bk=bk, mi=mi: e.matmul(
                                bk[:, 0:n], lhsT=wv[:, kt, mi * 128:(mi + 1) * 128], rhs=hT[:, kt, 0:n],
                                start=(kt == 0), stop=(kt == 15)), ["hT", wk], [bkk], sig=(kt == 15))
                        if small:
                            for si, (c0, ln) in enumerate(segs):
                                V(lambda e, bk=bk, m=m, si=si, c0=c0, ln=ln: e.tensor_copy(
                                    uexs[:, m, si, 15:15 + ln], bk[:, c0:c0 + ln]), [bkk], ["uexs"])
                        else:
                            V(lambda e, bk=bk, m=m: e.tensor_copy(uext[:, m, 15:15 + n], bk[:, 0:n]), [bkk], ["uext"])
                if small:
                    for j in range(2):
                        V(lambda e: e.memset(stg[0:16, :], 0.0), [], ["tmpf"])
                        S.dma("sp", stg[0:15, :], sp_in[j], writes=["tmpf"])
                        for m in range(8):
                            P(lambda e, m=m: e.transpose(pk[:, m * 16:m * 16 + 16], stg[0:16, m * 128:(m + 1) * 128],
                                                         idf[0:16, 0:16]), ["tmpf", "idf"], ["B0", "B1"], sig=(m == 7))
                        V(lambda e, j=j: e.tensor_copy(uexs[:, :, j, 0:15],
                                                       pk[:, 0:128].rearrange("p (m t) -> p m t", t=16)[:, :, 0:15]),
                          ["B0", "B1"], ["uexs"])
                else:
                    V(lambda e: e.tensor_copy(uext[:, :, 0:15], uh[:]), ["uh"], ["uext"])
                if small:
                    views = [(uexs[:, :, si, :], 15 + ln, c0, ln) for si, (c0, ln) in enumerate(segs)]
                else:
                    views = [(uext, 15 + n, 0, n)]
                for (uv, L, c0, ln) in views:
                    for wi, wdw in enumerate((2, 4, 8, 16)):
                        src = uv[:, 2 * wi:2 * wi + 2, :]
                        sh = 1
                        cur_src = src
                        bufs = [pa, pb_]
                        bi = 0
                        for step in range(wi + 1):
                            dst = bufs[bi][:, :, 0:L]
                            V(lambda e, dst=dst, cur_src=cur_src, sh=sh: e.tensor_tensor(
                                out=dst[:, :, sh:L], in0=cur_src[:, :, sh:L], in1=cur_src[:, :, 0:L - sh], op=ALU.add),
                              ["uext", "uexs", "tmpf"], ["tmpf"])
                            cur_src = dst
                            sh *= 2
                            bi ^= 1
                        dst = bufs[bi][:, :, 0:L]
                        V(lambda e, dst=dst, cur_src=cur_src, wdw=wdw: e.tensor_scalar(
                            out=dst[:, :, 15:L], in0=cur_src[:, :, 15:L], scalar1=1.0 / wdw, scalar2=None,
                            op0=ALU.mult), ["tmpf"], ["tmpf"])
                        if (not small) and gi == 0:
                            V(lambda e, dst=dst, cur_src=cur_src, wi=wi: e.tensor_tensor(
                                out=dst[:, :, 15:31], in0=cur_src[:, :, 15:31],
                                in1=invc[:, wi, :].unsqueeze(1).to_broadcast([128, 2, 16]), op=ALU.mult),
                              ["tmpf", "invc"], ["tmpf"])
                        V(lambda e, dst=dst, src=src, wi=wi, c0=c0, ln=ln, L=L: e.tensor_tensor(
                            out=pT[:, 2 * wi:2 * wi + 2, c0:c0 + ln], in0=dst[:, :, 15:L], in1=src[:, :, 15:L],
                            op=ALU.subtract), ["tmpf", "uext", "uexs"], ["qf"])
                if small:
                    V(lambda e: e.tensor_copy(uh[:], uexs[:, :, 2, 17:32]), ["uexs"], ["uh"])
                else:
                    V(lambda e: e.tensor_copy(uh[:], uext[:, :, 256:271]), ["uext"], ["uh"])

                def emit_pool_state(src_fn, dst_ap):
                    for m in range(8):
                        P(lambda e, m=m: e.transpose(pk[0:15, m * 128:(m + 1) * 128], src_fn(m), idf[:]),
                          ["uexs", "uh", "idf"], ["B0", "B1"], sig=(m == 7))
                    V(lambda e: e.tensor_copy(stg[0:15, :], pk[0:15, :]), ["B0", "B1"], ["tmpf"])
                    S.dma("sp", dst_ap, stg[0:15, :], reads=["tmpf"], writes=["pool_o"])
                if small:
                    for j in range(2):
                        emit_pool_state(lambda m, j=j: uexs[:, m, j, 16:31], pool_s[j])
                elif gi == NG - 1:
                    emit_pool_state(lambda m: uh[:, m, :], pool_p[:, :])
                if KSUB < 3.1:
                    return
                if small:
                    V(lambda e: e.memset(ob[:, 0, :], 0.0), [], ["ob"])
                    attend_small()
                else:
                    attend_main(gi)
                if KSUB < 3.2:
                    return
                S.barrier()
                for ti, (nr, col0) in enumerate(toks):
                    for hh in range(H):
                        P(lambda e, hh=hh: e.transpose(pbf1[:, hh, 0:nr], ob[0:nr, ti, hh * 128:(hh + 1) * 128],
                                                       idb[0:nr, 0:nr]), ["ob", "idb"], ["pbf1"], sig=(hh == H - 1))
                    V(lambda e: e.tensor_copy(oT[:, :, col0:col0 + nr], pbf1[:, :, 0:nr]), ["pbf1"], ["oT"])
                if KSUB < 3.3:
                    return
                for wi in range(4):
                    for mo in range(2):
                        bk, bkk = nbank()
                        for ki in range(2):
                            P(lambda e, ki=ki, bk=bk, wi=wi, mo=mo: e.matmul(
                                bk[:, 0:n], lhsT=wpl[:, 2 * wi + ki, mo * 128:(mo + 1) * 128],
                                rhs=pT[:, 2 * wi + ki, 0:n], start=(ki == 0), stop=(ki == 1)),
                              ["qf", "wpl"], [bkk], sig=(ki == 1))
                        A(lambda e, bk=bk, wi=wi, mo=mo: e.activation(
                            out=pbT[:, 2 * wi + mo, 0:n], in_=bk[:, 0:n], func=AF.Copy,
                            scale=psc[:, 2 * wi + mo:2 * wi + mo + 1]), [bkk, "psc"], ["qraw"])
                if KSUB < 3.33:
                    return
                for pc in range(4):
                    wga = wload(wbp_in[8 + pc], 16, 512)
                    wgb = wload(wbp_in[12 + pc], 16, 512)
                    i = ring_i[0] % 3
                    ring_i[0] += 1
                    wua_v = ring[i][:, 0:4096].rearrange("p (k n) -> p k n", n=512)
                    wup_v = ring[i][:, 4096:8192].rearrange("p (k n) -> p k n", n=512)
                    S.dma("sp", wua_v, wbp_ua[pc], reads=["wcast"], writes=["ring%d" % i])
                    S.dma("sp", wup_v, wbp_up[pc], reads=["wcast"], writes=["ring%d" % i])
                    wuk = "ring%d" % i
                    for mi in range(4):
                        m = pc * 4 + mi
                        ms = slice(mi * 128, (mi + 1) * 128)
                        specs = [(wga[0], wga[1], hT, "hT", 16), (wua_v, wuk, oT, "oT", 8),
                                 (wgb[0], wgb[1], hT, "hT", 16), (wup_v, wuk, pbT, "qraw", 8)]
                        bs = BSET[m % 2]
                        gz, gzk = (zc, "zc") if m % 2 == 0 else (zcB, "zcB")
                        gs, gsk = (szl, "szl") if m % 2 == 0 else (szlB, "szlB")
                        for bi_, (wv, wk, act, ak, nk) in enumerate(specs):
                            for kt in range(nk):
                                P(lambda e, kt=kt, wv=wv, act=act, bi_=bi_, nk=nk: e.matmul(
                                    bs[bi_][0][:, 0:n], lhsT=wv[:, kt, ms], rhs=act[:, kt, 0:n],
                                    start=(kt == 0), stop=(kt == nk - 1)), [wk, ak], [bs[bi_][1]], sig=(kt == nk - 1))
                        A(lambda e: e.activation(out=gz[:, 0:n], in_=bs[0][0][:, 0:n], func=AF.Sigmoid), [bs[0][1]], [gzk])
                        A(lambda e: e.activation(out=gs[:, 0:n], in_=bs[2][0][:, 0:n], func=AF.Sigmoid), [bs[2][1]], [gsk])
                        V(lambda e: e.tensor_tensor(out=gz[:, 0:n], in0=gz[:, 0:n], in1=bs[1][0][:, 0:n], op=ALU.mult),
                          [gzk, bs[1][1]], [gzk])
                        V(lambda e: e.tensor_tensor(out=gs[:, 0:n], in0=gs[:, 0:n], in1=bs[3][0][:, 0:n], op=ALU.mult),
                          [gsk, bs[3][1]], [gsk])
                        G(lambda e, m=m: e.tensor_tensor(out=mT[:, m, 0:n], in0=gz[:, 0:n], in1=gs[:, 0:n],
                                                         op=ALU.add), [gzk, gsk], ["mT"])
                if KSUB < 3.4:
                    return
                for cc in range(4):
                    wv, wk = wload(wbp_out[cc], 16, 512)
                    for ti, (nr, col0) in enumerate(toks):
                        bk, bkk = nbank()
                        for kt in range(16):
                            P(lambda e, kt=kt, wv=wv, bk=bk: e.matmul(
                                bk[0:nr, :], lhsT=mT[:, kt, col0:col0 + nr], rhs=wv[:, kt, :],
                                start=(kt == 0), stop=(kt == 15)), ["mT", wk], [bkk], sig=(kt == 15))
                        V(lambda e, bk=bk, cc=cc, ti=ti: e.tensor_tensor(
                            out=xg[0:nr, ti, cc * 512:(cc + 1) * 512], in0=xg[0:nr, ti, cc * 512:(cc + 1) * 512],
                            in1=bk[0:nr, :], op=ALU.add), ["xg%d" % ti, bkk], ["xg%d" % ti])
                if KSUB < 3.5:
                    return
                S.barrier()
                make_hT(toks, gF, "gF", lambda ti: (xg[0:toks[ti][0], ti, :], "xg%d" % ti))
                if small:
                    for j in range(2):
                        for r in range(6):
                            m0 = r * 8
                            mn = min(8, KF - m0)
                            S.dma("sp", stg[0:2, 0:mn * 128], sc_in[j, :, m0 * 128:(m0 + mn) * 128], writes=["tmpf"])
                            for mm in range(mn):
                                P(lambda e, mm=mm: e.transpose(pk[:, mm * 2:mm * 2 + 2],
                                                               stg[0:2, mm * 128:(mm + 1) * 128], idf[0:2, 0:2]),
                                  ["tmpf", "idf"], ["B0", "B1"], sig=(mm == mn - 1))
                            V(lambda e, j=j, m0=m0, mn=mn: e.tensor_copy(
                                sch[:, m0:m0 + mn, j, :], pk[:, 0:mn * 2].rearrange("p (m t) -> p m t", t=2)),
                              ["B0", "B1"], ["sch"])
                for m in range(KF):
                    if m % 4 == 0:
                        mn = min(4, KF - m)
                        wz = wload(wbp_fz[m // 4][:, :, 0:mn * 128], 16, mn * 128)
                        wvv = wload(wbp_fv[m // 4][:, :, 0:mn * 128], 16, mn * 128)
                    mi = m % 4
                    ms = slice(mi * 128, (mi + 1) * 128)
                    zx, zxk = ((zext, "zext"), (zextB, "zextB"))[m % 2]
                    zcc, zck = ((zc, "zc"), (zcB, "zcB"))[m % 2]
                    szz, szk = ((szl, "szl"), (szlB, "szlB"))[m % 2]
                    bz, bzk = ZR[m % 4]
                    bv, bvk = VR[m % 4]
                    for kt in range(16):
                        P(lambda e, kt=kt, bz=bz: e.matmul(bz[:, 0:n], lhsT=wz[0][:, kt, ms], rhs=hT[:, kt, 0:n],
                                                           start=(kt == 0), stop=(kt == 15)),
                          ["hT", wz[1]], [bzk], sig=(kt == 15))
                    for kt in range(16):
                        P(lambda e, kt=kt, bv=bv: e.matmul(bv[:, 0:n], lhsT=wvv[0][:, kt, ms], rhs=hT[:, kt, 0:n],
                                                           start=(kt == 0), stop=(kt == 15)),
                          ["hT", wvv[1]], [bvk], sig=(kt == 15))
                    for si, (c0, ln) in enumerate(segs):
                        if small:
                            if si < 2:
                                V(lambda e, si=si, m=m: e.tensor_copy(zx[:, si, 0:2], sch[:, m, si, :]),
                                  ["sch"], [zxk])
                            else:
                                V(lambda e, si=si: e.memset(zx[:, si, 0:2], 0.0), [], [zxk])
                        else:
                            G(lambda e, m=m: e.tensor_copy(zx[:, 0, 0:2], zh[:, m, :]), ["zh"], [zxk])
                        A(lambda e, si=si, c0=c0, ln=ln, bz=bz: e.copy(out=zx[:, si, 2:2 + ln], in_=bz[:, c0:c0 + ln]),
                          [bzk], [zxk])
                        V(lambda e, si=si, c0=c0, ln=ln, m=m: e.tensor_scalar(
                            out=zcc[:, c0:c0 + ln], in0=zx[:, si, 2:2 + ln], scalar1=cw[:, m, 2:3],
                            scalar2=cb[:, m:m + 1], op0=ALU.mult, op1=ALU.add), [zxk, "cw", "cb"], [zck])
                        for tap in (1, 0):
                            V(lambda e, si=si, c0=c0, ln=ln, m=m, tap=tap: e.scalar_tensor_tensor(
                                out=zcc[:, c0:c0 + ln], in0=zx[:, si, tap:tap + ln], scalar=cw[:, m, tap:tap + 1],
                                in1=zcc[:, c0:c0 + ln], op0=ALU.mult, op1=ALU.add), [zxk, "cw", zck], [zck])
                        A(lambda e, c0=c0, ln=ln: e.activation(out=szz[:, c0:c0 + ln], in_=zcc[:, c0:c0 + ln],
                                                               func=AF.Silu), [zck], [szk])
                        V(lambda e, c0=c0, ln=ln, m=m, bv=bv: e.tensor_tensor(
                            out=aT[:, m, c0:c0 + ln], in0=szz[:, c0:c0 + ln], in1=bv[:, c0:c0 + ln], op=ALU.mult),
                          [szk, bvk], ["aT"])
                        if small:
                            if si < 2:
                                V(lambda e, si=si, m=m: e.tensor_copy(zsave[:, m, si, :], zx[:, si, 16:18]),
                                  [zxk], ["zsave"])
                            else:
                                V(lambda e, m=m: e.tensor_scalar(out=zh[:, m, :], in0=zx[:, 2, 17:19],
                                                                 scalar1=hv[:, 0:1], scalar2=None, op0=ALU.mult),
                                  [zxk, "hv"], ["zh"])
                        else:
                            G(lambda e, m=m: e.tensor_copy(zh[:, m, :], zx[:, 0, 256:258]), [zxk], ["zh"])
                    if small and n > 81:
                        pass
                if small:
                    for (g0, g1) in ((16, 32), (48, 64), (81, SM)):
                        V(lambda e, g0=g0, g1=g1: e.memset(aT[:, :, g0:g1], 0.0), ["aT"], ["aT"])
                if KSUB < 3.6:
                    return
                deferred = []
                for cc in range(4):
                    accb = []
                    for ti in range(len(toks)):
                        accb.append(nbank())
                    for pcs, (k0, kn) in enumerate(((0, 16), (16, 16), (32, 11))):
                        wv, wk = wload(wbp_fo[cc][:, k0:k0 + kn, :], kn, 512)
                        for ti, (nr, col0) in enumerate(toks):
                            bk, bkk = accb[ti]
                            if pcs == 2 and ti == 0:
                                for dfn in deferred:
                                    dfn()
                                deferred = []
                            for kt in range(kn):
                                P(lambda e, kt=kt, wv=wv, bk=bk, k0=k0: e.matmul(
                                    bk[0:nr, :], lhsT=aT[:, k0 + kt, col0:col0 + nr], rhs=wv[:, kt, :],
                                    start=(k0 + kt == 0), stop=(k0 + kt == KF - 1)),
                                  ["aT", wk], [bkk], sig=(kt == kn - 1))
                    for ti, (nr, col0) in enumerate(toks):
                        bk, bkk = accb[ti]
                        yb = (cc * 2 + ti) % 2
                        V(lambda e, bk=bk, yb=yb, ti=ti, cc=cc: e.tensor_tensor(
                            out=yst[yb][0:nr, :], in0=xg[0:nr, ti, cc * 512:(cc + 1) * 512], in1=bk[0:nr, :],
                            op=ALU.add), ["xg%d" % ti, bkk], ["yst%d" % yb])
                        if small:
                            for j in range(2):
                                deferred.append(lambda j=j, cc=cc, yb=yb: S.dma(
                                    "sp", y_s[j, :, cc * 512:(cc + 1) * 512], yst[yb][32 * j:32 * j + 16, :],
                                    reads=["yst%d" % yb], writes=["y_s"]))
                        else:
                            r0 = (2 * gi + ti) * 128
                            deferred.append(lambda r0=r0, cc=cc, yb=yb: S.dma(
                                "sp", y_p[r0:r0 + 128, cc * 512:(cc + 1) * 512], yst[yb][:, :],
                                reads=["yst%d" % yb], writes=["y_p"]))
                for dfn in deferred:
                    dfn()
                S.barrier()

            def emit_conv_state(src_fn, dst_ap):
                for r in range(6):
                    m0 = r * 8
                    mn = min(8, KF - m0)
                    for mm in range(mn):
                        P(lambda e, mm=mm: e.transpose(pk[0:2, mm * 128:(mm + 1) * 128], src_fn(m0 + mm), idf[:]),
                          ["zsave", "zh", "idf"], ["B0", "B1"], sig=(mm == mn - 1))
                    V(lambda e, mn=mn: e.tensor_copy(stg[0:2, 0:mn * 128], pk[0:2, 0:mn * 128]), ["B0", "B1"], ["tmpf"])
                    S.dma("sp", dst_ap[:, m0 * 128:(m0 + mn) * 128], stg[0:2, 0:mn * 128], reads=["tmpf"],
                          writes=["conv_o"])

            if KSTOP >= 3 and KSMALL:
                do_group("small", -1)
                for j in range(2):
                    emit_conv_state(lambda m, j=j: zsave[:, m, j, :], conv_s[j])
            for gi in range(KGROUPS if KSTOP >= 4 else 0):
                do_group("main", gi)
            if KSTOP >= 4:
                emit_conv_state(lambda m: zh[:, m, :], conv_p[:, :])
            S.barrier()
    S.sync_all(["sp"])
    print("sbuf remaining", nc.sbuf_bytes_remaining)
    print("program: ins", S.nins, "waits", S.nwait, "cnt", S.cnt, flush=True)
    return nc


_NC_CACHE = {}


def kernel(_prep_only=False, **inp):
    f = lambda a: np.ascontiguousarray(np.asarray(a, dtype=np.float32))
    x_prompt = f(inp["x_prompt"])[0]
    x_sample = f(inp["x_sample"])
    cache_k = f(inp["cache_k"])[0].reshape(16, 2048, 1024)
    cache_v = f(inp["cache_v"])[0].reshape(16, 2048, 1024)
    state_pool = f(inp["state_pool"])[0]
    state_conv = f(inp["state_conv"])[0]
    shared = {
        "ident": np.eye(128, dtype=np.float32),
        "g_attn": f(inp["attn_norm_g"]).reshape(1, D),
        "g_ffn": f(inp["ffn_norm_g"]).reshape(1, D),
        "gq_t": np.tile(f(inp["q_norm_g"]).reshape(1, 128), (1, 8)),
        "gk_t": np.tile(f(inp["k_norm_g"]).reshape(1, 128), (1, 8)),
        "sg_t": np.tile(f(inp["subln_g"]).reshape(1, 128), (1, 8)),
        "lq1": f(inp["lambda_q1"]).reshape(1, 64), "lk1": f(inp["lambda_k1"]).reshape(1, 64),
        "lq2": f(inp["lambda_q2"]).reshape(1, 64), "lk2": f(inp["lambda_k2"]).reshape(1, 64),
        "psc_t": np.ascontiguousarray(f(inp["pool_scale"]).reshape(8, 128).T),
        "cw_t": np.ascontiguousarray(f(inp["conv_w"])[0].reshape(3, KF, 128).transpose(2, 1, 0).reshape(128, KF * 3)),
        "cb_t": np.ascontiguousarray(f(inp["conv_b"]).reshape(KF, 128).T),
        "w_in": f(inp["w_in"])[0], "w_pool": f(inp["w_pool"])[0].reshape(1024, 256),
        "w_ua": f(inp["w_up_attn"])[0], "w_up": f(inp["w_up_pool"])[0], "w_out": f(inp["w_out"])[0],
        "w_fi": f(inp["w_ffn_in"])[0], "w_fo": f(inp["w_ffn_out"])[0],
    }
    in_maps = []
    for c in range(NCORE):
        xs = np.zeros((NSL * 128, D), np.float32)
        ntrue = (16 * c + 16) * 128
        xs[NPT * 128 - ntrue:NPT * 128] = x_prompt[0:ntrue]
        sm = xs[NPT * 128:]
        sm[0:16] = x_sample[2 * c]
        sm[32:48] = x_sample[2 * c + 1]
        if c > 0:
            sm[64:81] = x_prompt[2048 * c - 17:2048 * c]
        kvalid = np.zeros((128, NSL), np.float32)
        kvalid[:, NPT - 16 * (c + 1):] = 1.0
        invcnt = np.zeros((4, 16), np.float32)
        for wi, w in enumerate((2, 4, 8, 16)):
            for t in range(16):
                invcnt[wi, t] = (1.0 / min(w, t + 1)) if c == 0 else 1.0 / w
        m = dict(shared)
        m.update({
            "xs": xs, "kvalid": kvalid,
            "ck": np.ascontiguousarray(cache_k[2 * c:2 * c + 2]), "cv": np.ascontiguousarray(cache_v[2 * c:2 * c + 2]),
            "sp_in": np.ascontiguousarray(state_pool[2 * c:2 * c + 2]),
            "sc_in": np.ascontiguousarray(state_conv[2 * c:2 * c + 2]),
            "invcnt": invcnt.reshape(1, 64),
            "hvalid": np.full((128, 1), 0.0 if c == 0 else 1.0, np.float32),
        })
        in_maps.append(m)
    if _prep_only:
        return in_maps
    if "nc" not in _NC_CACHE:
        _NC_CACHE["nc"] = build()
    res = run_bass_kernel_spmd(_NC_CACHE["nc"], in_maps, core_ids=list(range(NCORE)))
    R = res.results
    cat = lambda k: np.concatenate([np.asarray(R[c][k]) for c in range(NCORE)], axis=0)
    y_p = cat("y_p").reshape(1, 16384, D)
    y_s = cat("y_s").reshape(16, 16, D)
    k_p = cat("k_p").reshape(1, 1, 16384, 8, 128)
    v_p = cat("v_p").reshape(1, 1, 16384, 8, 128)
    pool_p = np.asarray(R[NCORE - 1]["pool_p"]).reshape(1, 1, 15, 1024)
    conv_p = np.asarray(R[NCORE - 1]["conv_p"]).reshape(1, 1, 2, DFF)
    k_s = cat("k_s").reshape(1, 16, 16, 8, 128)
    v_s = cat("v_s").reshape(1, 16, 16, 8, 128)
    pool_s = cat("pool_s").reshape(1, 16, 15, 1024)
    conv_s = cat("conv_s").reshape(1, 16, 2, DFF)
    return tuple(np.ascontiguousarray(a, dtype=np.float32) for a in
                 (y_p, y_s, k_p, v_p, pool_p, conv_p, k_s, v_s, pool_s, conv_s))
```

```python
import os
import numpy as np
from contextlib import ExitStack
import concourse.bass as bass
import concourse.mybir as mybir
from concourse.bass_utils import run_bass_kernel_spmd

F32 = mybir.dt.float32
BF16 = mybir.dt.bfloat16
AF = mybir.ActivationFunctionType
ALU = mybir.AluOpType
AX = mybir.AxisListType

D = 2048
H = 8
DFF = 5504
KF = 43
NPT = 128
NSL = 129
OWN0 = 112
NG = 8
EPS = 1e-6
LAM_INIT = 0.2
NCORE = 8
SM = 96
KSTOP = float(os.environ.get("KSTOP", "99"))
KTILES = int(os.environ.get("KTILES", "%d" % 129))
KSUB = float(os.environ.get("KSUB", "99"))
KSMALL = int(os.environ.get("KSMALL", "1"))
KGROUPS = int(os.environ.get("KGROUPS", "8"))
KATT = os.environ.get("KATT", "")


class _Tok:
    __slots__ = ("sem", "val", "eng")

    def __init__(self, sem, val, eng):
        self.sem = sem
        self.val = val
        self.eng = eng


class Sched:
    NDS = 40

    def __init__(self, nc):
        self.nc = nc
        self.eng = {"pe": nc.tensor, "act": nc.scalar, "dve": nc.vector,
                    "pool": nc.gpsimd, "sp": nc.sync}
        self.sem = {e: nc.alloc_semaphore("s_" + e) for e in ("pe", "act", "dve", "pool")}
        self.cnt = {e: 0 for e in self.sem}
        self.cur = {e: _Tok(self.sem[e], None, e) for e in self.sem}
        self.dsem = [nc.alloc_semaphore("d%d" % i) for i in range(self.NDS)]
        self.dcnt = [0] * self.NDS
        self.dnext = 0
        self.waited = {e: {} for e in self.eng}
        self.lastw = {}
        self.readers = {}
        self.nwait = 0
        self.nins = 0

    def _wait(self, e, tok):
        if tok is None:
            return
        if tok.val is None:
            if tok.eng == e:
                return
            raise RuntimeError("wait on unresolved token of %s from %s" % (tok.eng, e))
        w = self.waited[e]
        sid = id(tok.sem)
        if w.get(sid, 0) >= tok.val:
            return
        self.eng[e].wait_ge(tok.sem, tok.val)
        self.nwait += 1
        w[sid] = tok.val

    def _deps(self, e, reads, writes):
        toks = []
        for k in reads:
            toks.append(self.lastw.get(k))
        for k in writes:
            toks.append(self.lastw.get(k))
            toks.extend(self.readers.get(k, {}).values())
        best = {}
        for t in toks:
            if t is None:
                continue
            if t.eng == "pe" and e == "pe":
                continue
            if t.eng == e and t.val is not None and t.val <= self.cnt[e] - 3:
                continue
            if t.val is None:
                self._wait(e, t)
                continue
            sid = id(t.sem)
            if sid not in best or best[sid].val < t.val:
                best[sid] = t
        for t in best.values():
            self._wait(e, t)

    def _record(self, tok, reads, writes):
        for k in reads:
            self.readers.setdefault(k, {})[id(tok.sem)] = tok
        for k in writes:
            self.lastw[k] = tok
            self.readers[k] = {}

    def op(self, e, fn, reads=(), writes=(), signal=True):
        self._deps(e, reads, writes)
        ins = fn(self.eng[e])
        self.nins += 1
        tok = self.cur[e]
        self._record(tok, reads, writes)
        if signal:
            ins.then_inc(self.sem[e], 1)
            self.cnt[e] += 1
            tok.val = self.cnt[e]
            self.cur[e] = _Tok(self.sem[e], None, e)
        return ins

    def dma(self, q, out, in_, reads=(), writes=(), **kw):
        self._deps(q, reads, writes)
        i = self.dnext
        self.dnext = (i + 1) % self.NDS
        if self.dcnt[i]:
            self._wait(q, _Tok(self.dsem[i], self.dcnt[i], "dma"))
        ins = self.eng[q].dma_start(out=out, in_=in_, **kw)
        self.dcnt[i] += 16
        ins.then_inc(self.dsem[i], 16)
        self.nins += 1
        self._record(_Tok(self.dsem[i], self.dcnt[i], "dma"), reads, writes)
        return ins

    def sync_all(self, engines):
        for e in engines:
            for i in range(self.NDS):
                if self.dcnt[i]:
                    self._wait(e, _Tok(self.dsem[i], self.dcnt[i], "dma"))
            for x in self.sem:
                if x != e and self.cnt[x]:
                    assert self.cur[x].val is None
                    self._wait(e, _Tok(self.sem[x], self.cnt[x], x))

    def barrier(self):
        self.sync_all(["pe", "act", "dve", "pool", "sp"])
        self.lastw = {}
        self.readers = {}


def build():
    nc = bass.Bass("TRN2", target_bir_lowering=False)
    S = Sched(nc)

    def din(name, shape):
        return nc.dram_tensor(name, list(shape), F32, kind="ExternalInput").ap()

    def dout(name, shape):
        return nc.dram_tensor(name, list(shape), F32, kind="ExternalOutput").ap()

    def dscr(name, shape, dt=BF16):
        return nc.dram_tensor(name, list(shape), dt).ap()

    xs = din("xs", [NSL * 128, D])
    kvalid = din("kvalid", [128, NSL])
    ck = din("ck", [2, 2048, 1024])
    cv = din("cv", [2, 2048, 1024])
    sp_in = din("sp_in", [2, 15, 1024])
    sc_in = din("sc_in", [2, 2, DFF])
    invcnt = din("invcnt", [1, 64])
    hvalid = din("hvalid", [128, 1])
    ident = din("ident", [128, 128])
    g_attn = din("g_attn", [1, D])
    g_ffn = din("g_ffn", [1, D])
    gq_t = din("gq_t", [1, 1024])
    gk_t = din("gk_t", [1, 1024])
    sg_t = din("sg_t", [1, 1024])
    lq1 = din("lq1", [1, 64])
    lk1 = din("lk1", [1, 64])
    lq2 = din("lq2", [1, 64])
    lk2 = din("lk2", [1, 64])
    psc_t = din("psc_t", [128, 8])
    cw_t = din("cw_t", [128, KF * 3])
    cb_t = din("cb_t", [128, KF])
    w_in = din("w_in", [D, 8192])
    w_pool = din("w_pool", [1024, 256])
    w_ua = din("w_ua", [1024, D])
    w_up = din("w_up", [1024, D])
    w_out = din("w_out", [D, D])
    w_fi = din("w_fi", [D, 2 * DFF])
    w_fo = din("w_fo", [DFF, D])

    y_p = dout("y_p", [2048, D])
    y_s = dout("y_s", [2, 16, D])
    k_p = dout("k_p", [2048, 1024])
    v_p = dout("v_p", [2048, 1024])
    pool_p = dout("pool_p", [15, 1024])
    conv_p = dout("conv_p", [2, DFF])
    k_s = dout("k_s", [2, 16, 1024])
    v_s = dout("v_s", [2, 16, 1024])
    pool_s = dout("pool_s", [2, 15, 1024])
    conv_s = dout("conv_s", [2, 2, DFF])

    wbp_in = dscr("wbp_in", [16, 128, 16, 512])
    wb_pool = dscr("wb_pool", [1024, 256])
    wbp_ua = dscr("wbp_ua", [4, 128, 8, 512])
    wbp_up = dscr("wbp_up", [4, 128, 8, 512])
    wbp_out = dscr("wbp_out", [4, 128, 16, 512])
    wbp_fz = dscr("wbp_fz", [11, 128, 16, 512])
    wbp_fv = dscr("wbp_fv", [11, 128, 16, 512])
    wbp_fo = dscr("wbp_fo", [4, 128, KF, 512])
    KTs = dscr("KTs", [H, 128, NSL * 128])
    VXs = dscr("VXs", [H, 128, NSL, 130])
    KTc = dscr("KTc", [2, H, 128, 2048])
    VXc = dscr("VXc", [2, H, 128, 16, 130])

    pk = nc.alloc_psum_tensor("pk", [128, 1024], F32)
    pv = nc.alloc_psum_tensor("pv", [128, 1024], F32)
    pf4 = nc.alloc_psum_tensor("pf4", [128, 512], F32)
    pf5 = nc.alloc_psum_tensor("pf5", [128, 512], F32)
    pbf0 = nc.alloc_psum_tensor("pbf0", [128, 8, 128], BF16)
    pbf1 = nc.alloc_psum_tensor("pbf1", [128, 8, 128], BF16)
    B = [pk[:, 0:512], pk[:, 512:1024], pv[:, 0:512], pv[:, 512:1024], pf4[:], pf5[:]]
    BK = ["B0", "B1", "B2", "B3", "B4", "B5"]
    pbf0f = pbf0[:].rearrange("p a b -> p (a b)").bitcast(F32)
    pbf1f = pbf1[:].rearrange("p a b -> p (a b)").bitcast(F32)
    STB = [(pk, ("B0", "B1")), (pv, ("B2", "B3"))]
    ACC = [(pf4[:], "B4"), (pf5[:], "B5"), (pbf0f, "pbf0"), (pbf1f, "pbf1")]
    BSET = [[(B[0], "B0"), (B[1], "B1"), (B[2], "B2"), (B[3], "B3")], ACC]
    ZR = [(B[0], "B0"), (B[1], "B1"), (pf4[:], "B4"), (pf5[:], "B5")]
    VR = [(B[2], "B2"), (B[3], "B3"), (pbf0f, "pbf0"), (pbf1f, "pbf1")]

    def A(fn, r, w):
        return S.op("act", fn, r, w)

    def V(fn, r, w):
        return S.op("dve", fn, r, w)

    def G(fn, r, w):
        return S.op("pool", fn, r, w)

    def P(fn, r, w, sig=True):
        return S.op("pe", fn, r, w, signal=sig)

    casts = []

    def add_cast(dst, src, rows, c_lo, c_hi):
        for r0 in range(0, rows, 128):
            r1 = min(rows, r0 + 128)
            for c0 in range(c_lo, c_hi, 2048):
                c1 = min(c_hi, c0 + 2048)
                casts.append((dst[r0:r1, c0:c1], src[r0:r1, c0:c1]))

    def add_cast_p(dst4, src2, rows, c_lo, c_hi, piece0):
        for kt in range(rows // 128):
            r0 = kt * 128
            c0 = c_lo
            while c0 < c_hi:
                c1 = min(c_hi, c0 + 2048)
                nfull = (c1 - c0) // 512
                pj = piece0 + (c0 - c_lo) // 512
                if nfull:
                    casts.append((dst4[pj:pj + nfull, :, kt, :].rearrange("j p c -> p j c"),
                                  src2[r0:r0 + 128, c0:c0 + nfull * 512].rearrange("p (j c) -> p j c", c=512)))
                rem = (c1 - c0) - nfull * 512
                if rem:
                    casts.append((dst4[pj + nfull, :, kt, 0:rem], src2[r0:r0 + 128, c0 + nfull * 512:c1]))
                c0 = c1

    add_cast_p(wbp_in, w_in, D, 0, 1024, 0)
    add_cast_p(wbp_in, w_in, D, 3072, 8192, 6)
    add_cast(wb_pool, w_pool, 1024, 0, 256)
    add_cast_p(wbp_ua, w_ua, 1024, 0, D, 0)
    add_cast_p(wbp_up, w_up, 1024, 0, D, 0)
    add_cast_p(wbp_out, w_out, D, 0, D, 0)
    add_cast_p(wbp_fz, w_fi, D, 0, DFF, 0)
    add_cast_p(wbp_fv, w_fi, D, DFF, 2 * DFF, 0)
    add_cast_p(wbp_fo, w_fo, DFF, 0, D, 0)
    cast_pos = [0]

    def emit_casts(n):
        for _ in range(n):
            if cast_pos[0] < len(casts):
                d, s = casts[cast_pos[0]]
                cast_pos[0] += 1
                S.dma("pool", d, s, writes=["wcast"])

    with ExitStack() as top:
        def sb(name, shape, dt, st=top):
            return st.enter_context(nc.sbuf_tensor(name, list(shape), dt))

        idf = sb("idf", [128, 128], F32)
        idb = sb("idb", [128, 128], BF16)
        gA = sb("gA", [128, D], F32)
        kval = sb("kval", [128, NSL], F32)
        ss = sb("ss", [128, 4], F32)
        rs = sb("rs", [128, 4], F32)
        ssk = sb("ssk", [128, 16], F32)
        rk = sb("rk", [128, 16], F32)
        lamc = sb("lamc", [128, 8], F32)
        S.dma("sp", idf[:], ident[:, :], writes=["idf"])
        V(lambda e: e.tensor_copy(idb[:], idf[:]), ["idf"], ["idb"])
        S.dma("sp", gA[:], g_attn.partition_broadcast(128), writes=["gA"])
        S.dma("sp", kval[:], kvalid[:, :], writes=["kval"])

        def rstd_cols(src_ap, n, width, junk_ap, col, rkeys, jkey):
            A(lambda e: e.activation(out=junk_ap, in_=src_ap, func=AF.Square, accum_out=ss[0:n, col:col + 1]),
              rkeys, [jkey, "ss%d" % col])
            A(lambda e: e.activation(out=rs[0:n, col:col + 1], in_=ss[0:n, col:col + 1], func=AF.Ln,
                                     scale=1.0 / width, bias=EPS), ["ss%d" % col], ["rs%d" % col])
            A(lambda e: e.activation(out=rs[0:n, col:col + 1], in_=rs[0:n, col:col + 1], func=AF.Exp,
                                     scale=-0.5), ["rs%d" % col], ["rs%d" % col])

        def headnorm(raw, n, sqbuf, outf, gtile, rkey, sqkey, okey, gkey):
            A(lambda e: e.activation(out=sqbuf[0:n, 0:1024], in_=raw, func=AF.Square), [rkey], [sqkey])
            V(lambda e: e.reduce_sum(out=ssk[0:n, :], in_=sqbuf[0:n, 0:1024].rearrange("p (g d) -> p g d", d=64),
                                     axis=AX.X), [sqkey], ["ssk"])
            A(lambda e: e.activation(out=rk[0:n, :], in_=ssk[0:n, :], func=AF.Ln, scale=1.0 / 64, bias=EPS),
              ["ssk"], ["rk"])
            A(lambda e: e.activation(out=rk[0:n, :], in_=rk[0:n, :], func=AF.Exp, scale=-0.5), ["rk"], ["rk"])
            V(lambda e: e.tensor_tensor(out=outf.rearrange("p (g d) -> p g d", d=64),
                                        in0=raw.rearrange("p (g d) -> p g d", d=64),
                                        in1=rk[0:n, :].unsqueeze(2).to_broadcast([n, 16, 64]), op=ALU.mult),
              [rkey, "rk"], [okey])
            V(lambda e: e.tensor_tensor(out=outf, in0=outf, in1=gtile[0:n, :], op=ALU.mult), [okey, gkey], [okey])

        with ExitStack() as p1:
            def sb1(name, shape, dt):
                return sb(name, shape, dt, p1)

            wkv = sb1("wkv", [128, 16, 2048], BF16)
            gk = sb1("gk", [128, 1024], F32)
            xt = [sb1("xt%d" % i, [128, D], F32) for i in range(3)]
            tmp = sb1("tmp", [128, D], F32)
            hb = [sb1("hb%d" % i, [128, D], BF16) for i in range(2)]
            hT = [sb1("hT%d" % i, [128, 16, 128], BF16) for i in range(2)]
            kraw = sb1("kraw", [128, 1024], F32)
            kf = [sb1("kf%d" % i, [128, 1024], F32) for i in range(2)]
            kb = sb1("kb", [128, 1024], BF16)
            vf = [sb1("vf%d" % i, [128, 1024], F32) for i in range(2)]
            vx = [sb1("vx%d" % i, [128, 8, 4, 130], BF16) for i in range(2)]
            ktt = [sb1("ktt%d" % i, [128, 8, 512], BF16) for i in range(2)]

            S.dma("sp", gk[:], gk_t.partition_broadcast(128), writes=["gk"])
            for b_ in range(2):
                V(lambda e, b_=b_: e.memset(vx[b_][:], 0.0), [], ["vx%d" % b_])
            for kt in range(16):
                S.dma("pool", wkv[:, kt, :], w_in[kt * 128:(kt + 1) * 128, 1024:3072], writes=["wkv"])

            def ktrans(s_):
                b4 = (s_ // 4) % 2
                j4 = s_ % 4
                for hh in range(H):
                    P(lambda e, hh=hh: e.transpose(pbf1[:, hh, :], kb[:, hh * 128:(hh + 1) * 128], idb[:]),
                      ["kb", "idb"], ["pbf1"], sig=(hh == H - 1))
                V(lambda e: e.tensor_copy(ktt[b4][:, :, j4 * 128:(j4 + 1) * 128], pbf1[:]), ["pbf1"], ["ktt%d" % b4])
                if j4 == 3 or s_ == NSL - 1:
                    s0 = s_ - j4
                    cnt = j4 + 1
                    S.dma("sp", KTs[:, :, s0 * 128:(s0 + cnt) * 128].rearrange("h p t -> p h t"),
                          ktt[b4][:, :, 0:cnt * 128], reads=["ktt%d" % b4], writes=["KTs"])

            sqb = sb1("sqb", [128, 1024], F32)

            def tiles_iter():
                return [t for t in range(NSL) if not (t >= KTILES and t < NSL - 2)]

            def xload(s):
                b3 = s % 3
                S.dma("sp", xt[b3][:], xs[s * 128:(s + 1) * 128, :], writes=["xt%d" % b3])

            def hchain(s):
                b = s % 2
                b3 = s % 3
                xk, hk = "xt%d" % b3, "hb%d" % b
                rstd_cols(xt[b3][:], 128, D, hb[b][:], 0, [xk], hk)
                A(lambda e: e.activation(out=tmp[:], in_=xt[b3][:], func=AF.Copy, scale=rs[:, 0:1]),
                  [xk, "rs0"], ["tmp"])
                G(lambda e: e.tensor_tensor(out=hb[b][:], in0=tmp[:], in1=gA[:], op=ALU.mult),
                  ["tmp", "gA"], [hk])

            def hT_make(s):
                b = s % 2
                hk, hTk = "hb%d" % b, "hT%d" % b
                for half in range(2):
                    for j in range(8):
                        kt = half * 8 + j
                        P(lambda e, kt=kt, j=j: e.transpose(pbf0[:, j, :], hb[b][:, kt * 128:(kt + 1) * 128], idb[:]),
                          [hk, "idb"], ["pbf0"], sig=(j == 7))
                    V(lambda e, half=half: e.tensor_copy(hT[b][:, half * 8:(half + 1) * 8, :], pbf0[:]),
                      ["pbf0"], [hTk])

            def mm(s):
                b = s % 2
                hTk = "hT%d" % b
                for cc in range(4):
                    dst = pk if cc < 2 else pv
                    c0 = (cc % 2) * 512
                    for kt in range(16):
                        P(lambda e, kt=kt, cc=cc, dst=dst, c0=c0: e.matmul(
                            dst[:, c0:c0 + 512], lhsT=hT[b][:, kt, :], rhs=wkv[:, kt, cc * 512:(cc + 1) * 512],
                            start=(kt == 0), stop=(kt == 15)),
                          [hTk, "wkv"], ["pk" if cc < 2 else "pv"], sig=(kt == 15))

            def evac(s):
                b = s % 2
                V(lambda e: e.tensor_copy(kraw[:], pk[:]), ["pk"], ["kraw"])
                A(lambda e: e.copy(out=vf[b][:], in_=pv[:]), ["pv"], ["vf%d" % b])

            def tail(s):
                b = s % 2
                headnorm(kraw[:], 128, sqb, kf[b][:], gk, "kraw", "sqb", "kf%d" % b, "gk")
                V(lambda e: e.tensor_copy(kb[:], kf[b][:]), ["kf%d" % b], ["kb"])
                b4 = (s // 4) % 2
                j4 = s % 4
                A(lambda e: e.copy(out=vx[b4][:, :, j4, 0:128], in_=vf[b][:].rearrange("p (h d) -> p h d", d=128)),
                  ["vf%d" % b], ["vx%d" % b4])
                V(lambda e: e.tensor_copy(vx[b4][:, :, j4, 128:129],
                                          kval[:, s:s + 1].unsqueeze(1).to_broadcast([128, 8, 1])),
                  ["kval"], ["vx%d" % b4])
                if j4 == 3 or s == NSL - 1:
                    s0 = s - j4
                    cnt = j4 + 1
                    S.dma("sp", VXs[:, :, s0:s0 + cnt, :].rearrange("h p j e -> p h j e"), vx[b4][:, :, 0:cnt, :],
                          reads=["vx%d" % b4], writes=["VXs"])
                if OWN0 <= s < NPT:
                    r0 = (s - OWN0) * 128
                    S.dma("sp", k_p[r0:r0 + 128, :], kf[b][:], reads=["kf%d" % b], writes=["k_p"])
                    S.dma("sp", v_p[r0:r0 + 128, :], vf[b][:], reads=["vf%d" % b], writes=["v_p"])
                if s == NPT:
                    for j in range(2):
                        S.dma("sp", k_s[j], kf[b][32 * j:32 * j + 16, :], reads=["kf%d" % b], writes=["k_s"])
                        S.dma("sp", v_s[j], vf[b][32 * j:32 * j + 16, :], reads=["vf%d" % b], writes=["v_s"])

            tl = tiles_iter()
            xload(tl[0])
            xload(tl[1])
            hchain(tl[0])
            hT_make(tl[0])
            for idx, s in enumerate(tl):
                nxt = tl[idx + 1] if idx + 1 < len(tl) else None
                if idx + 2 < len(tl):
                    xload(tl[idx + 2])
                if nxt is not None:
                    hchain(nxt)
                emit_casts(2)
                mm(s)
                evac(s)
                if nxt is not None:
                    hT_make(nxt)
                if idx > 0:
                    ktrans(tl[idx - 1])
                tail(s)
            ktrans(tl[-1])

            for j in range(2 if KSTOP >= 2 else 0):
                for t in range(16):
                    i = j * 16 + t
                    b = i % 2
                    emit_casts(2)
                    S.dma("sp", xt[b][:, 0:1024], ck[j, t * 128:(t + 1) * 128, :], writes=["xt%d" % b])
                    S.dma("sp", xt[b][:, 1024:2048], cv[j, t * 128:(t + 1) * 128, :], writes=["xt%d" % b])
                    for hh in range(H):
                        P(lambda e, hh=hh: e.transpose(pk[:, hh * 128:(hh + 1) * 128],
                                                       xt[b][:, hh * 128:(hh + 1) * 128], idf[:]),
                          ["xt%d" % b, "idf"], ["pk"], sig=(hh == H - 1))
                    V(lambda e: e.tensor_copy(ktt[b][:, :, 0:128], pk[:].rearrange("p (h t) -> p h t", t=128)),
                      ["pk"], ["ktt%d" % b])
                    S.dma("sp", KTc[j, :, :, t * 128:(t + 1) * 128].rearrange("h p t -> p h t"), ktt[b][:, :, 0:128],
                          reads=["ktt%d" % b], writes=["KTc"])
                    G(lambda e: e.tensor_copy(vx[b][:, :, 0, 0:128],
                                              xt[b][:, 1024:2048].rearrange("p (h d) -> p h d", d=128)),
                      ["xt%d" % b], ["vx%d" % b])
                    V(lambda e: e.tensor_copy(vx[b][:, :, 0, 128:129],
                                              kval[:, NPT:NPT + 1].unsqueeze(1).to_broadcast([128, 8, 1])),
                      ["kval"], ["vx%d" % b])
                    S.dma("sp", VXc[j, :, :, t, :].rearrange("h p e -> p h e"), vx[b][:, :, 0, :],
                          reads=["vx%d" % b], writes=["VXc"])
            emit_casts(len(casts))
            S.barrier()

        with ExitStack() as p2:
            def sb2(name, shape, dt):
                return sb(name, shape, dt, p2)

            NC_ = 256
            gF = sb2("gF", [128, D], F32)
            gq = sb2("gq", [128, 1024], F32)
            sg = sb2("sg", [128, 1024], F32)
            invc = sb2("invc", [128, 4, 16], F32)
            hv = sb2("hv", [128, 1], F32)
            psc = sb2("psc", [128, 8], F32)
            cw = sb2("cw", [128, KF, 3], F32)
            cb = sb2("cb", [128, KF], F32)
            lt = [sb2("lt%d" % i, [128, 64], F32) for i in range(4)]
            wpl = sb2("wpl", [128, 8, 256], BF16)
            xg = sb2("xg", [128, 2, D], F32)
            hT = sb2("hTm", [128, 16, NC_], BF16)
            tmpf = sb2("tmpf", [128, D], F32)
            hb = sb2("hbm", [128, D], BF16)
            qraw = sb2("qraw", [128, 1024], F32)
            qf = sb2("qf", [128, 1024], F32)
            qb = sb2("qb", [128, 1024], BF16)
            RA = sb2("RA", [128, 5504], F32)
            RB = sb2("RB", [128, 7728], F32)
            ring = [sb2("ring%d" % i, [128, 8192], BF16) for i in range(3)]
            zext = sb2("zext", [128, 3, 258], F32)
            zextB = sb2("zextB", [128, 3, 258], F32)
            zcB = sb2("zcB", [128, NC_], F32)
            szlB = sb2("szlB", [128, NC_], F32)
            zc = sb2("zc", [128, NC_], F32)
            szl = sb2("szl", [128, NC_], F32)
            zh = sb2("zh", [128, KF, 2], F32)
            sch = sb2("sch", [128, KF, 2, 2], F32)
            uh = sb2("uh", [128, 8, 15], F32)
            uexs = sb2("uexs", [128, 8, 3, 32], F32)
            zsave = sb2("zsave", [128, KF, 2, 2], F32)
            l12 = sb2("l12", [128, 4], F32)
            ofin = sb2("ofin", [128, 128], F32)
            t1f = sb2("t1f", [128, 128], F32)
            stg = tmpf[:, 0:1024]
            yst = [sb2("yst%d" % i, [128, 512], F32) for i in range(2)]

            RAb = RA[:].bitcast(BF16)
            RBb = RB[:].bitcast(BF16)
            aT = RAb[:, 0:KF * NC_].rearrange("p (k n) -> p k n", n=NC_)
            QT = RAb[:, 0:16 * NC_].rearrange("p (c k n) -> p c k n", c=2, n=NC_)
            uext = RA[:, 2048:4216].rearrange("p (k n) -> p k n", n=271)
            pa = tmpf[:, 0:542].rearrange("p (k n) -> p k n", n=271)
            pb_ = tmpf[:, 542:1084].rearrange("p (k n) -> p k n", n=271)
            pT = qf[:].bitcast(BF16).rearrange("p (k n) -> p k n", n=NC_)
            pbT = qraw[:].bitcast(BF16).rearrange("p (k n) -> p k n", n=NC_)
            ob = RAb[:, 8552:8552 + 2048].rearrange("p (t f) -> p t f", f=1024)
            KTr = [RBb[:, i * 2048:(i + 1) * 2048] for i in range(3)]
            VXr = [RBb[:, 6144 + i * 2080:6144 + (i + 1) * 2080].rearrange("p (t e) -> p t e", e=130)
                   for i in range(3)]
            PTr = [RBb[:, 12384 + i * 1024:12384 + (i + 1) * 1024].rearrange("p (j c q) -> p j c q", c=2, q=256)
                   for i in range(3)]
            oT = RBb[:, 0:2048].rearrange("p (k n) -> p k n", n=NC_)
            mT = RBb[:, 2048:2048 + 4096].rearrange("p (k n) -> p k n", n=NC_)

            S.dma("sp", gF[:], g_ffn.partition_broadcast(128), writes=["gF"])
            S.dma("sp", gq[:], gq_t.partition_broadcast(128), writes=["gq"])
            S.dma("sp", sg[:], sg_t.partition_broadcast(128), writes=["sg"])
            S.dma("sp", invc[:].rearrange("p a b -> p (a b)"), invcnt.partition_broadcast(128), writes=["invc"])
            S.dma("sp", hv[:], hvalid[:, :], writes=["hv"])
            S.dma("sp", psc[:], psc_t[:, :], writes=["psc"])
            S.dma("sp", cw[:].rearrange("p a b -> p (a b)"), cw_t[:, :], writes=["cw"])
            S.dma("sp", cb[:], cb_t[:, :], writes=["cb"])
            S.dma("sp", wpl[:], wb_pool.rearrange("(k p) n -> p k n", p=128), reads=["wcast"], writes=["wpl"])
            for i, lv in enumerate((lq1, lk1, lq2, lk2)):
                S.dma("sp", lt[i][:], lv.partition_broadcast(128), writes=["lt%d" % i])
            V(lambda e: e.tensor_scalar(out=sg[:], in0=sg[:], scalar1=1.0 - LAM_INIT, scalar2=None, op0=ALU.mult),
              ["sg"], ["sg"])
            for i in range(2):
                V(lambda e, i=i: e.tensor_tensor(out=lt[2 * i][:], in0=lt[2 * i][:], in1=lt[2 * i + 1][:], op=ALU.mult),
                  ["lt%d" % (2 * i), "lt%d" % (2 * i + 1)], ["lt%d" % (2 * i)])
                V(lambda e, i=i: e.reduce_sum(out=lamc[:, i:i + 1], in_=lt[2 * i][:], axis=AX.X),
                  ["lt%d" % (2 * i)], ["lamc"])
            A(lambda e: e.activation(out=lamc[:, 0:2], in_=lamc[:, 0:2], func=AF.Exp), ["lamc"], ["lamc"])
            V(lambda e: e.tensor_tensor(out=lamc[:, 2:3], in0=lamc[:, 1:2], in1=lamc[:, 0:1], op=ALU.subtract),
              ["lamc"], ["lamc"])
            V(lambda e: e.tensor_scalar(out=lamc[:, 2:3], in0=lamc[:, 2:3], scalar1=-LAM_INIT, scalar2=None,
                                        op0=ALU.add), ["lamc"], ["lamc"])
            G(lambda e: e.memset(uh[:], 0.0), [], ["uh"])
            G(lambda e: e.memset(zh[:], 0.0), [], ["zh"])
            G(lambda e: e.memset(uexs[:], 0.0), [], ["uexs"])
            G(lambda e: e.memset(zext[:], 0.0), [], ["zext"])
            G(lambda e: e.memset(zextB[:], 0.0), [], ["zextB"])

            ring_i = [0]

            def wload(src_ap, kt_n, ncols):
                i = ring_i[0] % 3
                ring_i[0] += 1
                view = ring[i][:, 0:kt_n * ncols].rearrange("p (k n) -> p k n", n=ncols)
                S.dma("sp", view, src_ap, reads=["wcast"], writes=["ring%d" % i])
                return view, "ring%d" % i

            def wsrc(wb, r0, kt_n, c0, ncols):
                return wb[r0:r0 + kt_n * 128, c0:c0 + ncols].rearrange("(k p) n -> p k n", p=128)

            bank_i = {}

            def nbank(lo=0, n=4):
                c = bank_i.get((lo, n), 0)
                bank_i[(lo, n)] = c + 1
                i = lo + c % n
                return B[i], BK[i]

            def make_hT(toks, gtile, gkey, src_of):
                for ti, (nr, col0) in enumerate(toks):
                    src, skey = src_of(ti)
                    rstd_cols(src, nr, D, hb[0:nr, :], 0, [skey], "hbm")
                    A(lambda e: e.activation(out=tmpf[0:nr, :], in_=src, func=AF.Copy, scale=rs[0:nr, 0:1]),
                      [skey, "rs0"], ["tmpf"])
                    V(lambda e: e.tensor_tensor(out=hb[0:nr, :], in0=tmpf[0:nr, :], in1=gtile[0:nr, :], op=ALU.mult),
                      ["tmpf", gkey], ["hbm"])
                    for half in range(2):
                        for j in range(8):
                            kt = half * 8 + j
                            P(lambda e, kt=kt, j=j: e.transpose(pbf0[:, j, 0:nr], hb[0:nr, kt * 128:(kt + 1) * 128],
                                                                idb[0:nr, 0:nr]),
                              ["hbm", "idb"], ["pbf0"], sig=(j == 7))
                        V(lambda e, half=half: e.tensor_copy(hT[:, half * 8:(half + 1) * 8, col0:col0 + nr],
                                                             pbf0[:, :, 0:nr]), ["pbf0"], ["hT"])

            def finalize(acc0, acc1, pb0, nq, ti, hh, k0, k1):
                sl = slice(pb0, pb0 + nq)
                if KATT in ("st", "st0", "st0c0", "av", "nomask"):
                    return
                V(lambda e: e.tensor_copy(l12[sl, 0:1], acc0[:, 128:129]), [k0], ["l12"])
                V(lambda e: e.tensor_copy(l12[sl, 1:2], acc1[:, 128:129]), [k1], ["l12"])
                V(lambda e: e.tensor_scalar(out=l12[sl, 0:2], in0=l12[sl, 0:2], scalar1=1e-30, scalar2=None,
                                            op0=ALU.add), ["l12"], ["l12"])
                V(lambda e: e.reciprocal(l12[sl, 2:4], l12[sl, 0:2]), ["l12"], ["l12"])
                V(lambda e: e.tensor_tensor(out=l12[sl, 3:4], in0=l12[sl, 3:4], in1=lamc[sl, 2:3], op=ALU.mult),
                  ["l12", "lamc"], ["l12"])
                V(lambda e: e.tensor_scalar(out=t1f[sl, :], in0=acc0[:, 0:128], scalar1=l12[sl, 2:3], scalar2=None,
                                            op0=ALU.mult), [k0, "l12"], ["t1f"])
                V(lambda e: e.scalar_tensor_tensor(out=ofin[sl, :], in0=acc1[:, 0:128], scalar=l12[sl, 3:4],
                                                   in1=t1f[sl, :], op0=ALU.mult, op1=ALU.add),
                  [k1, "l12", "t1f"], ["ofin"])
                A(lambda e: e.activation(out=t1f[sl, :], in_=ofin[sl, :], func=AF.Square,
                                         accum_out=ss[sl, 1:2]), ["ofin"], ["t1f", "ss1"])
                A(lambda e: e.activation(out=rs[sl, 1:2], in_=ss[sl, 1:2], func=AF.Ln, scale=1.0 / 128, bias=EPS),
                  ["ss1"], ["rs1"])
                A(lambda e: e.activation(out=rs[sl, 1:2], in_=rs[sl, 1:2], func=AF.Exp, scale=-0.5),
                  ["rs1"], ["rs1"])
                A(lambda e: e.activation(out=t1f[sl, :], in_=ofin[sl, :], func=AF.Copy, scale=rs[sl, 1:2]),
                  ["ofin", "rs1"], ["t1f"])
                G(lambda e: e.tensor_tensor(out=ob[sl, ti, hh * 128:(hh + 1) * 128], in0=t1f[sl, :],
                                            in1=sg[sl, hh * 128:(hh + 1) * 128], op=ALU.mult),
                  ["t1f", "sg"], ["ob"])

            kv_i = [0]

            def kv_load(kt_src, vx_src, ntile, krows=128, kbase=0):
                i = kv_i[0] % 3
                kv_i[0] += 1
                if kt_src is not None:
                    S.dma("sp", KTr[i][:, 0:kt_src.shape[-1]], kt_src, reads=["KTs", "KTc"], writes=["KTr%d" % i])
                S.dma("sp", VXr[i][kbase:kbase + krows, 0:ntile, :], vx_src, reads=["VXs", "VXc"],
                      writes=["VXr%d" % i])
                return i

            pt_i = [0]

            def key_step(hh, tiles, nk, kb0, qlo, qhi, accs_per_tile, masks, stb):
                stt, stkeys = STB[stb]
                pi = pt_i[0] % 3
                pt_i[0] += 1
                PT = PTr[pi]
                ptk = "PT%d" % pi
                nt = len(tiles)
                for j, (i, tt) in enumerate(tiles):
                    for c in range(2):
                        o0 = j * 512 + c * 256
                        P(lambda e, c=c, i=i, tt=tt, o0=o0: e.matmul(
                            stt[kb0:kb0 + nk, o0 + qlo:o0 + qhi], lhsT=KTr[i][:, tt * 128:tt * 128 + nk],
                            rhs=QT[:, c, hh, qlo:qhi], start=True, stop=True),
                          ["KTr%d" % i, "QT"], [stkeys[j]], sig=(c == 1))
                if qlo == 0 and qhi == 256:
                    src = stt[kb0:kb0 + nk, 0:nt * 512]
                    dst = PT[kb0:kb0 + nk, 0:nt, :, :].rearrange("p j c q -> p (j c q)")
                else:
                    src = stt[kb0:kb0 + nk, 0:nt * 512].rearrange("p (j c q) -> p j c q", c=2, q=256)[:, :, :, qlo:qhi]
                    dst = PT[kb0:kb0 + nk, 0:nt, :, qlo:qhi]
                A(lambda e: e.activation(out=dst, in_=src, func=AF.Exp, scale=0.125),
                  [stkeys[j] for j in range(nt)], [ptk])
                for (q0,) in masks:
                    V(lambda e, q0=q0: e.memset(PT[64:128, 0, :, q0:q0 + 64], 0.0), [], [ptk])

                def av():
                    for j, (i, tt) in enumerate(tiles):
                        accs = accs_per_tile[j]
                        for ai, (c, acc, akey, qc0, qc1, fi, la) in enumerate(accs):
                            P(lambda e, c=c, acc=acc, qc0=qc0, qc1=qc1, fi=fi, la=la, i=i, tt=tt, j=j: e.matmul(
                                acc, lhsT=PT[kb0:kb0 + nk, j, c, qc0:qc1], rhs=VXr[i][kb0:kb0 + nk, tt, 0:129],
                                start=fi, stop=la),
                              [ptk, "VXr%d" % i], [akey], sig=(j == nt - 1 and ai == len(accs) - 1))
                return av

            pend_av = [None]

            def pipe_push(av):
                if pend_av[0] is not None:
                    pend_av[0]()
                pend_av[0] = av

            def pipe_flush():
                pipe_push(None)

            def attend_main(gi):
                j0 = 2 * gi
                T = OWN0 + j0 + 2
                nch = (T + 15) // 16
                TP = OWN0 + j0
                for hh in range(H):
                    loaded = {}

                    def ld(ch):
                        t0 = ch * 16
                        nt = min(16, T - t0)
                        loaded[ch] = kv_load(KTs[hh, :, t0 * 128:(t0 + nt) * 128], VXs[hh, :, t0:t0 + nt, :], nt)
                    ld(0)
                    if nch > 1:
                        ld(1)

                    def accs_for(t):
                        full = t <= OWN0 + j0
                        accs = []
                        for c in range(2):
                            for jj in range(2):
                                if jj == 0 and not full:
                                    continue
                                last_t = OWN0 + j0 + jj
                                accs.append((c, ACC[2 * c + jj][0][:, 0:129], ACC[2 * c + jj][1], jj * 128,
                                             (jj + 1) * 128, t == 0, t == last_t))
                        return accs
                    step = 0
                    t = 0
                    while t < T:
                        ch, tt = divmod(t, 16)
                        if tt == 2 and ch + 2 < nch:
                            ld(ch + 2)
                        if t < TP:
                            i = loaded[ch]
                            pipe_push(key_step(hh, [(i, tt), (i, tt + 1)], 128, 0, 0, 256,
                                               [accs_for(t), accs_for(t + 1)], [], step % 2))
                            t += 2
                        else:
                            i = loaded[ch]
                            full = t <= OWN0 + j0
                            masks = [(0,)] if t == OWN0 + j0 else [(128,)]
                            pipe_push(key_step(hh, [(i, tt)], 128, 0, 0 if full else 128, 256,
                                               [accs_for(t)], masks, step % 2))
                            t += 1
                        step += 1
                    pipe_flush()
                    for jj in range(2):
                        finalize(ACC[jj][0][:, 0:129], ACC[2 + jj][0][:, 0:129], 0, 128, jj, hh, ACC[jj][1],
                                 ACC[2 + jj][1])

            def attend_small():
                for hh in range(H):
                    for j in range(2):
                        i = kv_load(KTc[j, hh, :, :], VXc[j, hh, :, :, :], 16)
                        pb0 = 32 * j
                        a0 = ACC[0][0][pb0:pb0 + 16, 0:129]
                        a1 = ACC[1][0][pb0:pb0 + 16, 0:129]
                        for t in range(0, 16, 2):
                            accs = [[(0, a0, ACC[0][1], pb0, pb0 + 16, tq == 0, False),
                                     (1, a1, ACC[1][1], pb0, pb0 + 16, tq == 0, False)] for tq in (t, t + 1)]
                            pipe_push(key_step(hh, [(i, t), (i, t + 1)], 128, 0, pb0, pb0 + 16, accs, [],
                                               (t // 2) % 2))
                        pipe_flush()
                        i2 = kv_load(KTs[hh, :, NPT * 128 + pb0:NPT * 128 + pb0 + 16],
                                     VXs[hh, pb0:pb0 + 16, NPT:NPT + 1, :], 1, krows=16, kbase=pb0)
                        accs = [[(0, a0, ACC[0][1], pb0, pb0 + 16, False, True),
                                 (1, a1, ACC[1][1], pb0, pb0 + 16, False, True)]]
                        pipe_push(key_step(hh, [(i2, 0)], 16, pb0, pb0, pb0 + 16, accs, [], 0))
                        pipe_flush()
                        finalize(a0, a1, pb0, 16, 0, hh, ACC[0][1], ACC[1][1])
                    a0 = ACC[0][0][64:81, 0:129]
                    a1 = ACC[1][0][64:81, 0:129]
                    nch = OWN0 // 16
                    step = 0
                    for ch in range(nch):
                        i = kv_load(KTs[hh, :, ch * 2048:(ch + 1) * 2048], VXs[hh, :, ch * 16:(ch + 1) * 16, :], 16)
                        for tt in range(0, 16, 2):
                            t = ch * 16 + tt
                            accs = [[(0, a0, ACC[0][1], 64, 81, tq == 0, tq == OWN0 - 1),
                                     (1, a1, ACC[1][1], 64, 81, tq == 0, tq == OWN0 - 1)] for tq in (t, t + 1)]
                            pipe_push(key_step(hh, [(i, tt), (i, tt + 1)], 128, 0, 64, 81, accs, [], step % 2))
                            step += 1
                        pipe_flush()
                    finalize(a0, a1, 64, 17, 0, hh, ACC[0][1], ACC[1][1])

            def do_group(kind, gi):
                small = kind == "small"
                if small:
                    n = SM
                    toks = [(SM, 0)]
                    rows0 = [NPT * 128]
                    segs = [(0, 16), (32, 16), (64, 17)]
                else:
                    n = 256
                    toks = [(128, 0), (128, 128)]
                    rows0 = [(OWN0 + 2 * gi) * 128, (OWN0 + 2 * gi + 1) * 128]
                    segs = [(0, 256)]
                for ti, (nr, col0) in enumerate(toks):
                    S.dma("sp", xg[0:nr, ti, :], xs[rows0[ti]:rows0[ti] + nr, :], writes=["xg%d" % ti])
                make_hT(toks, gA, "gA", lambda ti: (xg[0:toks[ti][0], ti, :], "xg%d" % ti))
                V(lambda e: e.memset(QT[64:128, 0, :, :], 0.0), [], ["QT"])
                V(lambda e: e.memset(QT[0:64, 1, :, :], 0.0), [], ["QT"])
                wq = [wload(wbp_in[cc], 16, 512) for cc in range(2)]
                for ti, (nr, col0) in enumerate(toks):
                    for cc in range(2):
                        wv, wk = wq[cc]
                        bk, bkk = nbank()
                        for kt in range(16):
                            P(lambda e, kt=kt, wv=wv, bk=bk: e.matmul(
                                bk[0:nr, :], lhsT=hT[:, kt, col0:col0 + nr], rhs=wv[:, kt, :],
                                start=(kt == 0), stop=(kt == 15)), ["hT", wk], [bkk], sig=(kt == 15))
                        V(lambda e, bk=bk, cc=cc: e.tensor_copy(qraw[0:nr, cc * 512:(cc + 1) * 512], bk[0:nr, :]),
                          [bkk], ["qraw"])
                    headnorm(qraw[0:nr, :], nr, tmpf, qf[0:nr, :], gq, "qraw", "tmpf", "qf", "gq")
                    G(lambda e: e.tensor_copy(qb[0:nr, :], qf[0:nr, :]), ["qf"], ["qb"])
                    for hh in range(H):
                        P(lambda e, hh=hh: e.transpose(pbf1[:, hh, 0:nr], qb[0:nr, hh * 128:(hh + 1) * 128],
                                                       idb[0:nr, 0:nr]), ["qb", "idb"], ["pbf1"], sig=(hh == H - 1))
                    V(lambda e: e.tensor_copy(QT[0:64, 0, :, col0:col0 + nr], pbf1[0:64, :, 0:nr]), ["pbf1"], ["QT"])
                    V(lambda e: e.tensor_copy(QT[64:128, 1, :, col0:col0 + nr], pbf1[64:128, :, 0:nr]),
                      ["pbf1"], ["QT"])
                for pc in range(2):
                    wv, wk = wload(wbp_in[6 + pc], 16, 512)
                    for mi in range(4):
                        m = pc * 4 + mi
                        bk, bkk = nbank()
                        for kt in range(16):
                            P(lambda e, kt=kt, wv=wv, bk=bk, mi=mi: e.matmul(
                                bk[:, 0:n], lhsT=wv[:, kt, mi * 128:(mi + 1) * 128], rhs=hT[:, kt, 0:n],
                                start=(kt == 0), stop=(kt == 15)), ["hT", wk], [bkk], sig=(kt == 15))
                        if small:
                            for si, (c0, ln) in enumerate(segs):
                                V(lambda e, bk=bk, m=m, si=si, c0=c0, ln=ln: e.tensor_copy(
                                    uexs[:, m, si, 15:15 + ln], bk[:, c0:c0 + ln]), [bkk], ["uexs"])
                        else:
                            V(lambda e, bk=bk, m=m: e.tensor_copy(uext[:, m, 15:15 + n], bk[:, 0:n]), [bkk], ["uext"])
                if small:
                    for j in range(2):
                        V(lambda e: e.memset(stg[0:16, :], 0.0), [], ["tmpf"])
                        S.dma("sp", stg[0:15, :], sp_in[j], writes=["tmpf"])
                        for m in range(8):
                            P(lambda e, m=m: e.transpose(pk[:, m * 16:m * 16 + 16], stg[0:16, m * 128:(m + 1) * 128],
                                                         idf[0:16, 0:16]), ["tmpf", "idf"], ["B0", "B1"], sig=(m == 7))
                        V(lambda e, j=j: e.tensor_copy(uexs[:, :, j, 0:15],
                                                       pk[:, 0:128].rearrange("p (m t) -> p m t", t=16)[:, :, 0:15]),
                          ["B0", "B1"], ["uexs"])
                else:
                    V(lambda e: e.tensor_copy(uext[:, :, 0:15], uh[:]), ["uh"], ["uext"])
                if small:
                    views = [(uexs[:, :, si, :], 15 + ln, c0, ln) for si, (c0, ln) in enumerate(segs)]
                else:
                    views = [(uext, 15 + n, 0, n)]
                for (uv, L, c0, ln) in views:
                    for wi, wdw in enumerate((2, 4, 8, 16)):
                        src = uv[:, 2 * wi:2 * wi + 2, :]
                        sh = 1
                        cur_src = src
                        bufs = [pa, pb_]
                        bi = 0
                        for step in range(wi + 1):
                            dst = bufs[bi][:, :, 0:L]
                            V(lambda e, dst=dst, cur_src=cur_src, sh=sh: e.tensor_tensor(
                                out=dst[:, :, sh:L], in0=cur_src[:, :, sh:L], in1=cur_src[:, :, 0:L - sh], op=ALU.add),
                              ["uext", "uexs", "tmpf"], ["tmpf"])
                            cur_src = dst
                            sh *= 2
                            bi ^= 1
                        dst = bufs[bi][:, :, 0:L]
                        V(lambda e, dst=dst, cur_src=cur_src, wdw=wdw: e.tensor_scalar(
                            out=dst[:, :, 15:L], in0=cur_src[:, :, 15:L], scalar1=1.0 / wdw, scalar2=None,
                            op0=ALU.mult), ["tmpf"], ["tmpf"])
                        if (not small) and gi == 0:
                            V(lambda e, dst=dst, cur_src=cur_src, wi=wi: e.tensor_tensor(
                                out=dst[:, :, 15:31], in0=cur_src[:, :, 15:31],
                                in1=invc[:, wi, :].unsqueeze(1).to_broadcast([128, 2, 16]), op=ALU.mult),
                              ["tmpf", "invc"], ["tmpf"])
                        V(lambda e, dst=dst, src=src, wi=wi, c0=c0, ln=ln, L=L: e.tensor_tensor(
                            out=pT[:, 2 * wi:2 * wi + 2, c0:c0 + ln], in0=dst[:, :, 15:L], in1=src[:, :, 15:L],
                            op=ALU.subtract), ["tmpf", "uext", "uexs"], ["qf"])
                if small:
                    V(lambda e: e.tensor_copy(uh[:], uexs[:, :, 2, 17:32]), ["uexs"], ["uh"])
                else:
                    V(lambda e: e.tensor_copy(uh[:], uext[:, :, 256:271]), ["uext"], ["uh"])

                def emit_pool_state(src_fn, dst_ap):
                    for m in range(8):
                        P(lambda e, m=m: e.transpose(pk[0:15, m * 128:(m + 1) * 128], src_fn(m), idf[:]),
                          ["uexs", "uh", "idf"], ["B0", "B1"], sig=(m == 7))
                    V(lambda e: e.tensor_copy(stg[0:15, :], pk[0:15, :]), ["B0", "B1"], ["tmpf"])
                    S.dma("sp", dst_ap, stg[0:15, :], reads=["tmpf"], writes=["pool_o"])
                if small:
                    for j in range(2):
                        emit_pool_state(lambda m, j=j: uexs[:, m, j, 16:31], pool_s[j])
                elif gi == NG - 1:
                    emit_pool_state(lambda m: uh[:, m, :], pool_p[:, :])
                pre_ga0 = wload(wbp_in[8], 16, 512)
                pre_gb0 = wload(wbp_in[12], 16, 512)
                if KSUB < 3.1:
                    return
                if small:
                    V(lambda e: e.memset(ob[:, 0, :], 0.0), [], ["ob"])
                    attend_small()
                else:
                    attend_main(gi)
                if KSUB < 3.2:
                    return
                S.barrier()
                for ti, (nr, col0) in enumerate(toks):
                    for hh in range(H):
                        P(lambda e, hh=hh: e.transpose(pbf1[:, hh, 0:nr], ob[0:nr, ti, hh * 128:(hh + 1) * 128],
                                                       idb[0:nr, 0:nr]), ["ob", "idb"], ["pbf1"], sig=(hh == H - 1))
                    V(lambda e: e.tensor_copy(oT[:, :, col0:col0 + nr], pbf1[:, :, 0:nr]), ["pbf1"], ["oT"])
                if KSUB < 3.3:
                    return
                for wi in range(4):
                    for mo in range(2):
                        bk, bkk = nbank()
                        for ki in range(2):
                            P(lambda e, ki=ki, bk=bk, wi=wi, mo=mo: e.matmul(
                                bk[:, 0:n], lhsT=wpl[:, 2 * wi + ki, mo * 128:(mo + 1) * 128],
                                rhs=pT[:, 2 * wi + ki, 0:n], start=(ki == 0), stop=(ki == 1)),
                              ["qf", "wpl"], [bkk], sig=(ki == 1))
                        A(lambda e, bk=bk, wi=wi, mo=mo: e.activation(
                            out=pbT[:, 2 * wi + mo, 0:n], in_=bk[:, 0:n], func=AF.Copy,
                            scale=psc[:, 2 * wi + mo:2 * wi + mo + 1]), [bkk, "psc"], ["qraw"])
                if KSUB < 3.33:
                    return
                for pc in range(4):
                    wga = pre_ga0 if pc == 0 else wload(wbp_in[8 + pc], 16, 512)
                    wgb = pre_gb0 if pc == 0 else wload(wbp_in[12 + pc], 16, 512)
                    i = ring_i[0] % 3
                    ring_i[0] += 1
                    wua_v = ring[i][:, 0:4096].rearrange("p (k n) -> p k n", n=512)
                    wup_v = ring[i][:, 4096:8192].rearrange("p (k n) -> p k n", n=512)
                    S.dma("sp", wua_v, wbp_ua[pc], reads=["wcast"], writes=["ring%d" % i])
                    S.dma("sp", wup_v, wbp_up[pc], reads=["wcast"], writes=["ring%d" % i])
                    wuk = "ring%d" % i
                    for mi in range(4):
                        m = pc * 4 + mi
                        ms = slice(mi * 128, (mi + 1) * 128)
                        specs = [(wga[0], wga[1], hT, "hT", 16), (wua_v, wuk, oT, "oT", 8),
                                 (wgb[0], wgb[1], hT, "hT", 16), (wup_v, wuk, pbT, "qraw", 8)]
                        bs = BSET[m % 2]
                        gz, gzk = (zc, "zc") if m % 2 == 0 else (zcB, "zcB")
                        gs, gsk = (szl, "szl") if m % 2 == 0 else (szlB, "szlB")
                        for bi_, (wv, wk, act, ak, nk) in enumerate(specs):
                            for kt in range(nk):
                                P(lambda e, kt=kt, wv=wv, act=act, bi_=bi_, nk=nk: e.matmul(
                                    bs[bi_][0][:, 0:n], lhsT=wv[:, kt, ms], rhs=act[:, kt, 0:n],
                                    start=(kt == 0), stop=(kt == nk - 1)), [wk, ak], [bs[bi_][1]], sig=(kt == nk - 1))
                        A(lambda e: e.activation(out=gz[:, 0:n], in_=bs[0][0][:, 0:n], func=AF.Sigmoid), [bs[0][1]], [gzk])
                        A(lambda e: e.activation(out=gs[:, 0:n], in_=bs[2][0][:, 0:n], func=AF.Sigmoid), [bs[2][1]], [gsk])
                        V(lambda e: e.tensor_tensor(out=gz[:, 0:n], in0=gz[:, 0:n], in1=bs[1][0][:, 0:n], op=ALU.mult),
                          [gzk, bs[1][1]], [gzk])
                        V(lambda e: e.tensor_tensor(out=gs[:, 0:n], in0=gs[:, 0:n], in1=bs[3][0][:, 0:n], op=ALU.mult),
                          [gsk, bs[3][1]], [gsk])
                        G(lambda e, m=m: e.tensor_tensor(out=mT[:, m, 0:n], in0=gz[:, 0:n], in1=gs[:, 0:n],
                                                         op=ALU.add), [gzk, gsk], ["mT"])
                if KSUB < 3.4:
                    return
                for cc in range(4):
                    wv, wk = wload(wbp_out[cc], 16, 512)
                    for ti, (nr, col0) in enumerate(toks):
                        bk, bkk = nbank()
                        for kt in range(16):
                            P(lambda e, kt=kt, wv=wv, bk=bk: e.matmul(
                                bk[0:nr, :], lhsT=mT[:, kt, col0:col0 + nr], rhs=wv[:, kt, :],
                                start=(kt == 0), stop=(kt == 15)), ["mT", wk], [bkk], sig=(kt == 15))
                        V(lambda e, bk=bk, cc=cc, ti=ti: e.tensor_tensor(
                            out=xg[0:nr, ti, cc * 512:(cc + 1) * 512], in0=xg[0:nr, ti, cc * 512:(cc + 1) * 512],
                            in1=bk[0:nr, :], op=ALU.add), ["xg%d" % ti, bkk], ["xg%d" % ti])
                if KSUB < 3.5:
                    return
                S.barrier()
                make_hT(toks, gF, "gF", lambda ti: (xg[0:toks[ti][0], ti, :], "xg%d" % ti))
                if small:
                    for j in range(2):
                        for r in range(6):
                            m0 = r * 8
                            mn = min(8, KF - m0)
                            S.dma("sp", stg[0:2, 0:mn * 128], sc_in[j, :, m0 * 128:(m0 + mn) * 128], writes=["tmpf"])
                            for mm in range(mn):
                                P(lambda e, mm=mm: e.transpose(pk[:, mm * 2:mm * 2 + 2],
                                                               stg[0:2, mm * 128:(mm + 1) * 128], idf[0:2, 0:2]),
                                  ["tmpf", "idf"], ["B0", "B1"], sig=(mm == mn - 1))
                            V(lambda e, j=j, m0=m0, mn=mn: e.tensor_copy(
                                sch[:, m0:m0 + mn, j, :], pk[:, 0:mn * 2].rearrange("p (m t) -> p m t", t=2)),
                              ["B0", "B1"], ["sch"])
                for m in range(KF):
                    if m % 4 == 0:
                        mn = min(4, KF - m)
                        wz = wload(wbp_fz[m // 4][:, :, 0:mn * 128], 16, mn * 128)
                        wvv = wload(wbp_fv[m // 4][:, :, 0:mn * 128], 16, mn * 128)
                    mi = m % 4
                    ms = slice(mi * 128, (mi + 1) * 128)
                    zx, zxk = ((zext, "zext"), (zextB, "zextB"))[m % 2]
                    zcc, zck = ((zc, "zc"), (zcB, "zcB"))[m % 2]
                    szz, szk = ((szl, "szl"), (szlB, "szlB"))[m % 2]
                    bz, bzk = ZR[m % 4]
                    bv, bvk = VR[m % 4]
                    for kt in range(16):
                        P(lambda e, kt=kt, bz=bz: e.matmul(bz[:, 0:n], lhsT=wz[0][:, kt, ms], rhs=hT[:, kt, 0:n],
                                                           start=(kt == 0), stop=(kt == 15)),
                          ["hT", wz[1]], [bzk], sig=(kt == 15))
                    for kt in range(16):
                        P(lambda e, kt=kt, bv=bv: e.matmul(bv[:, 0:n], lhsT=wvv[0][:, kt, ms], rhs=hT[:, kt, 0:n],
                                                           start=(kt == 0), stop=(kt == 15)),
                          ["hT", wvv[1]], [bvk], sig=(kt == 15))
                    for si, (c0, ln) in enumerate(segs):
                        if small:
                            if si < 2:
                                V(lambda e, si=si, m=m: e.tensor_copy(zx[:, si, 0:2], sch[:, m, si, :]),
                                  ["sch"], [zxk])
                            else:
                                V(lambda e, si=si: e.memset(zx[:, si, 0:2], 0.0), [], [zxk])
                        else:
                            G(lambda e, m=m: e.tensor_copy(zx[:, 0, 0:2], zh[:, m, :]), ["zh"], [zxk])
                        A(lambda e, si=si, c0=c0, ln=ln, bz=bz: e.copy(out=zx[:, si, 2:2 + ln], in_=bz[:, c0:c0 + ln]),
                          [bzk], [zxk])
                        V(lambda e, si=si, c0=c0, ln=ln, m=m: e.tensor_scalar(
                            out=zcc[:, c0:c0 + ln], in0=zx[:, si, 2:2 + ln], scalar1=cw[:, m, 2:3],
                            scalar2=cb[:, m:m + 1], op0=ALU.mult, op1=ALU.add), [zxk, "cw", "cb"], [zck])
                        for tap in (1, 0):
                            V(lambda e, si=si, c0=c0, ln=ln, m=m, tap=tap: e.scalar_tensor_tensor(
                                out=zcc[:, c0:c0 + ln], in0=zx[:, si, tap:tap + ln], scalar=cw[:, m, tap:tap + 1],
                                in1=zcc[:, c0:c0 + ln], op0=ALU.mult, op1=ALU.add), [zxk, "cw", zck], [zck])
                        A(lambda e, c0=c0, ln=ln: e.activation(out=szz[:, c0:c0 + ln], in_=zcc[:, c0:c0 + ln],
                                                               func=AF.Silu), [zck], [szk])
                        V(lambda e, c0=c0, ln=ln, m=m, bv=bv: e.tensor_tensor(
                            out=aT[:, m, c0:c0 + ln], in0=szz[:, c0:c0 + ln], in1=bv[:, c0:c0 + ln], op=ALU.mult),
                          [szk, bvk], ["aT"])
                        if small:
                            if si < 2:
                                V(lambda e, si=si, m=m: e.tensor_copy(zsave[:, m, si, :], zx[:, si, 16:18]),
                                  [zxk], ["zsave"])
                            else:
                                V(lambda e, m=m: e.tensor_scalar(out=zh[:, m, :], in0=zx[:, 2, 17:19],
                                                                 scalar1=hv[:, 0:1], scalar2=None, op0=ALU.mult),
                                  [zxk, "hv"], ["zh"])
                        else:
                            G(lambda e, m=m: e.tensor_copy(zh[:, m, :], zx[:, 0, 256:258]), [zxk], ["zh"])
                    if small and n > 81:
                        pass
                if small:
                    for (g0, g1) in ((16, 32), (48, 64), (81, SM)):
                        V(lambda e, g0=g0, g1=g1: e.memset(aT[:, :, g0:g1], 0.0), ["aT"], ["aT"])
                if KSUB < 3.6:
                    return
                deferred = []
                for cc in range(4):
                    accb = []
                    for ti in range(len(toks)):
                        accb.append(nbank())
                    for pcs, (k0, kn) in enumerate(((0, 16), (16, 16), (32, 11))):
                        wv, wk = wload(wbp_fo[cc][:, k0:k0 + kn, :], kn, 512)
                        for ti, (nr, col0) in enumerate(toks):
                            bk, bkk = accb[ti]
                            if pcs == 2 and ti == 0:
                                for dfn in deferred:
                                    dfn()
                                deferred = []
                            for kt in range(kn):
                                P(lambda e, kt=kt, wv=wv, bk=bk, k0=k0: e.matmul(
                                    bk[0:nr, :], lhsT=aT[:, k0 + kt, col0:col0 + nr], rhs=wv[:, kt, :],
                                    start=(k0 + kt == 0), stop=(k0 + kt == KF - 1)),
                                  ["aT", wk], [bkk], sig=(kt == kn - 1))
                    for ti, (nr, col0) in enumerate(toks):
                        bk, bkk = accb[ti]
                        yb = (cc * 2 + ti) % 2
                        V(lambda e, bk=bk, yb=yb, ti=ti, cc=cc: e.tensor_tensor(
                            out=yst[yb][0:nr, :], in0=xg[0:nr, ti, cc * 512:(cc + 1) * 512], in1=bk[0:nr, :],
                            op=ALU.add), ["xg%d" % ti, bkk], ["yst%d" % yb])
                        if small:
                            for j in range(2):
                                deferred.append(lambda j=j, cc=cc, yb=yb: S.dma(
                                    "sp", y_s[j, :, cc * 512:(cc + 1) * 512], yst[yb][32 * j:32 * j + 16, :],
                                    reads=["yst%d" % yb], writes=["y_s"]))
                        else:
                            r0 = (2 * gi + ti) * 128
                            deferred.append(lambda r0=r0, cc=cc, yb=yb: S.dma(
                                "sp", y_p[r0:r0 + 128, cc * 512:(cc + 1) * 512], yst[yb][:, :],
                                reads=["yst%d" % yb], writes=["y_p"]))
                for dfn in deferred:
                    dfn()
                S.barrier()

            def emit_conv_state(src_fn, dst_ap):
                for r in range(6):
                    m0 = r * 8
                    mn = min(8, KF - m0)
                    for mm in range(mn):
                        P(lambda e, mm=mm: e.transpose(pk[0:2, mm * 128:(mm + 1) * 128], src_fn(m0 + mm), idf[:]),
                          ["zsave", "zh", "idf"], ["B0", "B1"], sig=(mm == mn - 1))
                    V(lambda e, mn=mn: e.tensor_copy(stg[0:2, 0:mn * 128], pk[0:2, 0:mn * 128]), ["B0", "B1"], ["tmpf"])
                    S.dma("sp", dst_ap[:, m0 * 128:(m0 + mn) * 128], stg[0:2, 0:mn * 128], reads=["tmpf"],
                          writes=["conv_o"])

            if KSTOP >= 3 and KSMALL:
                do_group("small", -1)
                for j in range(2):
                    emit_conv_state(lambda m, j=j: zsave[:, m, j, :], conv_s[j])
            for gi in range(KGROUPS if KSTOP >= 4 else 0):
                do_group("main", gi)
            if KSTOP >= 4:
                emit_conv_state(lambda m: zh[:, m, :], conv_p[:, :])
            S.barrier()
    S.sync_all(["sp"])
    print("sbuf remaining", nc.sbuf_bytes_remaining)
    print("program: ins", S.nins, "waits", S.nwait, "cnt", S.cnt, flush=True)
    return nc


_NC_CACHE = {}


def kernel(_prep_only=False, **inp):
    f = lambda a: np.ascontiguousarray(np.asarray(a, dtype=np.float32))
    x_prompt = f(inp["x_prompt"])[0]
    x_sample = f(inp["x_sample"])
    cache_k = f(inp["cache_k"])[0].reshape(16, 2048, 1024)
    cache_v = f(inp["cache_v"])[0].reshape(16, 2048, 1024)
    state_pool = f(inp["state_pool"])[0]
    state_conv = f(inp["state_conv"])[0]
    shared = {
        "ident": np.eye(128, dtype=np.float32),
        "g_attn": f(inp["attn_norm_g"]).reshape(1, D),
        "g_ffn": f(inp["ffn_norm_g"]).reshape(1, D),
        "gq_t": np.tile(f(inp["q_norm_g"]).reshape(1, 128), (1, 8)),
        "gk_t": np.tile(f(inp["k_norm_g"]).reshape(1, 128), (1, 8)),
        "sg_t": np.tile(f(inp["subln_g"]).reshape(1, 128), (1, 8)),
        "lq1": f(inp["lambda_q1"]).reshape(1, 64), "lk1": f(inp["lambda_k1"]).reshape(1, 64),
        "lq2": f(inp["lambda_q2"]).reshape(1, 64), "lk2": f(inp["lambda_k2"]).reshape(1, 64),
        "psc_t": np.ascontiguousarray(f(inp["pool_scale"]).reshape(8, 128).T),
        "cw_t": np.ascontiguousarray(f(inp["conv_w"])[0].reshape(3, KF, 128).transpose(2, 1, 0).reshape(128, KF * 3)),
        "cb_t": np.ascontiguousarray(f(inp["conv_b"]).reshape(KF, 128).T),
        "w_in": f(inp["w_in"])[0], "w_pool": f(inp["w_pool"])[0].reshape(1024, 256),
        "w_ua": f(inp["w_up_attn"])[0], "w_up": f(inp["w_up_pool"])[0], "w_out": f(inp["w_out"])[0],
        "w_fi": f(inp["w_ffn_in"])[0], "w_fo": f(inp["w_ffn_out"])[0],
    }
    in_maps = []
    for c in range(NCORE):
        xs = np.zeros((NSL * 128, D), np.float32)
        ntrue = (16 * c + 16) * 128
        xs[NPT * 128 - ntrue:NPT * 128] = x_prompt[0:ntrue]
        sm = xs[NPT * 128:]
        sm[0:16] = x_sample[2 * c]
        sm[32:48] = x_sample[2 * c + 1]
        if c > 0:
            sm[64:81] = x_prompt[2048 * c - 17:2048 * c]
        kvalid = np.zeros((128, NSL), np.float32)
        kvalid[:, NPT - 16 * (c + 1):] = 1.0
        invcnt = np.zeros((4, 16), np.float32)
        for wi, w in enumerate((2, 4, 8, 16)):
            for t in range(16):
                invcnt[wi, t] = (1.0 / min(w, t + 1)) if c == 0 else 1.0 / w
        m = dict(shared)
        m.update({
            "xs": xs, "kvalid": kvalid,
            "ck": np.ascontiguousarray(cache_k[2 * c:2 * c + 2]), "cv": np.ascontiguousarray(cache_v[2 * c:2 * c + 2]),
            "sp_in": np.ascontiguousarray(state_pool[2 * c:2 * c + 2]),
            "sc_in": np.ascontiguousarray(state_conv[2 * c:2 * c + 2]),
            "invcnt": invcnt.reshape(1, 64),
            "hvalid": np.full((128, 1), 0.0 if c == 0 else 1.0, np.float32),
        })
        in_maps.append(m)
    if _prep_only:
        return in_maps
    if "nc" not in _NC_CACHE:
        _NC_CACHE["nc"] = build()
    res = run_bass_kernel_spmd(_NC_CACHE["nc"], in_maps, core_ids=list(range(NCORE)))
    R = res.results
    cat = lambda k: np.concatenate([np.asarray(R[c][k]) for c in range(NCORE)], axis=0)
    y_p = cat("y_p").reshape(1, 16384, D)
    y_s = cat("y_s").reshape(16, 16, D)
    k_p = cat("k_p").reshape(1, 1, 16384, 8, 128)
    v_p = cat("v_p").reshape(1, 1, 16384, 8, 128)
    pool_p = np.asarray(R[NCORE - 1]["pool_p"]).reshape(1, 1, 15, 1024)
    conv_p = np.asarray(R[NCORE - 1]["conv_p"]).reshape(1, 1, 2, DFF)
    k_s = cat("k_s").reshape(1, 16, 16, 8, 128)
    v_s = cat("v_s").reshape(1, 16, 16, 8, 128)
    pool_s = cat("pool_s").reshape(1, 16, 15, 1024)
    conv_s = cat("conv_s").reshape(1, 16, 2, DFF)
    return tuple(np.ascontiguousarray(a, dtype=np.float32) for a in
                 (y_p, y_s, k_p, v_p, pool_p, conv_p, k_s, v_s, pool_s, conv_s))
```

```python
import os
import numpy as np
from contextlib import ExitStack
import concourse.bass as bass
import concourse.mybir as mybir
from concourse.bass_utils import run_bass_kernel_spmd

F32 = mybir.dt.float32
BF16 = mybir.dt.bfloat16
AF = mybir.ActivationFunctionType
ALU = mybir.AluOpType
AX = mybir.AxisListType

D = 2048
H = 8
DFF = 5504
KF = 43
NPT = 128
NSL = 129
OWN0 = 112
NG = 8
EPS = 1e-6
LAM_INIT = 0.2
NCORE = 8
SM = 96
KSTOP = float(os.environ.get("KSTOP", "99"))
KTILES = int(os.environ.get("KTILES", "%d" % 129))
KSUB = float(os.environ.get("KSUB", "99"))
KSMALL = int(os.environ.get("KSMALL", "1"))
KGROUPS = int(os.environ.get("KGROUPS", "8"))
KATT = os.environ.get("KATT", "")


class _Tok:
    __slots__ = ("sem", "val", "eng")

    def __init__(self, sem, val, eng):
        self.sem = sem
        self.val = val
        self.eng = eng


class Sched:
    NDS = 40

    def __init__(self, nc):
        self.nc = nc
        self.eng = {"pe": nc.tensor, "act": nc.scalar, "dve": nc.vector,
                    "pool": nc.gpsimd, "sp": nc.sync}
        self.sem = {e: nc.alloc_semaphore("s_" + e) for e in ("pe", "act", "dve", "pool")}
        self.cnt = {e: 0 for e in self.sem}
        self.cur = {e: _Tok(self.sem[e], None, e) for e in self.sem}
        self.dsem = [nc.alloc_semaphore("d%d" % i) for i in range(self.NDS)]
        self.dcnt = [0] * self.NDS
        self.dnext = 0
        self.waited = {e: {} for e in self.eng}
        self.lastw = {}
        self.readers = {}
        self.nwait = 0
        self.nins = 0

    def _wait(self, e, tok):
        if tok is None:
            return
        if tok.val is None:
            if tok.eng == e:
                return
            raise RuntimeError("wait on unresolved token of %s from %s" % (tok.eng, e))
        w = self.waited[e]
        sid = id(tok.sem)
        if w.get(sid, 0) >= tok.val:
            return
        self.eng[e].wait_ge(tok.sem, tok.val)
        self.nwait += 1
        w[sid] = tok.val

    def _deps(self, e, reads, writes):
        toks = []
        for k in reads:
            toks.append(self.lastw.get(k))
        for k in writes:
            toks.append(self.lastw.get(k))
            toks.extend(self.readers.get(k, {}).values())
        best = {}
        for t in toks:
            if t is None:
                continue
            if t.eng == "pe" and e == "pe":
                continue
            if t.eng == e and t.val is not None and t.val <= self.cnt[e] - 3:
                continue
            if t.val is None:
                self._wait(e, t)
                continue
            sid = id(t.sem)
            if sid not in best or best[sid].val < t.val:
                best[sid] = t
        for t in best.values():
            self._wait(e, t)

    def _record(self, tok, reads, writes):
        for k in reads:
            self.readers.setdefault(k, {})[id(tok.sem)] = tok
        for k in writes:
            self.lastw[k] = tok
            self.readers[k] = {}

    def op(self, e, fn, reads=(), writes=(), signal=True):
        self._deps(e, reads, writes)
        ins = fn(self.eng[e])
        self.nins += 1
        tok = self.cur[e]
        self._record(tok, reads, writes)
        if signal:
            ins.then_inc(self.sem[e], 1)
            self.cnt[e] += 1
            tok.val = self.cnt[e]
            self.cur[e] = _Tok(self.sem[e], None, e)
        return ins

    def dma(self, q, out, in_, reads=(), writes=(), **kw):
        self._deps(q, reads, writes)
        i = self.dnext
        self.dnext = (i + 1) % self.NDS
        if self.dcnt[i]:
            self._wait(q, _Tok(self.dsem[i], self.dcnt[i], "dma"))
        ins = self.eng[q].dma_start(out=out, in_=in_, **kw)
        self.dcnt[i] += 16
        ins.then_inc(self.dsem[i], 16)
        self.nins += 1
        self._record(_Tok(self.dsem[i], self.dcnt[i], "dma"), reads, writes)
        return ins

    def sync_all(self, engines):
        for e in engines:
            for i in range(self.NDS):
                if self.dcnt[i]:
                    self._wait(e, _Tok(self.dsem[i], self.dcnt[i], "dma"))
            for x in self.sem:
                if x != e and self.cnt[x]:
                    assert self.cur[x].val is None
                    self._wait(e, _Tok(self.sem[x], self.cnt[x], x))

    def barrier(self):
        self.sync_all(["pe", "act", "dve", "pool", "sp"])
        self.lastw = {}
        self.readers = {}


def build():
    nc = bass.Bass("TRN2", target_bir_lowering=False)
    S = Sched(nc)

    def din(name, shape):
        return nc.dram_tensor(name, list(shape), F32, kind="ExternalInput").ap()

    def dout(name, shape):
        return nc.dram_tensor(name, list(shape), F32, kind="ExternalOutput").ap()

    def dscr(name, shape, dt=BF16):
        return nc.dram_tensor(name, list(shape), dt).ap()

    xs = din("xs", [NSL * 128, D])
    kvalid = din("kvalid", [128, NSL])
    ck = din("ck", [2, 2048, 1024])
    cv = din("cv", [2, 2048, 1024])
    sp_in = din("sp_in", [2, 15, 1024])
    sc_in = din("sc_in", [2, 2, DFF])
    invcnt = din("invcnt", [1, 64])
    hvalid = din("hvalid", [128, 1])
    ident = din("ident", [128, 128])
    g_attn = din("g_attn", [1, D])
    g_ffn = din("g_ffn", [1, D])
    gq_t = din("gq_t", [1, 1024])
    gk_t = din("gk_t", [1, 1024])
    sg_t = din("sg_t", [1, 1024])
    lq1 = din("lq1", [1, 64])
    lk1 = din("lk1", [1, 64])
    lq2 = din("lq2", [1, 64])
    lk2 = din("lk2", [1, 64])
    psc_t = din("psc_t", [128, 8])
    cw_t = din("cw_t", [128, KF * 3])
    cb_t = din("cb_t", [128, KF])
    w_in = din("w_in", [D, 8192])
    w_pool = din("w_pool", [1024, 256])
    w_ua = din("w_ua", [1024, D])
    w_up = din("w_up", [1024, D])
    w_out = din("w_out", [D, D])
    w_fi = din("w_fi", [D, 2 * DFF])
    w_fo = din("w_fo", [DFF, D])

    y_p = dout("y_p", [2048, D])
    y_s = dout("y_s", [2, 16, D])
    k_p = dout("k_p", [2048, 1024])
    v_p = dout("v_p", [2048, 1024])
    pool_p = dout("pool_p", [15, 1024])
    conv_p = dout("conv_p", [2, DFF])
    k_s = dout("k_s", [2, 16, 1024])
    v_s = dout("v_s", [2, 16, 1024])
    pool_s = dout("pool_s", [2, 15, 1024])
    conv_s = dout("conv_s", [2, 2, DFF])

    wbp_in = dscr("wbp_in", [16, 128, 16, 512])
    wb_pool = dscr("wb_pool", [1024, 256])
    wbp_ua = dscr("wbp_ua", [4, 128, 8, 512])
    wbp_up = dscr("wbp_up", [4, 128, 8, 512])
    wbp_out = dscr("wbp_out", [4, 128, 16, 512])
    wbp_fz = dscr("wbp_fz", [11, 128, 16, 512])
    wbp_fv = dscr("wbp_fv", [11, 128, 16, 512])
    wbp_fo = dscr("wbp_fo", [4, 128, KF, 512])
    KTs = dscr("KTs", [H, 128, NSL * 128])
    VXs = dscr("VXs", [H, 128, NSL, 130])
    KTc = dscr("KTc", [2, H, 128, 2048])
    VXc = dscr("VXc", [2, H, 128, 16, 130])

    pk = nc.alloc_psum_tensor("pk", [128, 1024], F32)
    pv = nc.alloc_psum_tensor("pv", [128, 1024], F32)
    pq = nc.alloc_psum_tensor("pq", [128, 1024], F32)
    pbf0 = nc.alloc_psum_tensor("pbf0", [128, 8, 128], BF16)
    pbf1 = nc.alloc_psum_tensor("pbf1", [128, 8, 128], BF16)
    B = [pk[:, 0:512], pk[:, 512:1024], pv[:, 0:512], pv[:, 512:1024], pq[:, 0:512], pq[:, 512:1024]]
    BK = ["B0", "B1", "B2", "B3", "B4", "B5"]
    pbf0f = pbf0[:].rearrange("p a b -> p (a b)").bitcast(F32)
    pbf1f = pbf1[:].rearrange("p a b -> p (a b)").bitcast(F32)
    STB = [(pk, ("B0", "B1")), (pv, ("B2", "B3")), (pq, ("B4", "B5"))]
    ACC = [(pbf0f[:, 0:256], "pbf0"), (pbf0f[:, 256:512], "pbf0"), (pbf1f[:, 0:256], "pbf1"), (pbf1f[:, 256:512], "pbf1")]
    BSET = [[(B[0], "B0"), (B[1], "B1"), (B[2], "B2"), (B[3], "B3")],
            [(B[4], "B4"), (B[5], "B5"), (pbf0f, "pbf0"), (pbf1f, "pbf1")]]
    ZR = [(B[0], "B0"), (B[1], "B1"), (B[4], "B4"), (B[5], "B5")]
    VR = [(B[2], "B2"), (B[3], "B3"), (pbf0f, "pbf0"), (pbf1f, "pbf1")]

    def A(fn, r, w):
        return S.op("act", fn, r, w)

    def V(fn, r, w):
        return S.op("dve", fn, r, w)

    def G(fn, r, w):
        return S.op("pool", fn, r, w)

    def P(fn, r, w, sig=True):
        return S.op("pe", fn, r, w, signal=sig)

    casts = []

    def add_cast(dst, src, rows, c_lo, c_hi):
        for r0 in range(0, rows, 128):
            r1 = min(rows, r0 + 128)
            for c0 in range(c_lo, c_hi, 2048):
                c1 = min(c_hi, c0 + 2048)
                casts.append((dst[r0:r1, c0:c1], src[r0:r1, c0:c1]))

    def add_cast_p(dst4, src2, rows, c_lo, c_hi, piece0):
        for kt in range(rows // 128):
            r0 = kt * 128
            c0 = c_lo
            while c0 < c_hi:
                c1 = min(c_hi, c0 + 2048)
                nfull = (c1 - c0) // 512
                pj = piece0 + (c0 - c_lo) // 512
                if nfull:
                    casts.append((dst4[pj:pj + nfull, :, kt, :].rearrange("j p c -> p j c"),
                                  src2[r0:r0 + 128, c0:c0 + nfull * 512].rearrange("p (j c) -> p j c", c=512)))
                rem = (c1 - c0) - nfull * 512
                if rem:
                    casts.append((dst4[pj + nfull, :, kt, 0:rem], src2[r0:r0 + 128, c0 + nfull * 512:c1]))
                c0 = c1

    add_cast_p(wbp_in, w_in, D, 0, 1024, 0)
    add_cast_p(wbp_in, w_in, D, 3072, 8192, 6)
    add_cast(wb_pool, w_pool, 1024, 0, 256)
    add_cast_p(wbp_ua, w_ua, 1024, 0, D, 0)
    add_cast_p(wbp_up, w_up, 1024, 0, D, 0)
    add_cast_p(wbp_out, w_out, D, 0, D, 0)
    add_cast_p(wbp_fz, w_fi, D, 0, DFF, 0)
    add_cast_p(wbp_fv, w_fi, D, DFF, 2 * DFF, 0)
    add_cast_p(wbp_fo, w_fo, DFF, 0, D, 0)
    cast_pos = [0]

    def emit_casts(n):
        for _ in range(n):
            if cast_pos[0] < len(casts):
                d, s = casts[cast_pos[0]]
                cast_pos[0] += 1
                S.dma("pool", d, s, writes=["wcast"])

    with ExitStack() as top:
        def sb(name, shape, dt, st=top):
            return st.enter_context(nc.sbuf_tensor(name, list(shape), dt))

        idf = sb("idf", [128, 128], F32)
        idb = sb("idb", [128, 128], BF16)
        gA = sb("gA", [128, D], F32)
        kval = sb("kval", [128, NSL], F32)
        ss = sb("ss", [128, 4], F32)
        rs = sb("rs", [128, 4], F32)
        ssk = sb("ssk", [128, 16], F32)
        rk = sb("rk", [128, 16], F32)
        lamc = sb("lamc", [128, 8], F32)
        S.dma("sp", idf[:], ident[:, :], writes=["idf"])
        V(lambda e: e.tensor_copy(idb[:], idf[:]), ["idf"], ["idb"])
        S.dma("sp", gA[:], g_attn.partition_broadcast(128), writes=["gA"])
        S.dma("sp", kval[:], kvalid[:, :], writes=["kval"])

        def rstd_cols(src_ap, n, width, junk_ap, col, rkeys, jkey):
            A(lambda e: e.activation(out=junk_ap, in_=src_ap, func=AF.Square, accum_out=ss[0:n, col:col + 1]),
              rkeys, [jkey, "ss%d" % col])
            A(lambda e: e.activation(out=rs[0:n, col:col + 1], in_=ss[0:n, col:col + 1], func=AF.Ln,
                                     scale=1.0 / width, bias=EPS), ["ss%d" % col], ["rs%d" % col])
            A(lambda e: e.activation(out=rs[0:n, col:col + 1], in_=rs[0:n, col:col + 1], func=AF.Exp,
                                     scale=-0.5), ["rs%d" % col], ["rs%d" % col])

        def headnorm(raw, n, sqbuf, outf, gtile, rkey, sqkey, okey, gkey):
            A(lambda e: e.activation(out=sqbuf[0:n, 0:1024], in_=raw, func=AF.Square), [rkey], [sqkey])
            V(lambda e: e.reduce_sum(out=ssk[0:n, :], in_=sqbuf[0:n, 0:1024].rearrange("p (g d) -> p g d", d=64),
                                     axis=AX.X), [sqkey], ["ssk"])
            A(lambda e: e.activation(out=rk[0:n, :], in_=ssk[0:n, :], func=AF.Ln, scale=1.0 / 64, bias=EPS),
              ["ssk"], ["rk"])
            A(lambda e: e.activation(out=rk[0:n, :], in_=rk[0:n, :], func=AF.Exp, scale=-0.5), ["rk"], ["rk"])
            V(lambda e: e.tensor_tensor(out=outf.rearrange("p (g d) -> p g d", d=64),
                                        in0=raw.rearrange("p (g d) -> p g d", d=64),
                                        in1=rk[0:n, :].unsqueeze(2).to_broadcast([n, 16, 64]), op=ALU.mult),
              [rkey, "rk"], [okey])
            V(lambda e: e.tensor_tensor(out=outf, in0=outf, in1=gtile[0:n, :], op=ALU.mult), [okey, gkey], [okey])

        with ExitStack() as p1:
            def sb1(name, shape, dt):
                return sb(name, shape, dt, p1)

            wkv = sb1("wkv", [128, 16, 2048], BF16)
            gk = sb1("gk", [128, 1024], F32)
            xt = [sb1("xt%d" % i, [128, D], F32) for i in range(3)]
            tmp = sb1("tmp", [128, D], F32)
            hb = [sb1("hb%d" % i, [128, D], BF16) for i in range(2)]
            hT = [sb1("hT%d" % i, [128, 16, 128], BF16) for i in range(2)]
            kraw = sb1("kraw", [128, 1024], F32)
            kf = [sb1("kf%d" % i, [128, 1024], F32) for i in range(2)]
            kb = sb1("kb", [128, 1024], BF16)
            vf = [sb1("vf%d" % i, [128, 1024], F32) for i in range(2)]
            vx = [sb1("vx%d" % i, [128, 8, 130], BF16) for i in range(2)]
            ktt = [sb1("ktt%d" % i, [128, 8, 128], BF16) for i in range(2)]

            S.dma("sp", gk[:], gk_t.partition_broadcast(128), writes=["gk"])
            for b_ in range(2):
                V(lambda e, b_=b_: e.memset(vx[b_][:], 0.0), [], ["vx%d" % b_])
            for kt in range(16):
                S.dma("pool", wkv[:, kt, :], w_in[kt * 128:(kt + 1) * 128, 1024:3072], writes=["wkv"])

            def ktrans(b, dst_ap):
                for hh in range(H):
                    P(lambda e, hh=hh: e.transpose(pbf1[:, hh, :], kb[:, hh * 128:(hh + 1) * 128], idb[:]),
                      ["kb", "idb"], ["pbf1"], sig=(hh == H - 1))
                V(lambda e: e.tensor_copy(ktt[b][:], pbf1[:]), ["pbf1"], ["ktt%d" % b])
                S.dma("sp", dst_ap, ktt[b][:], reads=["ktt%d" % b], writes=["KTs"])

            sqb = sb1("sqb", [128, 1024], F32)

            def tiles_iter():
                return [t for t in range(NSL) if not (t >= KTILES and t < NSL - 2)]

            def xload(s):
                b3 = s % 3
                S.dma("sp", xt[b3][:], xs[s * 128:(s + 1) * 128, :], writes=["xt%d" % b3])

            def hchain(s):
                b = s % 2
                b3 = s % 3
                xk, hk = "xt%d" % b3, "hb%d" % b
                rstd_cols(xt[b3][:], 128, D, hb[b][:], 0, [xk], hk)
                A(lambda e: e.activation(out=tmp[:], in_=xt[b3][:], func=AF.Copy, scale=rs[:, 0:1]),
                  [xk, "rs0"], ["tmp"])
                G(lambda e: e.tensor_tensor(out=hb[b][:], in0=tmp[:], in1=gA[:], op=ALU.mult),
                  ["tmp", "gA"], [hk])

            def hT_make(s):
                b = s % 2
                hk, hTk = "hb%d" % b, "hT%d" % b
                for half in range(2):
                    for j in range(8):
                        kt = half * 8 + j
                        P(lambda e, kt=kt, j=j: e.transpose(pbf0[:, j, :], hb[b][:, kt * 128:(kt + 1) * 128], idb[:]),
                          [hk, "idb"], ["pbf0"], sig=(j == 7))
                    V(lambda e, half=half: e.tensor_copy(hT[b][:, half * 8:(half + 1) * 8, :], pbf0[:]),
                      ["pbf0"], [hTk])

            def mm(s):
                b = s % 2
                hTk = "hT%d" % b
                for cc in range(4):
                    dst = pk if cc < 2 else pv
                    c0 = (cc % 2) * 512
                    for kt in range(16):
                        P(lambda e, kt=kt, cc=cc, dst=dst, c0=c0: e.matmul(
                            dst[:, c0:c0 + 512], lhsT=hT[b][:, kt, :], rhs=wkv[:, kt, cc * 512:(cc + 1) * 512],
                            start=(kt == 0), stop=(kt == 15)),
                          [hTk, "wkv"], ["pk" if cc < 2 else "pv"], sig=(kt == 15))

            def evac(s):
                b = s % 2
                V(lambda e: e.tensor_copy(kraw[:], pk[:]), ["pk"], ["kraw"])
                A(lambda e: e.copy(out=vf[b][:], in_=pv[:]), ["pv"], ["vf%d" % b])

            def tail(s):
                b = s % 2
                headnorm(kraw[:], 128, sqb, kf[b][:], gk, "kraw", "sqb", "kf%d" % b, "gk")
                V(lambda e: e.tensor_copy(kb[:], kf[b][:]), ["kf%d" % b], ["kb"])
                A(lambda e: e.copy(out=vx[b][:, :, 0:128], in_=vf[b][:].rearrange("p (h d) -> p h d", d=128)),
                  ["vf%d" % b], ["vx%d" % b])
                V(lambda e: e.tensor_copy(vx[b][:, :, 128:129],
                                          kval[:, s:s + 1].unsqueeze(1).to_broadcast([128, 8, 1])),
                  ["kval"], ["vx%d" % b])
                S.dma("sp", VXs[:, :, s, :].rearrange("h p e -> p h e"), vx[b][:],
                      reads=["vx%d" % b], writes=["VXs"])
                if OWN0 <= s < NPT:
                    r0 = (s - OWN0) * 128
                    S.dma("sp", k_p[r0:r0 + 128, :], kf[b][:], reads=["kf%d" % b], writes=["k_p"])
                    S.dma("sp", v_p[r0:r0 + 128, :], vf[b][:], reads=["vf%d" % b], writes=["v_p"])
                if s == NPT:
                    for j in range(2):
                        S.dma("sp", k_s[j], kf[b][32 * j:32 * j + 16, :], reads=["kf%d" % b], writes=["k_s"])
                        S.dma("sp", v_s[j], vf[b][32 * j:32 * j + 16, :], reads=["vf%d" % b], writes=["v_s"])

            tl = tiles_iter()
            xload(tl[0])
            xload(tl[1])
            hchain(tl[0])
            hT_make(tl[0])
            for idx, s in enumerate(tl):
                nxt = tl[idx + 1] if idx + 1 < len(tl) else None
                if idx + 2 < len(tl):
                    xload(tl[idx + 2])
                if nxt is not None:
                    hchain(nxt)
                emit_casts(2)
                mm(s)
                evac(s)
                if nxt is not None:
                    hT_make(nxt)
                if idx > 0:
                    sp_ = tl[idx - 1]
                    ktrans(sp_ % 2, KTs[:, :, sp_ * 128:(sp_ + 1) * 128].rearrange("h p t -> p h t"))
                tail(s)
            sp_ = tl[-1]
            ktrans(sp_ % 2, KTs[:, :, sp_ * 128:(sp_ + 1) * 128].rearrange("h p t -> p h t"))

            for j in range(2 if KSTOP >= 2 else 0):
                for t in range(16):
                    i = j * 16 + t
                    b = i % 2
                    emit_casts(2)
                    S.dma("sp", xt[b][:, 0:1024], ck[j, t * 128:(t + 1) * 128, :], writes=["xt%d" % b])
                    S.dma("sp", xt[b][:, 1024:2048], cv[j, t * 128:(t + 1) * 128, :], writes=["xt%d" % b])
                    for hh in range(H):
                        P(lambda e, hh=hh: e.transpose(pk[:, hh * 128:(hh + 1) * 128],
                                                       xt[b][:, hh * 128:(hh + 1) * 128], idf[:]),
                          ["xt%d" % b, "idf"], ["pk"], sig=(hh == H - 1))
                    V(lambda e: e.tensor_copy(ktt[b][:], pk[:].rearrange("p (h t) -> p h t", t=128)),
                      ["pk"], ["ktt%d" % b])
                    S.dma("sp", KTc[j, :, :, t * 128:(t + 1) * 128].rearrange("h p t -> p h t"), ktt[b][:],
                          reads=["ktt%d" % b], writes=["KTc"])
                    G(lambda e: e.tensor_copy(vx[b][:, :, 0:128],
                                              xt[b][:, 1024:2048].rearrange("p (h d) -> p h d", d=128)),
                      ["xt%d" % b], ["vx%d" % b])
                    V(lambda e: e.tensor_copy(vx[b][:, :, 128:129],
                                              kval[:, NPT:NPT + 1].unsqueeze(1).to_broadcast([128, 8, 1])),
                      ["kval"], ["vx%d" % b])
                    S.dma("sp", VXc[j, :, :, t, :].rearrange("h p e -> p h e"), vx[b][:],
                          reads=["vx%d" % b], writes=["VXc"])
            emit_casts(len(casts))
            S.barrier()

        with ExitStack() as p2:
            def sb2(name, shape, dt):
                return sb(name, shape, dt, p2)

            NC_ = 256
            gF = sb2("gF", [128, D], F32)
            gq = sb2("gq", [128, 1024], F32)
            sg = sb2("sg", [128, 1024], F32)
            invc = sb2("invc", [128, 4, 16], F32)
            hv = sb2("hv", [128, 1], F32)
            psc = sb2("psc", [128, 8], F32)
            cw = sb2("cw", [128, KF, 3], F32)
            cb = sb2("cb", [128, KF], F32)
            lt = [sb2("lt%d" % i, [128, 64], F32) for i in range(4)]
            wpl = sb2("wpl", [128, 8, 256], BF16)
            xg = sb2("xg", [128, 2, D], F32)
            hT = sb2("hTm", [128, 16, NC_], BF16)
            tmpf = sb2("tmpf", [128, D], F32)
            hb = sb2("hbm", [128, D], BF16)
            qraw = sb2("qraw", [128, 1024], F32)
            qf = sb2("qf", [128, 1024], F32)
            qb = sb2("qb", [128, 1024], BF16)
            RA = sb2("RA", [128, 5504], F32)
            RB = sb2("RB", [128, 7728], F32)
            ring = [sb2("ring%d" % i, [128, 8192], BF16) for i in range(3)]
            zext = sb2("zext", [128, 3, 258], F32)
            zextB = sb2("zextB", [128, 3, 258], F32)
            zcB = sb2("zcB", [128, NC_], F32)
            szlB = sb2("szlB", [128, NC_], F32)
            zc = sb2("zc", [128, NC_], F32)
            szl = sb2("szl", [128, NC_], F32)
            zh = sb2("zh", [128, KF, 2], F32)
            sch = sb2("sch", [128, KF, 2, 2], F32)
            uh = sb2("uh", [128, 8, 15], F32)
            uexs = sb2("uexs", [128, 8, 3, 32], F32)
            zsave = sb2("zsave", [128, KF, 2, 2], F32)
            l12 = sb2("l12", [128, 4], F32)
            ofin = sb2("ofin", [128, 128], F32)
            t1f = sb2("t1f", [128, 128], F32)
            stg = tmpf[:, 0:1024]
            yst = [sb2("yst%d" % i, [128, 512], F32) for i in range(2)]

            RAb = RA[:].bitcast(BF16)
            RBb = RB[:].bitcast(BF16)
            aT = RAb[:, 0:KF * NC_].rearrange("p (k n) -> p k n", n=NC_)
            QT = RAb[:, 0:16 * NC_].rearrange("p (c k n) -> p c k n", c=2, n=NC_)
            uext = RA[:, 0:2168].rearrange("p (k n) -> p k n", n=271)
            pa = RA[:, 2168:2710].rearrange("p (k n) -> p k n", n=271)
            pb_ = RA[:, 2710:3252].rearrange("p (k n) -> p k n", n=271)
            pT = RAb[:, 6504:6504 + 2048].rearrange("p (k n) -> p k n", n=NC_)
            pbT = RAb[:, 8552:8552 + 2048].rearrange("p (k n) -> p k n", n=NC_)
            ob = RAb[:, 8552:8552 + 2048].rearrange("p (t f) -> p t f", f=1024)
            KTr = [RBb[:, i * 2048:(i + 1) * 2048] for i in range(3)]
            VXr = [RBb[:, 6144 + i * 2080:6144 + (i + 1) * 2080].rearrange("p (t e) -> p t e", e=130)
                   for i in range(3)]
            PTr = [RBb[:, 12384 + i * 1024:12384 + (i + 1) * 1024].rearrange("p (j c q) -> p j c q", c=2, q=256)
                   for i in range(3)]
            oT = RBb[:, 0:2048].rearrange("p (k n) -> p k n", n=NC_)
            mT = RBb[:, 2048:2048 + 4096].rearrange("p (k n) -> p k n", n=NC_)

            S.dma("sp", gF[:], g_ffn.partition_broadcast(128), writes=["gF"])
            S.dma("sp", gq[:], gq_t.partition_broadcast(128), writes=["gq"])
            S.dma("sp", sg[:], sg_t.partition_broadcast(128), writes=["sg"])
            S.dma("sp", invc[:].rearrange("p a b -> p (a b)"), invcnt.partition_broadcast(128), writes=["invc"])
            S.dma("sp", hv[:], hvalid[:, :], writes=["hv"])
            S.dma("sp", psc[:], psc_t[:, :], writes=["psc"])
            S.dma("sp", cw[:].rearrange("p a b -> p (a b)"), cw_t[:, :], writes=["cw"])
            S.dma("sp", cb[:], cb_t[:, :], writes=["cb"])
            S.dma("sp", wpl[:], wb_pool.rearrange("(k p) n -> p k n", p=128), reads=["wcast"], writes=["wpl"])
            for i, lv in enumerate((lq1, lk1, lq2, lk2)):
                S.dma("sp", lt[i][:], lv.partition_broadcast(128), writes=["lt%d" % i])
            V(lambda e: e.tensor_scalar(out=sg[:], in0=sg[:], scalar1=1.0 - LAM_INIT, scalar2=None, op0=ALU.mult),
              ["sg"], ["sg"])
            for i in range(2):
                V(lambda e, i=i: e.tensor_tensor(out=lt[2 * i][:], in0=lt[2 * i][:], in1=lt[2 * i + 1][:], op=ALU.mult),
                  ["lt%d" % (2 * i), "lt%d" % (2 * i + 1)], ["lt%d" % (2 * i)])
                V(lambda e, i=i: e.reduce_sum(out=lamc[:, i:i + 1], in_=lt[2 * i][:], axis=AX.X),
                  ["lt%d" % (2 * i)], ["lamc"])
            A(lambda e: e.activation(out=lamc[:, 0:2], in_=lamc[:, 0:2], func=AF.Exp), ["lamc"], ["lamc"])
            V(lambda e: e.tensor_tensor(out=lamc[:, 2:3], in0=lamc[:, 1:2], in1=lamc[:, 0:1], op=ALU.subtract),
              ["lamc"], ["lamc"])
            V(lambda e: e.tensor_scalar(out=lamc[:, 2:3], in0=lamc[:, 2:3], scalar1=-LAM_INIT, scalar2=None,
                                        op0=ALU.add), ["lamc"], ["lamc"])
            G(lambda e: e.memset(uh[:], 0.0), [], ["uh"])
            G(lambda e: e.memset(zh[:], 0.0), [], ["zh"])
            G(lambda e: e.memset(uexs[:], 0.0), [], ["uexs"])
            G(lambda e: e.memset(zext[:], 0.0), [], ["zext"])
            G(lambda e: e.memset(zextB[:], 0.0), [], ["zextB"])

            ring_i = [0]

            def wload(src_ap, kt_n, ncols):
                i = ring_i[0] % 3
                ring_i[0] += 1
                view = ring[i][:, 0:kt_n * ncols].rearrange("p (k n) -> p k n", n=ncols)
                S.dma("sp", view, src_ap, reads=["wcast"], writes=["ring%d" % i])
                return view, "ring%d" % i

            def wsrc(wb, r0, kt_n, c0, ncols):
                return wb[r0:r0 + kt_n * 128, c0:c0 + ncols].rearrange("(k p) n -> p k n", p=128)

            bank_i = {}

            def nbank(lo=0, n=4):
                c = bank_i.get((lo, n), 0)
                bank_i[(lo, n)] = c + 1
                i = lo + c % n
                return B[i], BK[i]

            def make_hT(toks, gtile, gkey, src_of):
                for ti, (nr, col0) in enumerate(toks):
                    src, skey = src_of(ti)
                    rstd_cols(src, nr, D, hb[0:nr, :], 0, [skey], "hbm")
                    A(lambda e: e.activation(out=tmpf[0:nr, :], in_=src, func=AF.Copy, scale=rs[0:nr, 0:1]),
                      [skey, "rs0"], ["tmpf"])
                    V(lambda e: e.tensor_tensor(out=hb[0:nr, :], in0=tmpf[0:nr, :], in1=gtile[0:nr, :], op=ALU.mult),
                      ["tmpf", gkey], ["hbm"])
                    for half in range(2):
                        for j in range(8):
                            kt = half * 8 + j
                            P(lambda e, kt=kt, j=j: e.transpose(pbf0[:, j, 0:nr], hb[0:nr, kt * 128:(kt + 1) * 128],
                                                                idb[0:nr, 0:nr]),
                              ["hbm", "idb"], ["pbf0"], sig=(j == 7))
                        V(lambda e, half=half: e.tensor_copy(hT[:, half * 8:(half + 1) * 8, col0:col0 + nr],
                                                             pbf0[:, :, 0:nr]), ["pbf0"], ["hT"])

            def finalize(acc0, acc1, pb0, nq, ti, hh, k0, k1):
                sl = slice(pb0, pb0 + nq)
                if KATT in ("st", "st0", "st0c0", "av", "nomask"):
                    return
                V(lambda e: e.tensor_copy(l12[sl, 0:1], acc0[:, 128:129]), [k0], ["l12"])
                V(lambda e: e.tensor_copy(l12[sl, 1:2], acc1[:, 128:129]), [k1], ["l12"])
                V(lambda e: e.tensor_scalar(out=l12[sl, 0:2], in0=l12[sl, 0:2], scalar1=1e-30, scalar2=None,
                                            op0=ALU.add), ["l12"], ["l12"])
                V(lambda e: e.reciprocal(l12[sl, 2:4], l12[sl, 0:2]), ["l12"], ["l12"])
                V(lambda e: e.tensor_tensor(out=l12[sl, 3:4], in0=l12[sl, 3:4], in1=lamc[sl, 2:3], op=ALU.mult),
                  ["l12", "lamc"], ["l12"])
                V(lambda e: e.tensor_scalar(out=t1f[sl, :], in0=acc0[:, 0:128], scalar1=l12[sl, 2:3], scalar2=None,
                                            op0=ALU.mult), [k0, "l12"], ["t1f"])
                V(lambda e: e.scalar_tensor_tensor(out=ofin[sl, :], in0=acc1[:, 0:128], scalar=l12[sl, 3:4],
                                                   in1=t1f[sl, :], op0=ALU.mult, op1=ALU.add),
                  [k1, "l12", "t1f"], ["ofin"])
                A(lambda e: e.activation(out=t1f[sl, :], in_=ofin[sl, :], func=AF.Square,
                                         accum_out=ss[sl, 1:2]), ["ofin"], ["t1f", "ss1"])
                A(lambda e: e.activation(out=rs[sl, 1:2], in_=ss[sl, 1:2], func=AF.Ln, scale=1.0 / 128, bias=EPS),
                  ["ss1"], ["rs1"])
                A(lambda e: e.activation(out=rs[sl, 1:2], in_=rs[sl, 1:2], func=AF.Exp, scale=-0.5),
                  ["rs1"], ["rs1"])
                A(lambda e: e.activation(out=t1f[sl, :], in_=ofin[sl, :], func=AF.Copy, scale=rs[sl, 1:2]),
                  ["ofin", "rs1"], ["t1f"])
                G(lambda e: e.tensor_tensor(out=ob[sl, ti, hh * 128:(hh + 1) * 128], in0=t1f[sl, :],
                                            in1=sg[sl, hh * 128:(hh + 1) * 128], op=ALU.mult),
                  ["t1f", "sg"], ["ob"])

            kv_i = [0]

            def kv_load(kt_src, vx_src, ntile, krows=128, kbase=0):
                i = kv_i[0] % 3
                kv_i[0] += 1
                if kt_src is not None:
                    S.dma("sp", KTr[i][:, 0:kt_src.shape[-1]], kt_src, reads=["KTs", "KTc"], writes=["KTr%d" % i])
                S.dma("sp", VXr[i][kbase:kbase + krows, 0:ntile, :], vx_src, reads=["VXs", "VXc"],
                      writes=["VXr%d" % i])
                return i

            pt_i = [0]

            def key_step(hh, tiles, nk, kb0, qlo, qhi, accs_per_tile, masks, stb):
                stt, stkeys = STB[stb]
                pi = pt_i[0] % 3
                pt_i[0] += 1
                PT = PTr[pi]
                ptk = "PT%d" % pi
                nt = len(tiles)
                for j, (i, tt) in enumerate(tiles):
                    for c in range(2):
                        o0 = j * 512 + c * 256
                        P(lambda e, c=c, i=i, tt=tt, o0=o0: e.matmul(
                            stt[kb0:kb0 + nk, o0 + qlo:o0 + qhi], lhsT=KTr[i][:, tt * 128:tt * 128 + nk],
                            rhs=QT[:, c, hh, qlo:qhi], start=True, stop=True),
                          ["KTr%d" % i, "QT"], [stkeys[j]], sig=(c == 1))
                if qlo == 0 and qhi == 256:
                    src = stt[kb0:kb0 + nk, 0:nt * 512]
                    dst = PT[kb0:kb0 + nk, 0:nt, :, :].rearrange("p j c q -> p (j c q)")
                else:
                    src = stt[kb0:kb0 + nk, 0:nt * 512].rearrange("p (j c q) -> p j c q", c=2, q=256)[:, :, :, qlo:qhi]
                    dst = PT[kb0:kb0 + nk, 0:nt, :, qlo:qhi]
                A(lambda e: e.activation(out=dst, in_=src, func=AF.Exp, scale=0.125),
                  [stkeys[j] for j in range(nt)], [ptk])
                for (q0,) in masks:
                    V(lambda e, q0=q0: e.memset(PT[64:128, 0, :, q0:q0 + 64], 0.0), [], [ptk])

                def av():
                    for j, (i, tt) in enumerate(tiles):
                        accs = accs_per_tile[j]
                        for ai, (c, acc, akey, qc0, qc1, fi, la) in enumerate(accs):
                            P(lambda e, c=c, acc=acc, qc0=qc0, qc1=qc1, fi=fi, la=la, i=i, tt=tt, j=j: e.matmul(
                                acc, lhsT=PT[kb0:kb0 + nk, j, c, qc0:qc1], rhs=VXr[i][kb0:kb0 + nk, tt, 0:129],
                                start=fi, stop=la),
                              [ptk, "VXr%d" % i], [akey], sig=(j == nt - 1 and ai == len(accs) - 1))
                return av

            pend_av = []

            def pipe_push(av):
                if av is not None:
                    pend_av.append(av)
                while len(pend_av) > 2:
                    pend_av.pop(0)()

            def pipe_flush():
                while pend_av:
                    pend_av.pop(0)()

            def attend_main(gi):
                j0 = 2 * gi
                T = OWN0 + j0 + 2
                nch = (T + 15) // 16
                TP = OWN0 + j0
                for hh in range(H):
                    loaded = {}

                    def ld(ch):
                        t0 = ch * 16
                        nt = min(16, T - t0)
                        loaded[ch] = kv_load(KTs[hh, :, t0 * 128:(t0 + nt) * 128], VXs[hh, :, t0:t0 + nt, :], nt)
                    ld(0)
                    if nch > 1:
                        ld(1)

                    def accs_for(t):
                        full = t <= OWN0 + j0
                        accs = []
                        for c in range(2):
                            for jj in range(2):
                                if jj == 0 and not full:
                                    continue
                                last_t = OWN0 + j0 + jj
                                accs.append((c, ACC[2 * c + jj][0][:, 0:129], ACC[2 * c + jj][1], jj * 128,
                                             (jj + 1) * 128, t == 0 and jj == 0, t == last_t))
                        return accs
                    step = 0
                    t = 0
                    while t < T:
                        ch, tt = divmod(t, 16)
                        if tt == 4 and ch + 2 < nch:
                            ld(ch + 2)
                        if t < TP:
                            i = loaded[ch]
                            pipe_push(key_step(hh, [(i, tt), (i, tt + 1)], 128, 0, 0, 256,
                                               [accs_for(t), accs_for(t + 1)], [], step % 3))
                            t += 2
                        else:
                            i = loaded[ch]
                            full = t <= OWN0 + j0
                            masks = [(0,)] if t == OWN0 + j0 else [(128,)]
                            pipe_push(key_step(hh, [(i, tt)], 128, 0, 0 if full else 128, 256,
                                               [accs_for(t)], masks, step % 3))
                            t += 1
                        step += 1
                    pipe_flush()
                    for jj in range(2):
                        finalize(ACC[jj][0][:, 0:129], ACC[2 + jj][0][:, 0:129], 0, 128, jj, hh, ACC[jj][1],
                                 ACC[2 + jj][1])

            def attend_small():
                for hh in range(H):
                    for j in range(2):
                        i = kv_load(KTc[j, hh, :, :], VXc[j, hh, :, :, :], 16)
                        pb0 = 32 * j
                        a0 = ACC[0][0][pb0:pb0 + 16, 0:129]
                        a1 = ACC[2][0][pb0:pb0 + 16, 0:129]
                        for t in range(0, 16, 2):
                            accs = [[(0, a0, ACC[0][1], pb0, pb0 + 16, tq == 0, False),
                                     (1, a1, ACC[2][1], pb0, pb0 + 16, tq == 0, False)] for tq in (t, t + 1)]
                            pipe_push(key_step(hh, [(i, t), (i, t + 1)], 128, 0, pb0, pb0 + 16, accs, [],
                                               (t // 2) % 3))
                        pipe_flush()
                        i2 = kv_load(KTs[hh, :, NPT * 128 + pb0:NPT * 128 + pb0 + 16],
                                     VXs[hh, pb0:pb0 + 16, NPT:NPT + 1, :], 1, krows=16, kbase=pb0)
                        accs = [[(0, a0, ACC[0][1], pb0, pb0 + 16, False, True),
                                 (1, a1, ACC[2][1], pb0, pb0 + 16, False, True)]]
                        pipe_push(key_step(hh, [(i2, 0)], 16, pb0, pb0, pb0 + 16, accs, [], 0))
                        pipe_flush()
                        finalize(a0, a1, pb0, 16, 0, hh, ACC[0][1], ACC[2][1])
                    a0 = ACC[0][0][64:81, 0:129]
                    a1 = ACC[2][0][64:81, 0:129]
                    nch = OWN0 // 16
                    step = 0
                    for ch in range(nch):
                        i = kv_load(KTs[hh, :, ch * 2048:(ch + 1) * 2048], VXs[hh, :, ch * 16:(ch + 1) * 16, :], 16)
                        for tt in range(0, 16, 2):
                            t = ch * 16 + tt
                            accs = [[(0, a0, ACC[0][1], 64, 81, tq == 0, tq == OWN0 - 1),
                                     (1, a1, ACC[2][1], 64, 81, tq == 0, tq == OWN0 - 1)] for tq in (t, t + 1)]
                            pipe_push(key_step(hh, [(i, tt), (i, tt + 1)], 128, 0, 64, 81, accs, [], step % 3))
                            step += 1
                        pipe_flush()
                    finalize(a0, a1, 64, 17, 0, hh, ACC[0][1], ACC[2][1])

            def do_group(kind, gi):
                small = kind == "small"
                if small:
                    n = SM
                    toks = [(SM, 0)]
                    rows0 = [NPT * 128]
                    segs = [(0, 16), (32, 16), (64, 17)]
                else:
                    n = 256
                    toks = [(128, 0), (128, 128)]
                    rows0 = [(OWN0 + 2 * gi) * 128, (OWN0 + 2 * gi + 1) * 128]
                    segs = [(0, 256)]
                for ti, (nr, col0) in enumerate(toks):
                    S.dma("sp", xg[0:nr, ti, :], xs[rows0[ti]:rows0[ti] + nr, :], writes=["xg%d" % ti])
                make_hT(toks, gA, "gA", lambda ti: (xg[0:toks[ti][0], ti, :], "xg%d" % ti))
                V(lambda e: e.memset(QT[64:128, 0, :, :], 0.0), [], ["QT"])
                V(lambda e: e.memset(QT[0:64, 1, :, :], 0.0), [], ["QT"])
                wq = [wload(wbp_in[cc], 16, 512) for cc in range(2)]
                for ti, (nr, col0) in enumerate(toks):
                    for cc in range(2):
                        wv, wk = wq[cc]
                        bk, bkk = nbank()
                        for kt in range(16):
                            P(lambda e, kt=kt, wv=wv, bk=bk: e.matmul(
                                bk[0:nr, :], lhsT=hT[:, kt, col0:col0 + nr], rhs=wv[:, kt, :],
                                start=(kt == 0), stop=(kt == 15)), ["hT", wk], [bkk], sig=(kt == 15))
                        V(lambda e, bk=bk, cc=cc: e.tensor_copy(qraw[0:nr, cc * 512:(cc + 1) * 512], bk[0:nr, :]),
                          [bkk], ["qraw"])
                    headnorm(qraw[0:nr, :], nr, tmpf, qf[0:nr, :], gq, "qraw", "tmpf", "qf", "gq")
                    G(lambda e: e.tensor_copy(qb[0:nr, :], qf[0:nr, :]), ["qf"], ["qb"])
                    for hh in range(H):
                        P(lambda e, hh=hh: e.transpose(pbf1[:, hh, 0:nr], qb[0:nr, hh * 128:(hh + 1) * 128],
                                                       idb[0:nr, 0:nr]), ["qb", "idb"], ["pbf1"], sig=(hh == H - 1))
                    V(lambda e: e.tensor_copy(QT[0:64, 0, :, col0:col0 + nr], pbf1[0:64, :, 0:nr]), ["pbf1"], ["QT"])
                    V(lambda e: e.tensor_copy(QT[64:128, 1, :, col0:col0 + nr], pbf1[64:128, :, 0:nr]),
                      ["pbf1"], ["QT"])
                if KSUB < 3.1:
                    return
                if small:
                    V(lambda e: e.memset(ob[:, 0, :], 0.0), [], ["ob"])
                    attend_small()
                else:
                    attend_main(gi)
                if KSUB < 3.2:
                    return
                S.barrier()
                for ti, (nr, col0) in enumerate(toks):
                    for hh in range(H):
                        P(lambda e, hh=hh: e.transpose(pbf1[:, hh, 0:nr], ob[0:nr, ti, hh * 128:(hh + 1) * 128],
                                                       idb[0:nr, 0:nr]), ["ob", "idb"], ["pbf1"], sig=(hh == H - 1))
                    V(lambda e: e.tensor_copy(oT[:, :, col0:col0 + nr], pbf1[:, :, 0:nr]), ["pbf1"], ["oT"])
                for pc in range(2):
                    wv, wk = wload(wbp_in[6 + pc], 16, 512)
                    for mi in range(4):
                        m = pc * 4 + mi
                        bk, bkk = nbank()
                        for kt in range(16):
                            P(lambda e, kt=kt, wv=wv, bk=bk, mi=mi: e.matmul(
                                bk[:, 0:n], lhsT=wv[:, kt, mi * 128:(mi + 1) * 128], rhs=hT[:, kt, 0:n],
                                start=(kt == 0), stop=(kt == 15)), ["hT", wk], [bkk], sig=(kt == 15))
                        if small:
                            for si, (c0, ln) in enumerate(segs):
                                V(lambda e, bk=bk, m=m, si=si, c0=c0, ln=ln: e.tensor_copy(
                                    uexs[:, m, si, 15:15 + ln], bk[:, c0:c0 + ln]), [bkk], ["uexs"])
                        else:
                            V(lambda e, bk=bk, m=m: e.tensor_copy(uext[:, m, 15:15 + n], bk[:, 0:n]), [bkk], ["uext"])
                if small:
                    for j in range(2):
                        V(lambda e: e.memset(stg[0:16, :], 0.0), [], ["tmpf"])
                        S.dma("sp", stg[0:15, :], sp_in[j], writes=["tmpf"])
                        for m in range(8):
                            P(lambda e, m=m: e.transpose(pk[:, m * 16:m * 16 + 16], stg[0:16, m * 128:(m + 1) * 128],
                                                         idf[0:16, 0:16]), ["tmpf", "idf"], ["B0", "B1"], sig=(m == 7))
                        V(lambda e, j=j: e.tensor_copy(uexs[:, :, j, 0:15],
                                                       pk[:, 0:128].rearrange("p (m t) -> p m t", t=16)[:, :, 0:15]),
                          ["B0", "B1"], ["uexs"])
                else:
                    V(lambda e: e.tensor_copy(uext[:, :, 0:15], uh[:]), ["uh"], ["uext"])
                if small:
                    views = [(uexs[:, :, si, :], 15 + ln, c0, ln) for si, (c0, ln) in enumerate(segs)]
                else:
                    views = [(uext, 15 + n, 0, n)]
                for (uv, L, c0, ln) in views:
                    for wi, wdw in enumerate((2, 4, 8, 16)):
                        src = uv[:, 2 * wi:2 * wi + 2, :]
                        sh = 1
                        cur_src = src
                        bufs = [pa, pb_]
                        bi = 0
                        for step in range(wi + 1):
                            dst = bufs[bi][:, :, 0:L]
                            V(lambda e, dst=dst, cur_src=cur_src, sh=sh: e.tensor_tensor(
                                out=dst[:, :, sh:L], in0=cur_src[:, :, sh:L], in1=cur_src[:, :, 0:L - sh], op=ALU.add),
                              ["uext", "uexs", "pa", "pb"], ["pa" if bi == 0 else "pb"])
                            cur_src = dst
                            sh *= 2
                            bi ^= 1
                        dst = bufs[bi][:, :, 0:L]
                        V(lambda e, dst=dst, cur_src=cur_src, wdw=wdw: e.tensor_scalar(
                            out=dst[:, :, 15:L], in0=cur_src[:, :, 15:L], scalar1=1.0 / wdw, scalar2=None,
                            op0=ALU.mult), ["pa", "pb"], ["pa" if bi == 0 else "pb"])
                        if (not small) and gi == 0:
                            V(lambda e, dst=dst, cur_src=cur_src, wi=wi: e.tensor_tensor(
                                out=dst[:, :, 15:31], in0=cur_src[:, :, 15:31],
                                in1=invc[:, wi, :].unsqueeze(1).to_broadcast([128, 2, 16]), op=ALU.mult),
                              ["pa", "pb", "invc"], ["pa" if bi == 0 else "pb"])
                        V(lambda e, dst=dst, src=src, wi=wi, c0=c0, ln=ln, L=L: e.tensor_tensor(
                            out=pT[:, 2 * wi:2 * wi + 2, c0:c0 + ln], in0=dst[:, :, 15:L], in1=src[:, :, 15:L],
                            op=ALU.subtract), ["pa", "pb", "uext", "uexs"], ["pT"])
                if small:
                    V(lambda e: e.tensor_copy(uh[:], uexs[:, :, 2, 17:32]), ["uexs"], ["uh"])
                else:
                    V(lambda e: e.tensor_copy(uh[:], uext[:, :, 256:271]), ["uext"], ["uh"])

                def emit_pool_state(src_fn, dst_ap):
                    for m in range(8):
                        P(lambda e, m=m: e.transpose(pk[0:15, m * 128:(m + 1) * 128], src_fn(m), idf[:]),
                          ["uexs", "uh", "idf"], ["B0", "B1"], sig=(m == 7))
                    V(lambda e: e.tensor_copy(stg[0:15, :], pk[0:15, :]), ["B0", "B1"], ["tmpf"])
                    S.dma("sp", dst_ap, stg[0:15, :], reads=["tmpf"], writes=["pool_o"])
                if small:
                    for j in range(2):
                        emit_pool_state(lambda m, j=j: uexs[:, m, j, 16:31], pool_s[j])
                elif gi == NG - 1:
                    emit_pool_state(lambda m: uh[:, m, :], pool_p[:, :])
                if KSUB < 3.3:
                    return
                for wi in range(4):
                    for mo in range(2):
                        bk, bkk = nbank()
                        for ki in range(2):
                            P(lambda e, ki=ki, bk=bk, wi=wi, mo=mo: e.matmul(
                                bk[:, 0:n], lhsT=wpl[:, 2 * wi + ki, mo * 128:(mo + 1) * 128],
                                rhs=pT[:, 2 * wi + ki, 0:n], start=(ki == 0), stop=(ki == 1)),
                              ["pT", "wpl"], [bkk], sig=(ki == 1))
                        A(lambda e, bk=bk, wi=wi, mo=mo: e.activation(
                            out=pbT[:, 2 * wi + mo, 0:n], in_=bk[:, 0:n], func=AF.Copy,
                            scale=psc[:, 2 * wi + mo:2 * wi + mo + 1]), [bkk, "psc"], ["pbT"])
                if KSUB < 3.33:
                    return
                for pc in range(4):
                    wga = wload(wbp_in[8 + pc], 16, 512)
                    wgb = wload(wbp_in[12 + pc], 16, 512)
                    i = ring_i[0] % 3
                    ring_i[0] += 1
                    wua_v = ring[i][:, 0:4096].rearrange("p (k n) -> p k n", n=512)
                    wup_v = ring[i][:, 4096:8192].rearrange("p (k n) -> p k n", n=512)
                    S.dma("sp", wua_v, wbp_ua[pc], reads=["wcast"], writes=["ring%d" % i])
                    S.dma("sp", wup_v, wbp_up[pc], reads=["wcast"], writes=["ring%d" % i])
                    wuk = "ring%d" % i
                    for mi in range(4):
                        m = pc * 4 + mi
                        ms = slice(mi * 128, (mi + 1) * 128)
                        specs = [(wga[0], wga[1], hT, "hT", 16), (wua_v, wuk, oT, "oT", 8),
                                 (wgb[0], wgb[1], hT, "hT", 16), (wup_v, wuk, pbT, "pbT", 8)]
                        bs = BSET[m % 2]
                        gz, gzk = (zc, "zc") if m % 2 == 0 else (zcB, "zcB")
                        gs, gsk = (szl, "szl") if m % 2 == 0 else (szlB, "szlB")
                        for bi_, (wv, wk, act, ak, nk) in enumerate(specs):
                            for kt in range(nk):
                                P(lambda e, kt=kt, wv=wv, act=act, bi_=bi_, nk=nk: e.matmul(
                                    bs[bi_][0][:, 0:n], lhsT=wv[:, kt, ms], rhs=act[:, kt, 0:n],
                                    start=(kt == 0), stop=(kt == nk - 1)), [wk, ak], [bs[bi_][1]], sig=(kt == nk - 1))
                        A(lambda e: e.activation(out=gz[:, 0:n], in_=bs[0][0][:, 0:n], func=AF.Sigmoid), [bs[0][1]], [gzk])
                        A(lambda e: e.activation(out=gs[:, 0:n], in_=bs[2][0][:, 0:n], func=AF.Sigmoid), [bs[2][1]], [gsk])
                        V(lambda e: e.tensor_tensor(out=gz[:, 0:n], in0=gz[:, 0:n], in1=bs[1][0][:, 0:n], op=ALU.mult),
                          [gzk, bs[1][1]], [gzk])
                        V(lambda e: e.tensor_tensor(out=gs[:, 0:n], in0=gs[:, 0:n], in1=bs[3][0][:, 0:n], op=ALU.mult),
                          [gsk, bs[3][1]], [gsk])
                        G(lambda e, m=m: e.tensor_tensor(out=mT[:, m, 0:n], in0=gz[:, 0:n], in1=gs[:, 0:n],
                                                         op=ALU.add), [gzk, gsk], ["mT"])
                if KSUB < 3.4:
                    return
                for cc in range(4):
                    wv, wk = wload(wbp_out[cc], 16, 512)
                    for ti, (nr, col0) in enumerate(toks):
                        bk, bkk = nbank()
                        for kt in range(16):
                            P(lambda e, kt=kt, wv=wv, bk=bk: e.matmul(
                                bk[0:nr, :], lhsT=mT[:, kt, col0:col0 + nr], rhs=wv[:, kt, :],
                                start=(kt == 0), stop=(kt == 15)), ["mT", wk], [bkk], sig=(kt == 15))
                        V(lambda e, bk=bk, cc=cc, ti=ti: e.tensor_tensor(
                            out=xg[0:nr, ti, cc * 512:(cc + 1) * 512], in0=xg[0:nr, ti, cc * 512:(cc + 1) * 512],
                            in1=bk[0:nr, :], op=ALU.add), ["xg%d" % ti, bkk], ["xg%d" % ti])
                if KSUB < 3.5:
                    return
                S.barrier()
                make_hT(toks, gF, "gF", lambda ti: (xg[0:toks[ti][0], ti, :], "xg%d" % ti))
                if small:
                    for j in range(2):
                        for r in range(6):
                            m0 = r * 8
                            mn = min(8, KF - m0)
                            S.dma("sp", stg[0:2, 0:mn * 128], sc_in[j, :, m0 * 128:(m0 + mn) * 128], writes=["tmpf"])
                            for mm in range(mn):
                                P(lambda e, mm=mm: e.transpose(pk[:, mm * 2:mm * 2 + 2],
                                                               stg[0:2, mm * 128:(mm + 1) * 128], idf[0:2, 0:2]),
                                  ["tmpf", "idf"], ["B0", "B1"], sig=(mm == mn - 1))
                            V(lambda e, j=j, m0=m0, mn=mn: e.tensor_copy(
                                sch[:, m0:m0 + mn, j, :], pk[:, 0:mn * 2].rearrange("p (m t) -> p m t", t=2)),
                              ["B0", "B1"], ["sch"])
                for m in range(KF):
                    if m % 4 == 0:
                        mn = min(4, KF - m)
                        wz = wload(wbp_fz[m // 4][:, :, 0:mn * 128], 16, mn * 128)
                        wvv = wload(wbp_fv[m // 4][:, :, 0:mn * 128], 16, mn * 128)
                    mi = m % 4
                    ms = slice(mi * 128, (mi + 1) * 128)
                    zx, zxk = ((zext, "zext"), (zextB, "zextB"))[m % 2]
                    zcc, zck = ((zc, "zc"), (zcB, "zcB"))[m % 2]
                    szz, szk = ((szl, "szl"), (szlB, "szlB"))[m % 2]
                    bz, bzk = ZR[m % 4]
                    bv, bvk = VR[m % 4]
                    for kt in range(16):
                        P(lambda e, kt=kt, bz=bz: e.matmul(bz[:, 0:n], lhsT=wz[0][:, kt, ms], rhs=hT[:, kt, 0:n],
                                                           start=(kt == 0), stop=(kt == 15)),
                          ["hT", wz[1]], [bzk], sig=(kt == 15))
                    for kt in range(16):
                        P(lambda e, kt=kt, bv=bv: e.matmul(bv[:, 0:n], lhsT=wvv[0][:, kt, ms], rhs=hT[:, kt, 0:n],
                                                           start=(kt == 0), stop=(kt == 15)),
                          ["hT", wvv[1]], [bvk], sig=(kt == 15))
                    for si, (c0, ln) in enumerate(segs):
                        if small:
                            if si < 2:
                                V(lambda e, si=si, m=m: e.tensor_copy(zx[:, si, 0:2], sch[:, m, si, :]),
                                  ["sch"], [zxk])
                            else:
                                V(lambda e, si=si: e.memset(zx[:, si, 0:2], 0.0), [], [zxk])
                        else:
                            G(lambda e, m=m: e.tensor_copy(zx[:, 0, 0:2], zh[:, m, :]), ["zh"], [zxk])
                        A(lambda e, si=si, c0=c0, ln=ln, bz=bz: e.copy(out=zx[:, si, 2:2 + ln], in_=bz[:, c0:c0 + ln]),
                          [bzk], [zxk])
                        V(lambda e, si=si, c0=c0, ln=ln, m=m: e.tensor_scalar(
                            out=zcc[:, c0:c0 + ln], in0=zx[:, si, 2:2 + ln], scalar1=cw[:, m, 2:3],
                            scalar2=cb[:, m:m + 1], op0=ALU.mult, op1=ALU.add), [zxk, "cw", "cb"], [zck])
                        for tap in (1, 0):
                            V(lambda e, si=si, c0=c0, ln=ln, m=m, tap=tap: e.scalar_tensor_tensor(
                                out=zcc[:, c0:c0 + ln], in0=zx[:, si, tap:tap + ln], scalar=cw[:, m, tap:tap + 1],
                                in1=zcc[:, c0:c0 + ln], op0=ALU.mult, op1=ALU.add), [zxk, "cw", zck], [zck])
                        A(lambda e, c0=c0, ln=ln: e.activation(out=szz[:, c0:c0 + ln], in_=zcc[:, c0:c0 + ln],
                                                               func=AF.Silu), [zck], [szk])
                        V(lambda e, c0=c0, ln=ln, m=m, bv=bv: e.tensor_tensor(
                            out=aT[:, m, c0:c0 + ln], in0=szz[:, c0:c0 + ln], in1=bv[:, c0:c0 + ln], op=ALU.mult),
                          [szk, bvk], ["aT"])
                        if small:
                            if si < 2:
                                V(lambda e, si=si, m=m: e.tensor_copy(zsave[:, m, si, :], zx[:, si, 16:18]),
                                  [zxk], ["zsave"])
                            else:
                                V(lambda e, m=m: e.tensor_scalar(out=zh[:, m, :], in0=zx[:, 2, 17:19],
                                                                 scalar1=hv[:, 0:1], scalar2=None, op0=ALU.mult),
                                  [zxk, "hv"], ["zh"])
                        else:
                            G(lambda e, m=m: e.tensor_copy(zh[:, m, :], zx[:, 0, 256:258]), [zxk], ["zh"])
                    if small and n > 81:
                        pass
                if small:
                    for (g0, g1) in ((16, 32), (48, 64), (81, SM)):
                        V(lambda e, g0=g0, g1=g1: e.memset(aT[:, :, g0:g1], 0.0), ["aT"], ["aT"])
                if KSUB < 3.6:
                    return
                deferred = []
                for cc in range(4):
                    accb = []
                    for ti in range(len(toks)):
                        accb.append(nbank())
                    for pcs, (k0, kn) in enumerate(((0, 16), (16, 16), (32, 11))):
                        wv, wk = wload(wbp_fo[cc][:, k0:k0 + kn, :], kn, 512)
                        for ti, (nr, col0) in enumerate(toks):
                            bk, bkk = accb[ti]
                            if pcs == 2 and ti == 0:
                                for dfn in deferred:
                                    dfn()
                                deferred = []
                            for kt in range(kn):
                                P(lambda e, kt=kt, wv=wv, bk=bk, k0=k0: e.matmul(
                                    bk[0:nr, :], lhsT=aT[:, k0 + kt, col0:col0 + nr], rhs=wv[:, kt, :],
                                    start=(k0 + kt == 0), stop=(k0 + kt == KF - 1)),
                                  ["aT", wk], [bkk], sig=(kt == kn - 1))
                    for ti, (nr, col0) in enumerate(toks):
                        bk, bkk = accb[ti]
                        yb = (cc * 2 + ti) % 2
                        V(lambda e, bk=bk, yb=yb, ti=ti, cc=cc: e.tensor_tensor(
                            out=yst[yb][0:nr, :], in0=xg[0:nr, ti, cc * 512:(cc + 1) * 512], in1=bk[0:nr, :],
                            op=ALU.add), ["xg%d" % ti, bkk], ["yst%d" % yb])
                        if small:
                            for j in range(2):
                                deferred.append(lambda j=j, cc=cc, yb=yb: S.dma(
                                    "sp", y_s[j, :, cc * 512:(cc + 1) * 512], yst[yb][32 * j:32 * j + 16, :],
                                    reads=["yst%d" % yb], writes=["y_s"]))
                        else:
                            r0 = (2 * gi + ti) * 128
                            deferred.append(lambda r0=r0, cc=cc, yb=yb: S.dma(
                                "sp", y_p[r0:r0 + 128, cc * 512:(cc + 1) * 512], yst[yb][:, :],
                                reads=["yst%d" % yb], writes=["y_p"]))
                for dfn in deferred:
                    dfn()
                S.barrier()

            def emit_conv_state(src_fn, dst_ap):
                for r in range(6):
                    m0 = r * 8
                    mn = min(8, KF - m0)
                    for mm in range(mn):
                        P(lambda e, mm=mm: e.transpose(pk[0:2, mm * 128:(mm + 1) * 128], src_fn(m0 + mm), idf[:]),
                          ["zsave", "zh", "idf"], ["B0", "B1"], sig=(mm == mn - 1))
                    V(lambda e, mn=mn: e.tensor_copy(stg[0:2, 0:mn * 128], pk[0:2, 0:mn * 128]), ["B0", "B1"], ["tmpf"])
                    S.dma("sp", dst_ap[:, m0 * 128:(m0 + mn) * 128], stg[0:2, 0:mn * 128], reads=["tmpf"],
                          writes=["conv_o"])

            if KSTOP >= 3 and KSMALL:
                do_group("small", -1)
                for j in range(2):
                    emit_conv_state(lambda m, j=j: zsave[:, m, j, :], conv_s[j])
            for gi in range(KGROUPS if KSTOP >= 4 else 0):
                do_group("main", gi)
            if KSTOP >= 4:
                emit_conv_state(lambda m: zh[:, m, :], conv_p[:, :])
            S.barrier()
    S.sync_all(["sp"])
    print("sbuf remaining", nc.sbuf_bytes_remaining)
    print("program: ins", S.nins, "waits", S.nwait, "cnt", S.cnt, flush=True)
    return nc


_NC_CACHE = {}


def kernel(_prep_only=False, **inp):
    f = lambda a: np.ascontiguousarray(np.asarray(a, dtype=np.float32))
    x_prompt = f(inp["x_prompt"])[0]
    x_sample = f(inp["x_sample"])
    cache_k = f(inp["cache_k"])[0].reshape(16, 2048, 1024)
    cache_v = f(inp["cache_v"])[0].reshape(16, 2048, 1024)
    state_pool = f(inp["state_pool"])[0]
    state_conv = f(inp["state_conv"])[0]
    shared = {
        "ident": np.eye(128, dtype=np.float32),
        "g_attn": f(inp["attn_norm_g"]).reshape(1, D),
        "g_ffn": f(inp["ffn_norm_g"]).reshape(1, D),
        "gq_t": np.tile(f(inp["q_norm_g"]).reshape(1, 128), (1, 8)),
        "gk_t": np.tile(f(inp["k_norm_g"]).reshape(1, 128), (1, 8)),
        "sg_t": np.tile(f(inp["subln_g"]).reshape(1, 128), (1, 8)),
        "lq1": f(inp["lambda_q1"]).reshape(1, 64), "lk1": f(inp["lambda_k1"]).reshape(1, 64),
        "lq2": f(inp["lambda_q2"]).reshape(1, 64), "lk2": f(inp["lambda_k2"]).reshape(1, 64),
        "psc_t": np.ascontiguousarray(f(inp["pool_scale"]).reshape(8, 128).T),
        "cw_t": np.ascontiguousarray(f(inp["conv_w"])[0].reshape(3, KF, 128).transpose(2, 1, 0).reshape(128, KF * 3)),
        "cb_t": np.ascontiguousarray(f(inp["conv_b"]).reshape(KF, 128).T),
        "w_in": f(inp["w_in"])[0], "w_pool": f(inp["w_pool"])[0].reshape(1024, 256),
        "w_ua": f(inp["w_up_attn"])[0], "w_up": f(inp["w_up_pool"])[0], "w_out": f(inp["w_out"])[0],
        "w_fi": f(inp["w_ffn_in"])[0], "w_fo": f(inp["w_ffn_out"])[0],
    }
    in_maps = []
    for c in range(NCORE):
        xs = np.zeros((NSL * 128, D), np.float32)
        ntrue = (16 * c + 16) * 128
        xs[NPT * 128 - ntrue:NPT * 128] = x_prompt[0:ntrue]
        sm = xs[NPT * 128:]
        sm[0:16] = x_sample[2 * c]
        sm[32:48] = x_sample[2 * c + 1]
        if c > 0:
            sm[64:81] = x_prompt[2048 * c - 17:2048 * c]
        kvalid = np.zeros((128, NSL), np.float32)
        kvalid[:, NPT - 16 * (c + 1):] = 1.0
        invcnt = np.zeros((4, 16), np.float32)
        for wi, w in enumerate((2, 4, 8, 16)):
            for t in range(16):
                invcnt[wi, t] = (1.0 / min(w, t + 1)) if c == 0 else 1.0 / w
        m = dict(shared)
        m.update({
            "xs": xs, "kvalid": kvalid,
            "ck": np.ascontiguousarray(cache_k[2 * c:2 * c + 2]), "cv": np.ascontiguousarray(cache_v[2 * c:2 * c + 2]),
            "sp_in": np.ascontiguousarray(state_pool[2 * c:2 * c + 2]),
            "sc_in": np.ascontiguousarray(state_conv[2 * c:2 * c + 2]),
            "invcnt": invcnt.reshape(1, 64),
            "hvalid": np.full((128, 1), 0.0 if c == 0 else 1.0, np.float32),
        })
        in_maps.append(m)
    if _prep_only:
        return in_maps
    if "nc" not in _NC_CACHE:
        _NC_CACHE["nc"] = build()
    res = run_bass_kernel_spmd(_NC_CACHE["nc"], in_maps, core_ids=list(range(NCORE)))
    R = res.results
    cat = lambda k: np.concatenate([np.asarray(R[c][k]) for c in range(NCORE)], axis=0)
    y_p = cat("y_p").reshape(1, 16384, D)
    y_s = cat("y_s").reshape(16, 16, D)
    k_p = cat("k_p").reshape(1, 1, 16384, 8, 128)
    v_p = cat("v_p").reshape(1, 1, 16384, 8, 128)
    pool_p = np.asarray(R[NCORE - 1]["pool_p"]).reshape(1, 1, 15, 1024)
    conv_p = np.asarray(R[NCORE - 1]["conv_p"]).reshape(1, 1, 2, DFF)
    k_s = cat("k_s").reshape(1, 16, 16, 8, 128)
    v_s = cat("v_s").reshape(1, 16, 16, 8, 128)
    pool_s = cat("pool_s").reshape(1, 16, 15, 1024)
    conv_s = cat("conv_s").reshape(1, 16, 2, DFF)
    return tuple(np.ascontiguousarray(a, dtype=np.float32) for a in
                 (y_p, y_s, k_p, v_p, pool_p, conv_p, k_s, v_s, pool_s, conv_s))
```

```python
import os
import numpy as np
from contextlib import ExitStack
import concourse.bass as bass
import concourse.mybir as mybir
from concourse.bass_utils import run_bass_kernel_spmd

F32 = mybir.dt.float32
BF16 = mybir.dt.bfloat16
AF = mybir.ActivationFunctionType
ALU = mybir.AluOpType
AX = mybir.AxisListType

D = 2048
H = 8
DFF = 5504
KF = 43
NPT = 128
NSL = 129
OWN0 = 112
NG = 8
EPS = 1e-6
LAM_INIT = 0.2
NCORE = 8
SM = 96
KSTOP = float(os.environ.get("KSTOP", "99"))
KTILES = int(os.environ.get("KTILES", "%d" % 129))
KSUB = float(os.environ.get("KSUB", "99"))
KSMALL = int(os.environ.get("KSMALL", "1"))
KGROUPS = int(os.environ.get("KGROUPS", "8"))
KATT = os.environ.get("KATT", "")


class _Tok:
    __slots__ = ("sem", "val", "eng")

    def __init__(self, sem, val, eng):
        self.sem = sem
        self.val = val
        self.eng = eng


class Sched:
    NDS = 40

    def __init__(self, nc):
        self.nc = nc
        self.eng = {"pe": nc.tensor, "act": nc.scalar, "dve": nc.vector,
                    "pool": nc.gpsimd, "sp": nc.sync}
        self.sem = {e: nc.alloc_semaphore("s_" + e) for e in ("pe", "act", "dve", "pool")}
        self.cnt = {e: 0 for e in self.sem}
        self.cur = {e: _Tok(self.sem[e], None, e) for e in self.sem}
        self.dsem = [nc.alloc_semaphore("d%d" % i) for i in range(self.NDS)]
        self.dcnt = [0] * self.NDS
        self.dnext = 0
        self.waited = {e: {} for e in self.eng}
        self.lastw = {}
        self.readers = {}
        self.nwait = 0
        self.nins = 0

    def _wait(self, e, tok):
        if tok is None:
            return
        if tok.val is None:
            if tok.eng == e:
                return
            raise RuntimeError("wait on unresolved token of %s from %s" % (tok.eng, e))
        w = self.waited[e]
        sid = id(tok.sem)
        if w.get(sid, 0) >= tok.val:
            return
        self.eng[e].wait_ge(tok.sem, tok.val)
        self.nwait += 1
        w[sid] = tok.val

    def _deps(self, e, reads, writes):
        toks = []
        for k in reads:
            toks.append(self.lastw.get(k))
        for k in writes:
            toks.append(self.lastw.get(k))
            toks.extend(self.readers.get(k, {}).values())
        best = {}
        for t in toks:
            if t is None:
                continue
            if t.eng == "pe" and e == "pe":
                continue
            if t.eng == e and t.val is not None and t.val <= self.cnt[e] - 3:
                continue
            if t.val is None:
                self._wait(e, t)
                continue
            sid = id(t.sem)
            if sid not in best or best[sid].val < t.val:
                best[sid] = t
        for t in best.values():
            self._wait(e, t)

    def _record(self, tok, reads, writes):
        for k in reads:
            self.readers.setdefault(k, {})[id(tok.sem)] = tok
        for k in writes:
            self.lastw[k] = tok
            self.readers[k] = {}

    def op(self, e, fn, reads=(), writes=(), signal=True):
        self._deps(e, reads, writes)
        ins = fn(self.eng[e])
        self.nins += 1
        tok = self.cur[e]
        self._record(tok, reads, writes)
        if signal:
            ins.then_inc(self.sem[e], 1)
            self.cnt[e] += 1
            tok.val = self.cnt[e]
            self.cur[e] = _Tok(self.sem[e], None, e)
        return ins

    def dma(self, q, out, in_, reads=(), writes=(), **kw):
        self._deps(q, reads, writes)
        i = self.dnext
        self.dnext = (i + 1) % self.NDS
        if self.dcnt[i]:
            self._wait(q, _Tok(self.dsem[i], self.dcnt[i], "dma"))
        ins = self.eng[q].dma_start(out=out, in_=in_, **kw)
        self.dcnt[i] += 16
        ins.then_inc(self.dsem[i], 16)
        self.nins += 1
        self._record(_Tok(self.dsem[i], self.dcnt[i], "dma"), reads, writes)
        return ins

    def sync_all(self, engines):
        for e in engines:
            for i in range(self.NDS):
                if self.dcnt[i]:
                    self._wait(e, _Tok(self.dsem[i], self.dcnt[i], "dma"))
            for x in self.sem:
                if x != e and self.cnt[x]:
                    assert self.cur[x].val is None
                    self._wait(e, _Tok(self.sem[x], self.cnt[x], x))

    def barrier(self):
        self.sync_all(["pe", "act", "dve", "pool", "sp"])
        self.lastw = {}
        self.readers = {}


def build():
    nc = bass.Bass("TRN2", target_bir_lowering=False)
    S = Sched(nc)

    def din(name, shape):
        return nc.dram_tensor(name, list(shape), F32, kind="ExternalInput").ap()

    def dout(name, shape):
        return nc.dram_tensor(name, list(shape), F32, kind="ExternalOutput").ap()

    def dscr(name, shape, dt=BF16):
        return nc.dram_tensor(name, list(shape), dt).ap()

    xs = din("xs", [NSL * 128, D])
    kvalid = din("kvalid", [128, NSL])
    ck = din("ck", [2, 2048, 1024])
    cv = din("cv", [2, 2048, 1024])
    sp_in = din("sp_in", [2, 15, 1024])
    sc_in = din("sc_in", [2, 2, DFF])
    invcnt = din("invcnt", [1, 64])
    hvalid = din("hvalid", [128, 1])
    ident = din("ident", [128, 128])
    g_attn = din("g_attn", [1, D])
    g_ffn = din("g_ffn", [1, D])
    gq_t = din("gq_t", [1, 1024])
    gk_t = din("gk_t", [1, 1024])
    sg_t = din("sg_t", [1, 1024])
    lq1 = din("lq1", [1, 64])
    lk1 = din("lk1", [1, 64])
    lq2 = din("lq2", [1, 64])
    lk2 = din("lk2", [1, 64])
    psc_t = din("psc_t", [128, 8])
    cw_t = din("cw_t", [128, KF * 3])
    cb_t = din("cb_t", [128, KF])
    w_in = din("w_in", [D, 8192])
    w_pool = din("w_pool", [1024, 256])
    w_ua = din("w_ua", [1024, D])
    w_up = din("w_up", [1024, D])
    w_out = din("w_out", [D, D])
    w_fi = din("w_fi", [D, 2 * DFF])
    w_fo = din("w_fo", [DFF, D])

    y_p = dout("y_p", [2048, D])
    y_s = dout("y_s", [2, 16, D])
    k_p = dout("k_p", [2048, 1024])
    v_p = dout("v_p", [2048, 1024])
    pool_p = dout("pool_p", [15, 1024])
    conv_p = dout("conv_p", [2, DFF])
    k_s = dout("k_s", [2, 16, 1024])
    v_s = dout("v_s", [2, 16, 1024])
    pool_s = dout("pool_s", [2, 15, 1024])
    conv_s = dout("conv_s", [2, 2, DFF])

    wbp_in = dscr("wbp_in", [16, 128, 16, 512])
    wb_pool = dscr("wb_pool", [1024, 256])
    wbp_ua = dscr("wbp_ua", [4, 128, 8, 512])
    wbp_up = dscr("wbp_up", [4, 128, 8, 512])
    wbp_out = dscr("wbp_out", [4, 128, 16, 512])
    wbp_fz = dscr("wbp_fz", [11, 128, 16, 512])
    wbp_fv = dscr("wbp_fv", [11, 128, 16, 512])
    wbp_fo = dscr("wbp_fo", [4, 128, KF, 512])
    KTs = dscr("KTs", [H, 128, NSL * 128])
    VXs = dscr("VXs", [H, 128, NSL, 130])
    KTc = dscr("KTc", [2, H, 128, 2048])
    VXc = dscr("VXc", [2, H, 128, 16, 130])

    pk = nc.alloc_psum_tensor("pk", [128, 1024], F32)
    pv = nc.alloc_psum_tensor("pv", [128, 1024], F32)
    pq = nc.alloc_psum_tensor("pq", [128, 1024], F32)
    pbf0 = nc.alloc_psum_tensor("pbf0", [128, 8, 128], BF16)
    pbf1 = nc.alloc_psum_tensor("pbf1", [128, 8, 128], BF16)
    B = [pk[:, 0:512], pk[:, 512:1024], pv[:, 0:512], pv[:, 512:1024], pq[:, 0:512], pq[:, 512:1024]]
    BK = ["B0", "B1", "B2", "B3", "B4", "B5"]
    pbf0f = pbf0[:].rearrange("p a b -> p (a b)").bitcast(F32)
    pbf1f = pbf1[:].rearrange("p a b -> p (a b)").bitcast(F32)
    STB = [(pk, ("B0", "B1")), (pv, ("B2", "B3")), (pq, ("B4", "B5"))]
    ACC = [(pbf0f[:, 0:256], "pbf0"), (pbf0f[:, 256:512], "pbf0"), (pbf1f[:, 0:256], "pbf1"), (pbf1f[:, 256:512], "pbf1")]
    BSET = [[(B[0], "B0"), (B[1], "B1"), (B[2], "B2"), (B[3], "B3")],
            [(B[4], "B4"), (B[5], "B5"), (pbf0f, "pbf0"), (pbf1f, "pbf1")]]
    ZR = [(B[0], "B0"), (B[1], "B1"), (B[4], "B4"), (B[5], "B5")]
    VR = [(B[2], "B2"), (B[3], "B3"), (pbf0f, "pbf0"), (pbf1f, "pbf1")]

    def A(fn, r, w):
        return S.op("act", fn, r, w)

    def V(fn, r, w):
        return S.op("dve", fn, r, w)

    def G(fn, r, w):
        return S.op("pool", fn, r, w)

    def P(fn, r, w, sig=True):
        return S.op("pe", fn, r, w, signal=sig)

    casts = []

    def add_cast(dst, src, rows, c_lo, c_hi):
        for r0 in range(0, rows, 128):
            r1 = min(rows, r0 + 128)
            for c0 in range(c_lo, c_hi, 2048):
                c1 = min(c_hi, c0 + 2048)
                casts.append((dst[r0:r1, c0:c1], src[r0:r1, c0:c1]))

    def add_cast_p(dst4, src2, rows, c_lo, c_hi, piece0):
        for kt in range(rows // 128):
            r0 = kt * 128
            c0 = c_lo
            while c0 < c_hi:
                c1 = min(c_hi, c0 + 2048)
                nfull = (c1 - c0) // 512
                pj = piece0 + (c0 - c_lo) // 512
                if nfull:
                    casts.append((dst4[pj:pj + nfull, :, kt, :].rearrange("j p c -> p j c"),
                                  src2[r0:r0 + 128, c0:c0 + nfull * 512].rearrange("p (j c) -> p j c", c=512)))
                rem = (c1 - c0) - nfull * 512
                if rem:
                    casts.append((dst4[pj + nfull, :, kt, 0:rem], src2[r0:r0 + 128, c0 + nfull * 512:c1]))
                c0 = c1

    add_cast_p(wbp_in, w_in, D, 0, 1024, 0)
    add_cast_p(wbp_in, w_in, D, 3072, 8192, 6)
    add_cast(wb_pool, w_pool, 1024, 0, 256)
    add_cast_p(wbp_ua, w_ua, 1024, 0, D, 0)
    add_cast_p(wbp_up, w_up, 1024, 0, D, 0)
    add_cast_p(wbp_out, w_out, D, 0, D, 0)
    add_cast_p(wbp_fz, w_fi, D, 0, DFF, 0)
    add_cast_p(wbp_fv, w_fi, D, DFF, 2 * DFF, 0)
    add_cast_p(wbp_fo, w_fo, DFF, 0, D, 0)
    cast_pos = [0]

    def emit_casts(n):
        for _ in range(n):
            if cast_pos[0] < len(casts):
                d, s = casts[cast_pos[0]]
                cast_pos[0] += 1
                S.dma("pool", d, s, writes=["wcast"])

    with ExitStack() as top:
        def sb(name, shape, dt, st=top):
            return st.enter_context(nc.sbuf_tensor(name, list(shape), dt))

        idf = sb("idf", [128, 128], F32)
        idb = sb("idb", [128, 128], BF16)
        gA = sb("gA", [128, D], F32)
        kval = sb("kval", [128, NSL], F32)
        ss = sb("ss", [128, 4], F32)
        rs = sb("rs", [128, 4], F32)
        ssk = sb("ssk", [128, 16], F32)
        rk = sb("rk", [128, 16], F32)
        lamc = sb("lamc", [128, 8], F32)
        S.dma("sp", idf[:], ident[:, :], writes=["idf"])
        V(lambda e: e.tensor_copy(idb[:], idf[:]), ["idf"], ["idb"])
        S.dma("sp", gA[:], g_attn.partition_broadcast(128), writes=["gA"])
        S.dma("sp", kval[:], kvalid[:, :], writes=["kval"])

        def rstd_cols(src_ap, n, width, junk_ap, col, rkeys, jkey):
            A(lambda e: e.activation(out=junk_ap, in_=src_ap, func=AF.Square, accum_out=ss[0:n, col:col + 1]),
              rkeys, [jkey, "ss%d" % col])
            A(lambda e: e.activation(out=rs[0:n, col:col + 1], in_=ss[0:n, col:col + 1], func=AF.Ln,
                                     scale=1.0 / width, bias=EPS), ["ss%d" % col], ["rs%d" % col])
            A(lambda e: e.activation(out=rs[0:n, col:col + 1], in_=rs[0:n, col:col + 1], func=AF.Exp,
                                     scale=-0.5), ["rs%d" % col], ["rs%d" % col])

        def headnorm(raw, n, sqbuf, outf, gtile, rkey, sqkey, okey, gkey):
            A(lambda e: e.activation(out=sqbuf[0:n, 0:1024], in_=raw, func=AF.Square), [rkey], [sqkey])
            V(lambda e: e.reduce_sum(out=ssk[0:n, :], in_=sqbuf[0:n, 0:1024].rearrange("p (g d) -> p g d", d=64),
                                     axis=AX.X), [sqkey], ["ssk"])
            A(lambda e: e.activation(out=rk[0:n, :], in_=ssk[0:n, :], func=AF.Ln, scale=1.0 / 64, bias=EPS),
              ["ssk"], ["rk"])
            A(lambda e: e.activation(out=rk[0:n, :], in_=rk[0:n, :], func=AF.Exp, scale=-0.5), ["rk"], ["rk"])
            V(lambda e: e.tensor_tensor(out=outf.rearrange("p (g d) -> p g d", d=64),
                                        in0=raw.rearrange("p (g d) -> p g d", d=64),
                                        in1=rk[0:n, :].unsqueeze(2).to_broadcast([n, 16, 64]), op=ALU.mult),
              [rkey, "rk"], [okey])
            V(lambda e: e.tensor_tensor(out=outf, in0=outf, in1=gtile[0:n, :], op=ALU.mult), [okey, gkey], [okey])

        with ExitStack() as p1:
            def sb1(name, shape, dt):
                return sb(name, shape, dt, p1)

            wkv = sb1("wkv", [128, 16, 2048], BF16)
            gk = sb1("gk", [128, 1024], F32)
            xt = [sb1("xt%d" % i, [128, D], F32) for i in range(3)]
            tmp = sb1("tmp", [128, D], F32)
            hb = [sb1("hb%d" % i, [128, D], BF16) for i in range(2)]
            hT = [sb1("hT%d" % i, [128, 16, 128], BF16) for i in range(2)]
            kraw = sb1("kraw", [128, 1024], F32)
            kf = [sb1("kf%d" % i, [128, 1024], F32) for i in range(2)]
            kb = sb1("kb", [128, 1024], BF16)
            vf = [sb1("vf%d" % i, [128, 1024], F32) for i in range(2)]
            vx = [sb1("vx%d" % i, [128, 8, 130], BF16) for i in range(2)]
            ktt = [sb1("ktt%d" % i, [128, 8, 128], BF16) for i in range(2)]

            S.dma("sp", gk[:], gk_t.partition_broadcast(128), writes=["gk"])
            for b_ in range(2):
                V(lambda e, b_=b_: e.memset(vx[b_][:], 0.0), [], ["vx%d" % b_])
            for kt in range(16):
                S.dma("pool", wkv[:, kt, :], w_in[kt * 128:(kt + 1) * 128, 1024:3072], writes=["wkv"])

            def ktrans(b, dst_ap):
                for hh in range(H):
                    P(lambda e, hh=hh: e.transpose(pbf1[:, hh, :], kb[:, hh * 128:(hh + 1) * 128], idb[:]),
                      ["kb", "idb"], ["pbf1"], sig=(hh == H - 1))
                V(lambda e: e.tensor_copy(ktt[b][:], pbf1[:]), ["pbf1"], ["ktt%d" % b])
                S.dma("sp", dst_ap, ktt[b][:], reads=["ktt%d" % b], writes=["KTs"])

            sqb = sb1("sqb", [128, 1024], F32)

            def tiles_iter():
                return [t for t in range(NSL) if not (t >= KTILES and t < NSL - 2)]

            def xload(s):
                b3 = s % 3
                S.dma("sp", xt[b3][:], xs[s * 128:(s + 1) * 128, :], writes=["xt%d" % b3])

            def hchain(s):
                b = s % 2
                b3 = s % 3
                xk, hk = "xt%d" % b3, "hb%d" % b
                rstd_cols(xt[b3][:], 128, D, hb[b][:], 0, [xk], hk)
                A(lambda e: e.activation(out=tmp[:], in_=xt[b3][:], func=AF.Copy, scale=rs[:, 0:1]),
                  [xk, "rs0"], ["tmp"])
                G(lambda e: e.tensor_tensor(out=hb[b][:], in0=tmp[:], in1=gA[:], op=ALU.mult),
                  ["tmp", "gA"], [hk])

            def hT_make(s):
                b = s % 2
                hk, hTk = "hb%d" % b, "hT%d" % b
                for half in range(2):
                    for j in range(8):
                        kt = half * 8 + j
                        P(lambda e, kt=kt, j=j: e.transpose(pbf0[:, j, :], hb[b][:, kt * 128:(kt + 1) * 128], idb[:]),
                          [hk, "idb"], ["pbf0"], sig=(j == 7))
                    V(lambda e, half=half: e.tensor_copy(hT[b][:, half * 8:(half + 1) * 8, :], pbf0[:]),
                      ["pbf0"], [hTk])

            def mm(s):
                b = s % 2
                hTk = "hT%d" % b
                for cc in range(4):
                    dst = pk if cc < 2 else pv
                    c0 = (cc % 2) * 512
                    for kt in range(16):
                        P(lambda e, kt=kt, cc=cc, dst=dst, c0=c0: e.matmul(
                            dst[:, c0:c0 + 512], lhsT=hT[b][:, kt, :], rhs=wkv[:, kt, cc * 512:(cc + 1) * 512],
                            start=(kt == 0), stop=(kt == 15)),
                          [hTk, "wkv"], ["pk" if cc < 2 else "pv"], sig=(kt == 15))

            def evac(s):
                b = s % 2
                V(lambda e: e.tensor_copy(kraw[:], pk[:]), ["pk"], ["kraw"])
                A(lambda e: e.copy(out=vf[b][:], in_=pv[:]), ["pv"], ["vf%d" % b])

            def tail(s):
                b = s % 2
                headnorm(kraw[:], 128, sqb, kf[b][:], gk, "kraw", "sqb", "kf%d" % b, "gk")
                V(lambda e: e.tensor_copy(kb[:], kf[b][:]), ["kf%d" % b], ["kb"])
                A(lambda e: e.copy(out=vx[b][:, :, 0:128], in_=vf[b][:].rearrange("p (h d) -> p h d", d=128)),
                  ["vf%d" % b], ["vx%d" % b])
                V(lambda e: e.tensor_copy(vx[b][:, :, 128:129],
                                          kval[:, s:s + 1].unsqueeze(1).to_broadcast([128, 8, 1])),
                  ["kval"], ["vx%d" % b])
                S.dma("sp", VXs[:, :, s, :].rearrange("h p e -> p h e"), vx[b][:],
                      reads=["vx%d" % b], writes=["VXs"])
                if OWN0 <= s < NPT:
                    r0 = (s - OWN0) * 128
                    S.dma("sp", k_p[r0:r0 + 128, :], kf[b][:], reads=["kf%d" % b], writes=["k_p"])
                    S.dma("sp", v_p[r0:r0 + 128, :], vf[b][:], reads=["vf%d" % b], writes=["v_p"])
                if s == NPT:
                    for j in range(2):
                        S.dma("sp", k_s[j], kf[b][32 * j:32 * j + 16, :], reads=["kf%d" % b], writes=["k_s"])
                        S.dma("sp", v_s[j], vf[b][32 * j:32 * j + 16, :], reads=["vf%d" % b], writes=["v_s"])

            tl = tiles_iter()
            xload(tl[0])
            xload(tl[1])
            hchain(tl[0])
            hT_make(tl[0])
            for idx, s in enumerate(tl):
                nxt = tl[idx + 1] if idx + 1 < len(tl) else None
                if idx + 2 < len(tl):
                    xload(tl[idx + 2])
                if nxt is not None:
                    hchain(nxt)
                emit_casts(2)
                mm(s)
                evac(s)
                if nxt is not None:
                    hT_make(nxt)
                if idx > 0:
                    sp_ = tl[idx - 1]
                    ktrans(sp_ % 2, KTs[:, :, sp_ * 128:(sp_ + 1) * 128].rearrange("h p t -> p h t"))
                tail(s)
            sp_ = tl[-1]
            ktrans(sp_ % 2, KTs[:, :, sp_ * 128:(sp_ + 1) * 128].rearrange("h p t -> p h t"))

            for j in range(2 if KSTOP >= 2 else 0):
                for t in range(16):
                    i = j * 16 + t
                    b = i % 2
                    emit_casts(2)
                    S.dma("sp", xt[b][:, 0:1024], ck[j, t * 128:(t + 1) * 128, :], writes=["xt%d" % b])
                    S.dma("sp", xt[b][:, 1024:2048], cv[j, t * 128:(t + 1) * 128, :], writes=["xt%d" % b])
                    for hh in range(H):
                        P(lambda e, hh=hh: e.transpose(pk[:, hh * 128:(hh + 1) * 128],
                                                       xt[b][:, hh * 128:(hh + 1) * 128], idf[:]),
                          ["xt%d" % b, "idf"], ["pk"], sig=(hh == H - 1))
                    V(lambda e: e.tensor_copy(ktt[b][:], pk[:].rearrange("p (h t) -> p h t", t=128)),
                      ["pk"], ["ktt%d" % b])
                    S.dma("sp", KTc[j, :, :, t * 128:(t + 1) * 128].rearrange("h p t -> p h t"), ktt[b][:],
                          reads=["ktt%d" % b], writes=["KTc"])
                    G(lambda e: e.tensor_copy(vx[b][:, :, 0:128],
                                              xt[b][:, 1024:2048].rearrange("p (h d) -> p h d", d=128)),
                      ["xt%d" % b], ["vx%d" % b])
                    V(lambda e: e.tensor_copy(vx[b][:, :, 128:129],
                                              kval[:, NPT:NPT + 1].unsqueeze(1).to_broadcast([128, 8, 1])),
                      ["kval"], ["vx%d" % b])
                    S.dma("sp", VXc[j, :, :, t, :].rearrange("h p e -> p h e"), vx[b][:],
                          reads=["vx%d" % b], writes=["VXc"])
            emit_casts(len(casts))
            S.barrier()

        with ExitStack() as p2:
            def sb2(name, shape, dt):
                return sb(name, shape, dt, p2)

            NC_ = 256
            gF = sb2("gF", [128, D], F32)
            gq = sb2("gq", [128, 1024], F32)
            sg = sb2("sg", [128, 1024], F32)
            invc = sb2("invc", [128, 4, 16], F32)
            hv = sb2("hv", [128, 1], F32)
            psc = sb2("psc", [128, 8], F32)
            cw = sb2("cw", [128, KF, 3], F32)
            cb = sb2("cb", [128, KF], F32)
            lt = [sb2("lt%d" % i, [128, 64], F32) for i in range(4)]
            wpl = sb2("wpl", [128, 8, 256], BF16)
            xg = sb2("xg", [128, 2, D], F32)
            hT = sb2("hTm", [128, 16, NC_], BF16)
            tmpf = sb2("tmpf", [128, D], F32)
            hb = sb2("hbm", [128, D], BF16)
            qraw = sb2("qraw", [128, 1024], F32)
            qf = sb2("qf", [128, 1024], F32)
            qb = sb2("qb", [128, 1024], BF16)
            RA = sb2("RA", [128, 5504], F32)
            RB = sb2("RB", [128, 7728], F32)
            ring = [sb2("ring%d" % i, [128, 8192], BF16) for i in range(3)]
            zext = sb2("zext", [128, 3, 258], F32)
            zextB = sb2("zextB", [128, 3, 258], F32)
            zcB = sb2("zcB", [128, NC_], F32)
            szlB = sb2("szlB", [128, NC_], F32)
            zc = sb2("zc", [128, NC_], F32)
            szl = sb2("szl", [128, NC_], F32)
            zh = sb2("zh", [128, KF, 2], F32)
            sch = sb2("sch", [128, KF, 2, 2], F32)
            uh = sb2("uh", [128, 8, 15], F32)
            uexs = sb2("uexs", [128, 8, 3, 32], F32)
            zsave = sb2("zsave", [128, KF, 2, 2], F32)
            l12 = sb2("l12", [128, 4], F32)
            ofin = sb2("ofin", [128, 128], F32)
            t1f = sb2("t1f", [128, 128], F32)
            stg = tmpf[:, 0:1024]
            yst = [sb2("yst%d" % i, [128, 512], F32) for i in range(2)]

            RAb = RA[:].bitcast(BF16)
            RBb = RB[:].bitcast(BF16)
            aT = RAb[:, 0:KF * NC_].rearrange("p (k n) -> p k n", n=NC_)
            QT = RAb[:, 0:16 * NC_].rearrange("p (c k n) -> p c k n", c=2, n=NC_)
            uext = RA[:, 2048:4216].rearrange("p (k n) -> p k n", n=271)
            pa = tmpf[:, 0:542].rearrange("p (k n) -> p k n", n=271)
            pb_ = tmpf[:, 542:1084].rearrange("p (k n) -> p k n", n=271)
            pT = qf[:].bitcast(BF16).rearrange("p (k n) -> p k n", n=NC_)
            pbT = qraw[:].bitcast(BF16).rearrange("p (k n) -> p k n", n=NC_)
            ob = RAb[:, 8552:8552 + 2048].rearrange("p (t f) -> p t f", f=1024)
            KTr = [RBb[:, i * 2048:(i + 1) * 2048] for i in range(3)]
            VXr = [RBb[:, 6144 + i * 2080:6144 + (i + 1) * 2080].rearrange("p (t e) -> p t e", e=130)
                   for i in range(3)]
            PTr = [RBb[:, 12384 + i * 1024:12384 + (i + 1) * 1024].rearrange("p (j c q) -> p j c q", c=2, q=256)
                   for i in range(3)]
            oT = RBb[:, 0:2048].rearrange("p (k n) -> p k n", n=NC_)
            mT = RBb[:, 2048:2048 + 4096].rearrange("p (k n) -> p k n", n=NC_)

            S.dma("sp", gF[:], g_ffn.partition_broadcast(128), writes=["gF"])
            S.dma("sp", gq[:], gq_t.partition_broadcast(128), writes=["gq"])
            S.dma("sp", sg[:], sg_t.partition_broadcast(128), writes=["sg"])
            S.dma("sp", invc[:].rearrange("p a b -> p (a b)"), invcnt.partition_broadcast(128), writes=["invc"])
            S.dma("sp", hv[:], hvalid[:, :], writes=["hv"])
            S.dma("sp", psc[:], psc_t[:, :], writes=["psc"])
            S.dma("sp", cw[:].rearrange("p a b -> p (a b)"), cw_t[:, :], writes=["cw"])
            S.dma("sp", cb[:], cb_t[:, :], writes=["cb"])
            S.dma("sp", wpl[:], wb_pool.rearrange("(k p) n -> p k n", p=128), reads=["wcast"], writes=["wpl"])
            for i, lv in enumerate((lq1, lk1, lq2, lk2)):
                S.dma("sp", lt[i][:], lv.partition_broadcast(128), writes=["lt%d" % i])
            V(lambda e: e.tensor_scalar(out=sg[:], in0=sg[:], scalar1=1.0 - LAM_INIT, scalar2=None, op0=ALU.mult),
              ["sg"], ["sg"])
            for i in range(2):
                V(lambda e, i=i: e.tensor_tensor(out=lt[2 * i][:], in0=lt[2 * i][:], in1=lt[2 * i + 1][:], op=ALU.mult),
                  ["lt%d" % (2 * i), "lt%d" % (2 * i + 1)], ["lt%d" % (2 * i)])
                V(lambda e, i=i: e.reduce_sum(out=lamc[:, i:i + 1], in_=lt[2 * i][:], axis=AX.X),
                  ["lt%d" % (2 * i)], ["lamc"])
            A(lambda e: e.activation(out=lamc[:, 0:2], in_=lamc[:, 0:2], func=AF.Exp), ["lamc"], ["lamc"])
            V(lambda e: e.tensor_tensor(out=lamc[:, 2:3], in0=lamc[:, 1:2], in1=lamc[:, 0:1], op=ALU.subtract),
              ["lamc"], ["lamc"])
            V(lambda e: e.tensor_scalar(out=lamc[:, 2:3], in0=lamc[:, 2:3], scalar1=-LAM_INIT, scalar2=None,
                                        op0=ALU.add), ["lamc"], ["lamc"])
            G(lambda e: e.memset(uh[:], 0.0), [], ["uh"])
            G(lambda e: e.memset(zh[:], 0.0), [], ["zh"])
            G(lambda e: e.memset(uexs[:], 0.0), [], ["uexs"])
            G(lambda e: e.memset(zext[:], 0.0), [], ["zext"])
            G(lambda e: e.memset(zextB[:], 0.0), [], ["zextB"])

            ring_i = [0]

            def wload(src_ap, kt_n, ncols):
                i = ring_i[0] % 3
                ring_i[0] += 1
                view = ring[i][:, 0:kt_n * ncols].rearrange("p (k n) -> p k n", n=ncols)
                S.dma("sp", view, src_ap, reads=["wcast"], writes=["ring%d" % i])
                return view, "ring%d" % i

            def wsrc(wb, r0, kt_n, c0, ncols):
                return wb[r0:r0 + kt_n * 128, c0:c0 + ncols].rearrange("(k p) n -> p k n", p=128)

            bank_i = {}

            def nbank(lo=0, n=4):
                c = bank_i.get((lo, n), 0)
                bank_i[(lo, n)] = c + 1
                i = lo + c % n
                return B[i], BK[i]

            def make_hT(toks, gtile, gkey, src_of):
                for ti, (nr, col0) in enumerate(toks):
                    src, skey = src_of(ti)
                    rstd_cols(src, nr, D, hb[0:nr, :], 0, [skey], "hbm")
                    A(lambda e: e.activation(out=tmpf[0:nr, :], in_=src, func=AF.Copy, scale=rs[0:nr, 0:1]),
                      [skey, "rs0"], ["tmpf"])
                    V(lambda e: e.tensor_tensor(out=hb[0:nr, :], in0=tmpf[0:nr, :], in1=gtile[0:nr, :], op=ALU.mult),
                      ["tmpf", gkey], ["hbm"])
                    for half in range(2):
                        for j in range(8):
                            kt = half * 8 + j
                            P(lambda e, kt=kt, j=j: e.transpose(pbf0[:, j, 0:nr], hb[0:nr, kt * 128:(kt + 1) * 128],
                                                                idb[0:nr, 0:nr]),
                              ["hbm", "idb"], ["pbf0"], sig=(j == 7))
                        V(lambda e, half=half: e.tensor_copy(hT[:, half * 8:(half + 1) * 8, col0:col0 + nr],
                                                             pbf0[:, :, 0:nr]), ["pbf0"], ["hT"])

            def finalize(acc0, acc1, pb0, nq, ti, hh, k0, k1):
                sl = slice(pb0, pb0 + nq)
                if KATT in ("st", "st0", "st0c0", "av", "nomask"):
                    return
                V(lambda e: e.tensor_copy(l12[sl, 0:1], acc0[:, 128:129]), [k0], ["l12"])
                V(lambda e: e.tensor_copy(l12[sl, 1:2], acc1[:, 128:129]), [k1], ["l12"])
                V(lambda e: e.tensor_scalar(out=l12[sl, 0:2], in0=l12[sl, 0:2], scalar1=1e-30, scalar2=None,
                                            op0=ALU.add), ["l12"], ["l12"])
                V(lambda e: e.reciprocal(l12[sl, 2:4], l12[sl, 0:2]), ["l12"], ["l12"])
                V(lambda e: e.tensor_tensor(out=l12[sl, 3:4], in0=l12[sl, 3:4], in1=lamc[sl, 2:3], op=ALU.mult),
                  ["l12", "lamc"], ["l12"])
                V(lambda e: e.tensor_scalar(out=t1f[sl, :], in0=acc0[:, 0:128], scalar1=l12[sl, 2:3], scalar2=None,
                                            op0=ALU.mult), [k0, "l12"], ["t1f"])
                V(lambda e: e.scalar_tensor_tensor(out=ofin[sl, :], in0=acc1[:, 0:128], scalar=l12[sl, 3:4],
                                                   in1=t1f[sl, :], op0=ALU.mult, op1=ALU.add),
                  [k1, "l12", "t1f"], ["ofin"])
                A(lambda e: e.activation(out=t1f[sl, :], in_=ofin[sl, :], func=AF.Square,
                                         accum_out=ss[sl, 1:2]), ["ofin"], ["t1f", "ss1"])
                A(lambda e: e.activation(out=rs[sl, 1:2], in_=ss[sl, 1:2], func=AF.Ln, scale=1.0 / 128, bias=EPS),
                  ["ss1"], ["rs1"])
                A(lambda e: e.activation(out=rs[sl, 1:2], in_=rs[sl, 1:2], func=AF.Exp, scale=-0.5),
                  ["rs1"], ["rs1"])
                A(lambda e: e.activation(out=t1f[sl, :], in_=ofin[sl, :], func=AF.Copy, scale=rs[sl, 1:2]),
                  ["ofin", "rs1"], ["t1f"])
                G(lambda e: e.tensor_tensor(out=ob[sl, ti, hh * 128:(hh + 1) * 128], in0=t1f[sl, :],
                                            in1=sg[sl, hh * 128:(hh + 1) * 128], op=ALU.mult),
                  ["t1f", "sg"], ["ob"])

            kv_i = [0]

            def kv_load(kt_src, vx_src, ntile, krows=128, kbase=0):
                i = kv_i[0] % 3
                kv_i[0] += 1
                if kt_src is not None:
                    S.dma("sp", KTr[i][:, 0:kt_src.shape[-1]], kt_src, reads=["KTs", "KTc"], writes=["KTr%d" % i])
                S.dma("sp", VXr[i][kbase:kbase + krows, 0:ntile, :], vx_src, reads=["VXs", "VXc"],
                      writes=["VXr%d" % i])
                return i

            pt_i = [0]

            def key_step(hh, tiles, nk, kb0, qlo, qhi, accs_per_tile, masks, stb):
                stt, stkeys = STB[stb]
                pi = pt_i[0] % 3
                pt_i[0] += 1
                PT = PTr[pi]
                ptk = "PT%d" % pi
                nt = len(tiles)
                for j, (i, tt) in enumerate(tiles):
                    for c in range(2):
                        o0 = j * 512 + c * 256
                        P(lambda e, c=c, i=i, tt=tt, o0=o0: e.matmul(
                            stt[kb0:kb0 + nk, o0 + qlo:o0 + qhi], lhsT=KTr[i][:, tt * 128:tt * 128 + nk],
                            rhs=QT[:, c, hh, qlo:qhi], start=True, stop=True),
                          ["KTr%d" % i, "QT"], [stkeys[j]], sig=(c == 1))
                if qlo == 0 and qhi == 256:
                    src = stt[kb0:kb0 + nk, 0:nt * 512]
                    dst = PT[kb0:kb0 + nk, 0:nt, :, :].rearrange("p j c q -> p (j c q)")
                else:
                    src = stt[kb0:kb0 + nk, 0:nt * 512].rearrange("p (j c q) -> p j c q", c=2, q=256)[:, :, :, qlo:qhi]
                    dst = PT[kb0:kb0 + nk, 0:nt, :, qlo:qhi]
                A(lambda e: e.activation(out=dst, in_=src, func=AF.Exp, scale=0.125),
                  [stkeys[j] for j in range(nt)], [ptk])
                for (q0,) in masks:
                    V(lambda e, q0=q0: e.memset(PT[64:128, 0, :, q0:q0 + 64], 0.0), [], [ptk])

                def av():
                    for j, (i, tt) in enumerate(tiles):
                        accs = accs_per_tile[j]
                        for ai, (c, acc, akey, qc0, qc1, fi, la) in enumerate(accs):
                            P(lambda e, c=c, acc=acc, qc0=qc0, qc1=qc1, fi=fi, la=la, i=i, tt=tt, j=j: e.matmul(
                                acc, lhsT=PT[kb0:kb0 + nk, j, c, qc0:qc1], rhs=VXr[i][kb0:kb0 + nk, tt, 0:129],
                                start=fi, stop=la),
                              [ptk, "VXr%d" % i], [akey], sig=(j == nt - 1 and ai == len(accs) - 1))
                return av

            pend_av = []

            def pipe_push(av):
                if av is not None:
                    pend_av.append(av)
                while len(pend_av) > 2:
                    pend_av.pop(0)()

            def pipe_flush():
                while pend_av:
                    pend_av.pop(0)()

            def attend_main(gi):
                j0 = 2 * gi
                T = OWN0 + j0 + 2
                nch = (T + 15) // 16
                TP = OWN0 + j0
                for hh in range(H):
                    loaded = {}

                    def ld(ch):
                        t0 = ch * 16
                        nt = min(16, T - t0)
                        loaded[ch] = kv_load(KTs[hh, :, t0 * 128:(t0 + nt) * 128], VXs[hh, :, t0:t0 + nt, :], nt)
                    ld(0)
                    if nch > 1:
                        ld(1)

                    def accs_for(t):
                        full = t <= OWN0 + j0
                        accs = []
                        for c in range(2):
                            for jj in range(2):
                                if jj == 0 and not full:
                                    continue
                                last_t = OWN0 + j0 + jj
                                accs.append((c, ACC[2 * c + jj][0][:, 0:129], ACC[2 * c + jj][1], jj * 128,
                                             (jj + 1) * 128, t == 0 and jj == 0, t == last_t))
                        return accs
                    step = 0
                    t = 0
                    while t < T:
                        ch, tt = divmod(t, 16)
                        if tt == 4 and ch + 2 < nch:
                            ld(ch + 2)
                        if t < TP:
                            i = loaded[ch]
                            pipe_push(key_step(hh, [(i, tt), (i, tt + 1)], 128, 0, 0, 256,
                                               [accs_for(t), accs_for(t + 1)], [], step % 3))
                            t += 2
                        else:
                            i = loaded[ch]
                            full = t <= OWN0 + j0
                            masks = [(0,)] if t == OWN0 + j0 else [(128,)]
                            pipe_push(key_step(hh, [(i, tt)], 128, 0, 0 if full else 128, 256,
                                               [accs_for(t)], masks, step % 3))
                            t += 1
                        step += 1
                    pipe_flush()
                    for jj in range(2):
                        finalize(ACC[jj][0][:, 0:129], ACC[2 + jj][0][:, 0:129], 0, 128, jj, hh, ACC[jj][1],
                                 ACC[2 + jj][1])

            def attend_small():
                for hh in range(H):
                    for j in range(2):
                        i = kv_load(KTc[j, hh, :, :], VXc[j, hh, :, :, :], 16)
                        pb0 = 32 * j
                        a0 = ACC[0][0][pb0:pb0 + 16, 0:129]
                        a1 = ACC[2][0][pb0:pb0 + 16, 0:129]
                        for t in range(0, 16, 2):
                            accs = [[(0, a0, ACC[0][1], pb0, pb0 + 16, tq == 0, False),
                                     (1, a1, ACC[2][1], pb0, pb0 + 16, tq == 0, False)] for tq in (t, t + 1)]
                            pipe_push(key_step(hh, [(i, t), (i, t + 1)], 128, 0, pb0, pb0 + 16, accs, [],
                                               (t // 2) % 3))
                        pipe_flush()
                        i2 = kv_load(KTs[hh, :, NPT * 128 + pb0:NPT * 128 + pb0 + 16],
                                     VXs[hh, pb0:pb0 + 16, NPT:NPT + 1, :], 1, krows=16, kbase=pb0)
                        accs = [[(0, a0, ACC[0][1], pb0, pb0 + 16, False, True),
                                 (1, a1, ACC[2][1], pb0, pb0 + 16, False, True)]]
                        pipe_push(key_step(hh, [(i2, 0)], 16, pb0, pb0, pb0 + 16, accs, [], 0))
                        pipe_flush()
                        finalize(a0, a1, pb0, 16, 0, hh, ACC[0][1], ACC[2][1])
                    a0 = ACC[0][0][64:81, 0:129]
                    a1 = ACC[2][0][64:81, 0:129]
                    nch = OWN0 // 16
                    step = 0
                    for ch in range(nch):
                        i = kv_load(KTs[hh, :, ch * 2048:(ch + 1) * 2048], VXs[hh, :, ch * 16:(ch + 1) * 16, :], 16)
                        for tt in range(0, 16, 2):
                            t = ch * 16 + tt
                            accs = [[(0, a0, ACC[0][1], 64, 81, tq == 0, tq == OWN0 - 1),
                                     (1, a1, ACC[2][1], 64, 81, tq == 0, tq == OWN0 - 1)] for tq in (t, t + 1)]
                            pipe_push(key_step(hh, [(i, tt), (i, tt + 1)], 128, 0, 64, 81, accs, [], step % 3))
                            step += 1
                        pipe_flush()
                    finalize(a0, a1, 64, 17, 0, hh, ACC[0][1], ACC[2][1])

            def do_group(kind, gi):
                small = kind == "small"
                if small:
                    n = SM
                    toks = [(SM, 0)]
                    rows0 = [NPT * 128]
                    segs = [(0, 16), (32, 16), (64, 17)]
                else:
                    n = 256
                    toks = [(128, 0), (128, 128)]
                    rows0 = [(OWN0 + 2 * gi) * 128, (OWN0 + 2 * gi + 1) * 128]
                    segs = [(0, 256)]
                for ti, (nr, col0) in enumerate(toks):
                    S.dma("sp", xg[0:nr, ti, :], xs[rows0[ti]:rows0[ti] + nr, :], writes=["xg%d" % ti])
                make_hT(toks, gA, "gA", lambda ti: (xg[0:toks[ti][0], ti, :], "xg%d" % ti))
                V(lambda e: e.memset(QT[64:128, 0, :, :], 0.0), [], ["QT"])
                V(lambda e: e.memset(QT[0:64, 1, :, :], 0.0), [], ["QT"])
                wq = [wload(wbp_in[cc], 16, 512) for cc in range(2)]
                for ti, (nr, col0) in enumerate(toks):
                    for cc in range(2):
                        wv, wk = wq[cc]
                        bk, bkk = nbank()
                        for kt in range(16):
                            P(lambda e, kt=kt, wv=wv, bk=bk: e.matmul(
                                bk[0:nr, :], lhsT=hT[:, kt, col0:col0 + nr], rhs=wv[:, kt, :],
                                start=(kt == 0), stop=(kt == 15)), ["hT", wk], [bkk], sig=(kt == 15))
                        V(lambda e, bk=bk, cc=cc: e.tensor_copy(qraw[0:nr, cc * 512:(cc + 1) * 512], bk[0:nr, :]),
                          [bkk], ["qraw"])
                    headnorm(qraw[0:nr, :], nr, tmpf, qf[0:nr, :], gq, "qraw", "tmpf", "qf", "gq")
                    G(lambda e: e.tensor_copy(qb[0:nr, :], qf[0:nr, :]), ["qf"], ["qb"])
                    for hh in range(H):
                        P(lambda e, hh=hh: e.transpose(pbf1[:, hh, 0:nr], qb[0:nr, hh * 128:(hh + 1) * 128],
                                                       idb[0:nr, 0:nr]), ["qb", "idb"], ["pbf1"], sig=(hh == H - 1))
                    V(lambda e: e.tensor_copy(QT[0:64, 0, :, col0:col0 + nr], pbf1[0:64, :, 0:nr]), ["pbf1"], ["QT"])
                    V(lambda e: e.tensor_copy(QT[64:128, 1, :, col0:col0 + nr], pbf1[64:128, :, 0:nr]),
                      ["pbf1"], ["QT"])
                for pc in range(2):
                    wv, wk = wload(wbp_in[6 + pc], 16, 512)
                    for mi in range(4):
                        m = pc * 4 + mi
                        bk, bkk = nbank()
                        for kt in range(16):
                            P(lambda e, kt=kt, wv=wv, bk=bk, mi=mi: e.matmul(
                                bk[:, 0:n], lhsT=wv[:, kt, mi * 128:(mi + 1) * 128], rhs=hT[:, kt, 0:n],
                                start=(kt == 0), stop=(kt == 15)), ["hT", wk], [bkk], sig=(kt == 15))
                        if small:
                            for si, (c0, ln) in enumerate(segs):
                                V(lambda e, bk=bk, m=m, si=si, c0=c0, ln=ln: e.tensor_copy(
                                    uexs[:, m, si, 15:15 + ln], bk[:, c0:c0 + ln]), [bkk], ["uexs"])
                        else:
                            V(lambda e, bk=bk, m=m: e.tensor_copy(uext[:, m, 15:15 + n], bk[:, 0:n]), [bkk], ["uext"])
                if small:
                    for j in range(2):
                        V(lambda e: e.memset(stg[0:16, :], 0.0), [], ["tmpf"])
                        S.dma("sp", stg[0:15, :], sp_in[j], writes=["tmpf"])
                        for m in range(8):
                            P(lambda e, m=m: e.transpose(pk[:, m * 16:m * 16 + 16], stg[0:16, m * 128:(m + 1) * 128],
                                                         idf[0:16, 0:16]), ["tmpf", "idf"], ["B0", "B1"], sig=(m == 7))
                        V(lambda e, j=j: e.tensor_copy(uexs[:, :, j, 0:15],
                                                       pk[:, 0:128].rearrange("p (m t) -> p m t", t=16)[:, :, 0:15]),
                          ["B0", "B1"], ["uexs"])
                else:
                    V(lambda e: e.tensor_copy(uext[:, :, 0:15], uh[:]), ["uh"], ["uext"])
                if small:
                    views = [(uexs[:, :, si, :], 15 + ln, c0, ln) for si, (c0, ln) in enumerate(segs)]
                else:
                    views = [(uext, 15 + n, 0, n)]
                for (uv, L, c0, ln) in views:
                    for wi, wdw in enumerate((2, 4, 8, 16)):
                        src = uv[:, 2 * wi:2 * wi + 2, :]
                        sh = 1
                        cur_src = src
                        bufs = [pa, pb_]
                        bi = 0
                        for step in range(wi + 1):
                            dst = bufs[bi][:, :, 0:L]
                            V(lambda e, dst=dst, cur_src=cur_src, sh=sh: e.tensor_tensor(
                                out=dst[:, :, sh:L], in0=cur_src[:, :, sh:L], in1=cur_src[:, :, 0:L - sh], op=ALU.add),
                              ["uext", "uexs", "tmpf"], ["tmpf"])
                            cur_src = dst
                            sh *= 2
                            bi ^= 1
                        dst = bufs[bi][:, :, 0:L]
                        V(lambda e, dst=dst, cur_src=cur_src, wdw=wdw: e.tensor_scalar(
                            out=dst[:, :, 15:L], in0=cur_src[:, :, 15:L], scalar1=1.0 / wdw, scalar2=None,
                            op0=ALU.mult), ["tmpf"], ["tmpf"])
                        if (not small) and gi == 0:
                            V(lambda e, dst=dst, cur_src=cur_src, wi=wi: e.tensor_tensor(
                                out=dst[:, :, 15:31], in0=cur_src[:, :, 15:31],
                                in1=invc[:, wi, :].unsqueeze(1).to_broadcast([128, 2, 16]), op=ALU.mult),
                              ["tmpf", "invc"], ["tmpf"])
                        V(lambda e, dst=dst, src=src, wi=wi, c0=c0, ln=ln, L=L: e.tensor_tensor(
                            out=pT[:, 2 * wi:2 * wi + 2, c0:c0 + ln], in0=dst[:, :, 15:L], in1=src[:, :, 15:L],
                            op=ALU.subtract), ["tmpf", "uext", "uexs"], ["qf"])
                if small:
                    V(lambda e: e.tensor_copy(uh[:], uexs[:, :, 2, 17:32]), ["uexs"], ["uh"])
                else:
                    V(lambda e: e.tensor_copy(uh[:], uext[:, :, 256:271]), ["uext"], ["uh"])

                def emit_pool_state(src_fn, dst_ap):
                    for m in range(8):
                        P(lambda e, m=m: e.transpose(pk[0:15, m * 128:(m + 1) * 128], src_fn(m), idf[:]),
                          ["uexs", "uh", "idf"], ["B0", "B1"], sig=(m == 7))
                    V(lambda e: e.tensor_copy(stg[0:15, :], pk[0:15, :]), ["B0", "B1"], ["tmpf"])
                    S.dma("sp", dst_ap, stg[0:15, :], reads=["tmpf"], writes=["pool_o"])
                if small:
                    for j in range(2):
                        emit_pool_state(lambda m, j=j: uexs[:, m, j, 16:31], pool_s[j])
                elif gi == NG - 1:
                    emit_pool_state(lambda m: uh[:, m, :], pool_p[:, :])
                if KSUB < 3.1:
                    return
                if small:
                    V(lambda e: e.memset(ob[:, 0, :], 0.0), [], ["ob"])
                    attend_small()
                else:
                    attend_main(gi)
                if KSUB < 3.2:
                    return
                S.barrier()
                for ti, (nr, col0) in enumerate(toks):
                    for hh in range(H):
                        P(lambda e, hh=hh: e.transpose(pbf1[:, hh, 0:nr], ob[0:nr, ti, hh * 128:(hh + 1) * 128],
                                                       idb[0:nr, 0:nr]), ["ob", "idb"], ["pbf1"], sig=(hh == H - 1))
                    V(lambda e: e.tensor_copy(oT[:, :, col0:col0 + nr], pbf1[:, :, 0:nr]), ["pbf1"], ["oT"])
                if KSUB < 3.3:
                    return
                for wi in range(4):
                    for mo in range(2):
                        bk, bkk = nbank()
                        for ki in range(2):
                            P(lambda e, ki=ki, bk=bk, wi=wi, mo=mo: e.matmul(
                                bk[:, 0:n], lhsT=wpl[:, 2 * wi + ki, mo * 128:(mo + 1) * 128],
                                rhs=pT[:, 2 * wi + ki, 0:n], start=(ki == 0), stop=(ki == 1)),
                              ["qf", "wpl"], [bkk], sig=(ki == 1))
                        A(lambda e, bk=bk, wi=wi, mo=mo: e.activation(
                            out=pbT[:, 2 * wi + mo, 0:n], in_=bk[:, 0:n], func=AF.Copy,
                            scale=psc[:, 2 * wi + mo:2 * wi + mo + 1]), [bkk, "psc"], ["qraw"])
                if KSUB < 3.33:
                    return
                for pc in range(4):
                    wga = wload(wbp_in[8 + pc], 16, 512)
                    wgb = wload(wbp_in[12 + pc], 16, 512)
                    i = ring_i[0] % 3
                    ring_i[0] += 1
                    wua_v = ring[i][:, 0:4096].rearrange("p (k n) -> p k n", n=512)
                    wup_v = ring[i][:, 4096:8192].rearrange("p (k n) -> p k n", n=512)
                    S.dma("sp", wua_v, wbp_ua[pc], reads=["wcast"], writes=["ring%d" % i])
                    S.dma("sp", wup_v, wbp_up[pc], reads=["wcast"], writes=["ring%d" % i])
                    wuk = "ring%d" % i
                    for mi in range(4):
                        m = pc * 4 + mi
                        ms = slice(mi * 128, (mi + 1) * 128)
                        specs = [(wga[0], wga[1], hT, "hT", 16), (wua_v, wuk, oT, "oT", 8),
                                 (wgb[0], wgb[1], hT, "hT", 16), (wup_v, wuk, pbT, "qraw", 8)]
                        bs = BSET[m % 2]
                        gz, gzk = (zc, "zc") if m % 2 == 0 else (zcB, "zcB")
                        gs, gsk = (szl, "szl") if m % 2 == 0 else (szlB, "szlB")
                        for bi_, (wv, wk, act, ak, nk) in enumerate(specs):
                            for kt in range(nk):
                                P(lambda e, kt=kt, wv=wv, act=act, bi_=bi_, nk=nk: e.matmul(
                                    bs[bi_][0][:, 0:n], lhsT=wv[:, kt, ms], rhs=act[:, kt, 0:n],
                                    start=(kt == 0), stop=(kt == nk - 1)), [wk, ak], [bs[bi_][1]], sig=(kt == nk - 1))
                        A(lambda e: e.activation(out=gz[:, 0:n], in_=bs[0][0][:, 0:n], func=AF.Sigmoid), [bs[0][1]], [gzk])
                        A(lambda e: e.activation(out=gs[:, 0:n], in_=bs[2][0][:, 0:n], func=AF.Sigmoid), [bs[2][1]], [gsk])
                        V(lambda e: e.tensor_tensor(out=gz[:, 0:n], in0=gz[:, 0:n], in1=bs[1][0][:, 0:n], op=ALU.mult),
                          [gzk, bs[1][1]], [gzk])
                        V(lambda e: e.tensor_tensor(out=gs[:, 0:n], in0=gs[:, 0:n], in1=bs[3][0][:, 0:n], op=ALU.mult),
                          [gsk, bs[3][1]], [gsk])
                        G(lambda e, m=m: e.tensor_tensor(out=mT[:, m, 0:n], in0=gz[:, 0:n], in1=gs[:, 0:n],
                                                         op=ALU.add), [gzk, gsk], ["mT"])
                if KSUB < 3.4:
                    return
                for cc in range(4):
                    wv, wk = wload(wbp_out[cc], 16, 512)
                    for ti, (nr, col0) in enumerate(toks):
                        bk, bkk = nbank()
                        for kt in range(16):
                            P(lambda e, kt=kt, wv=wv, bk=bk: e.matmul(
                                bk[0:nr, :], lhsT=mT[:, kt, col0:col0 + nr], rhs=wv[:, kt, :],
                                start=(kt == 0), stop=(kt == 15)), ["mT", wk], [bkk], sig=(kt == 15))
                        V(lambda e, bk=bk, cc=cc, ti=ti: e.tensor_tensor(
                            out=xg[0:nr, ti, cc * 512:(cc + 1) * 512], in0=xg[0:nr, ti, cc * 512:(cc + 1) * 512],
                            in1=bk[0:nr, :], op=ALU.add), ["xg%d" % ti, bkk], ["xg%d" % ti])
                if KSUB < 3.5:
                    return
                S.barrier()
                make_hT(toks, gF, "gF", lambda ti: (xg[0:toks[ti][0], ti, :], "xg%d" % ti))
                if small:
                    for j in range(2):
                        for r in range(6):
                            m0 = r * 8
                            mn = min(8, KF - m0)
                            S.dma("sp", stg[0:2, 0:mn * 128], sc_in[j, :, m0 * 128:(m0 + mn) * 128], writes=["tmpf"])
                            for mm in range(mn):
                                P(lambda e, mm=mm: e.transpose(pk[:, mm * 2:mm * 2 + 2],
                                                               stg[0:2, mm * 128:(mm + 1) * 128], idf[0:2, 0:2]),
                                  ["tmpf", "idf"], ["B0", "B1"], sig=(mm == mn - 1))
                            V(lambda e, j=j, m0=m0, mn=mn: e.tensor_copy(
                                sch[:, m0:m0 + mn, j, :], pk[:, 0:mn * 2].rearrange("p (m t) -> p m t", t=2)),
                              ["B0", "B1"], ["sch"])
                for m in range(KF):
                    if m % 4 == 0:
                        mn = min(4, KF - m)
                        wz = wload(wbp_fz[m // 4][:, :, 0:mn * 128], 16, mn * 128)
                        wvv = wload(wbp_fv[m // 4][:, :, 0:mn * 128], 16, mn * 128)
                    mi = m % 4
                    ms = slice(mi * 128, (mi + 1) * 128)
                    zx, zxk = ((zext, "zext"), (zextB, "zextB"))[m % 2]
                    zcc, zck = ((zc, "zc"), (zcB, "zcB"))[m % 2]
                    szz, szk = ((szl, "szl"), (szlB, "szlB"))[m % 2]
                    bz, bzk = ZR[m % 4]
                    bv, bvk = VR[m % 4]
                    for kt in range(16):
                        P(lambda e, kt=kt, bz=bz: e.matmul(bz[:, 0:n], lhsT=wz[0][:, kt, ms], rhs=hT[:, kt, 0:n],
                                                           start=(kt == 0), stop=(kt == 15)),
                          ["hT", wz[1]], [bzk], sig=(kt == 15))
                    for kt in range(16):
                        P(lambda e, kt=kt, bv=bv: e.matmul(bv[:, 0:n], lhsT=wvv[0][:, kt, ms], rhs=hT[:, kt, 0:n],
                                                           start=(kt == 0), stop=(kt == 15)),
                          ["hT", wvv[1]], [bvk], sig=(kt == 15))
                    for si, (c0, ln) in enumerate(segs):
                        if small:
                            if si < 2:
                                V(lambda e, si=si, m=m: e.tensor_copy(zx[:, si, 0:2], sch[:, m, si, :]),
                                  ["sch"], [zxk])
                            else:
                                V(lambda e, si=si: e.memset(zx[:, si, 0:2], 0.0), [], [zxk])
                        else:
                            G(lambda e, m=m: e.tensor_copy(zx[:, 0, 0:2], zh[:, m, :]), ["zh"], [zxk])
                        A(lambda e, si=si, c0=c0, ln=ln, bz=bz: e.copy(out=zx[:, si, 2:2 + ln], in_=bz[:, c0:c0 + ln]),
                          [bzk], [zxk])
                        V(lambda e, si=si, c0=c0, ln=ln, m=m: e.tensor_scalar(
                            out=zcc[:, c0:c0 + ln], in0=zx[:, si, 2:2 + ln], scalar1=cw[:, m, 2:3],
                            scalar2=cb[:, m:m + 1], op0=ALU.mult, op1=ALU.add), [zxk, "cw", "cb"], [zck])
                        for tap in (1, 0):
                            V(lambda e, si=si, c0=c0, ln=ln, m=m, tap=tap: e.scalar_tensor_tensor(
                                out=zcc[:, c0:c0 + ln], in0=zx[:, si, tap:tap + ln], scalar=cw[:, m, tap:tap + 1],
                                in1=zcc[:, c0:c0 + ln], op0=ALU.mult, op1=ALU.add), [zxk, "cw", zck], [zck])
                        A(lambda e, c0=c0, ln=ln: e.activation(out=szz[:, c0:c0 + ln], in_=zcc[:, c0:c0 + ln],
                                                               func=AF.Silu), [zck], [szk])
                        V(lambda e, c0=c0, ln=ln, m=m, bv=bv: e.tensor_tensor(
                            out=aT[:, m, c0:c0 + ln], in0=szz[:, c0:c0 + ln], in1=bv[:, c0:c0 + ln], op=ALU.mult),
                          [szk, bvk], ["aT"])
                        if small:
                            if si < 2:
                                V(lambda e, si=si, m=m: e.tensor_copy(zsave[:, m, si, :], zx[:, si, 16:18]),
                                  [zxk], ["zsave"])
                            else:
                                V(lambda e, m=m: e.tensor_scalar(out=zh[:, m, :], in0=zx[:, 2, 17:19],
                                                                 scalar1=hv[:, 0:1], scalar2=None, op0=ALU.mult),
                                  [zxk, "hv"], ["zh"])
                        else:
                            G(lambda e, m=m: e.tensor_copy(zh[:, m, :], zx[:, 0, 256:258]), [zxk], ["zh"])
                    if small and n > 81:
                        pass
                if small:
                    for (g0, g1) in ((16, 32), (48, 64), (81, SM)):
                        V(lambda e, g0=g0, g1=g1: e.memset(aT[:, :, g0:g1], 0.0), ["aT"], ["aT"])
                if KSUB < 3.6:
                    return
                deferred = []
                for cc in range(4):
                    accb = []
                    for ti in range(len(toks)):
                        accb.append(nbank())
                    for pcs, (k0, kn) in enumerate(((0, 16), (16, 16), (32, 11))):
                        wv, wk = wload(wbp_fo[cc][:, k0:k0 + kn, :], kn, 512)
                        for ti, (nr, col0) in enumerate(toks):
                            bk, bkk = accb[ti]
                            if pcs == 2 and ti == 0:
                                for dfn in deferred:
                                    dfn()
                                deferred = []
                            for kt in range(kn):
                                P(lambda e, kt=kt, wv=wv, bk=bk, k0=k0: e.matmul(
                                    bk[0:nr, :], lhsT=aT[:, k0 + kt, col0:col0 + nr], rhs=wv[:, kt, :],
                                    start=(k0 + kt == 0), stop=(k0 + kt == KF - 1)),
                                  ["aT", wk], [bkk], sig=(kt == kn - 1))
                    for ti, (nr, col0) in enumerate(toks):
                        bk, bkk = accb[ti]
                        yb = (cc * 2 + ti) % 2
                        V(lambda e, bk=bk, yb=yb, ti=ti, cc=cc: e.tensor_tensor(
                            out=yst[yb][0:nr, :], in0=xg[0:nr, ti, cc * 512:(cc + 1) * 512], in1=bk[0:nr, :],
                            op=ALU.add), ["xg%d" % ti, bkk], ["yst%d" % yb])
                        if small:
                            for j in range(2):
                                deferred.append(lambda j=j, cc=cc, yb=yb: S.dma(
                                    "sp", y_s[j, :, cc * 512:(cc + 1) * 512], yst[yb][32 * j:32 * j + 16, :],
                                    reads=["yst%d" % yb], writes=["y_s"]))
                        else:
                            r0 = (2 * gi + ti) * 128
                            deferred.append(lambda r0=r0, cc=cc, yb=yb: S.dma(
                                "sp", y_p[r0:r0 + 128, cc * 512:(cc + 1) * 512], yst[yb][:, :],
                                reads=["yst%d" % yb], writes=["y_p"]))
                for dfn in deferred:
                    dfn()
                S.barrier()

            def emit_conv_state(src_fn, dst_ap):
                for r in range(6):
                    m0 = r * 8
                    mn = min(8, KF - m0)
                    for mm in range(mn):
                        P(lambda e, mm=mm: e.transpose(pk[0:2, mm * 128:(mm + 1) * 128], src_fn(m0 + mm), idf[:]),
                          ["zsave", "zh", "idf"], ["B0", "B1"], sig=(mm == mn - 1))
                    V(lambda e, mn=mn: e.tensor_copy(stg[0:2, 0:mn * 128], pk[0:2, 0:mn * 128]), ["B0", "B1"], ["tmpf"])
                    S.dma("sp", dst_ap[:, m0 * 128:(m0 + mn) * 128], stg[0:2, 0:mn * 128], reads=["tmpf"],
                          writes=["conv_o"])

            if KSTOP >= 3 and KSMALL:
                do_group("small", -1)
                for j in range(2):
                    emit_conv_state(lambda m, j=j: zsave[:, m, j, :], conv_s[j])
            for gi in range(KGROUPS if KSTOP >= 4 else 0):
                do_group("main", gi)
            if KSTOP >= 4:
                emit_conv_state(lambda m: zh[:, m, :], conv_p[:, :])
            S.barrier()
    S.sync_all(["sp"])
    print("sbuf remaining", nc.sbuf_bytes_remaining)
    print("program: ins", S.nins, "waits", S.nwait, "cnt", S.cnt, flush=True)
    return nc


_NC_CACHE = {}


def kernel(_prep_only=False, **inp):
    f = lambda a: np.ascontiguousarray(np.asarray(a, dtype=np.float32))
    x_prompt = f(inp["x_prompt"])[0]
    x_sample = f(inp["x_sample"])
    cache_k = f(inp["cache_k"])[0].reshape(16, 2048, 1024)
    cache_v = f(inp["cache_v"])[0].reshape(16, 2048, 1024)
    state_pool = f(inp["state_pool"])[0]
    state_conv = f(inp["state_conv"])[0]
    shared = {
        "ident": np.eye(128, dtype=np.float32),
        "g_attn": f(inp["attn_norm_g"]).reshape(1, D),
        "g_ffn": f(inp["ffn_norm_g"]).reshape(1, D),
        "gq_t": np.tile(f(inp["q_norm_g"]).reshape(1, 128), (1, 8)),
        "gk_t": np.tile(f(inp["k_norm_g"]).reshape(1, 128), (1, 8)),
        "sg_t": np.tile(f(inp["subln_g"]).reshape(1, 128), (1, 8)),
        "lq1": f(inp["lambda_q1"]).reshape(1, 64), "lk1": f(inp["lambda_k1"]).reshape(1, 64),
        "lq2": f(inp["lambda_q2"]).reshape(1, 64), "lk2": f(inp["lambda_k2"]).reshape(1, 64),
        "psc_t": np.ascontiguousarray(f(inp["pool_scale"]).reshape(8, 128).T),
        "cw_t": np.ascontiguousarray(f(inp["conv_w"])[0].reshape(3, KF, 128).transpose(2, 1, 0).reshape(128, KF * 3)),
        "cb_t": np.ascontiguousarray(f(inp["conv_b"]).reshape(KF, 128).T),
        "w_in": f(inp["w_in"])[0], "w_pool": f(inp["w_pool"])[0].reshape(1024, 256),
        "w_ua": f(inp["w_up_attn"])[0], "w_up": f(inp["w_up_pool"])[0], "w_out": f(inp["w_out"])[0],
        "w_fi": f(inp["w_ffn_in"])[0], "w_fo": f(inp["w_ffn_out"])[0],
    }
    in_maps = []
    for c in range(NCORE):
        xs = np.zeros((NSL * 128, D), np.float32)
        ntrue = (16 * c + 16) * 128
        xs[NPT * 128 - ntrue:NPT * 128] = x_prompt[0:ntrue]
        sm = xs[NPT * 128:]
        sm[0:16] = x_sample[2 * c]
        sm[32:48] = x_sample[2 * c + 1]
        if c > 0:
            sm[64:81] = x_prompt[2048 * c - 17:2048 * c]
        kvalid = np.zeros((128, NSL), np.float32)
        kvalid[:, NPT - 16 * (c + 1):] = 1.0
        invcnt = np.zeros((4, 16), np.float32)
        for wi, w in enumerate((2, 4, 8, 16)):
            for t in range(16):
                invcnt[wi, t] = (1.0 / min(w, t + 1)) if c == 0 else 1.0 / w
        m = dict(shared)
        m.update({
            "xs": xs, "kvalid": kvalid,
            "ck": np.ascontiguousarray(cache_k[2 * c:2 * c + 2]), "cv": np.ascontiguousarray(cache_v[2 * c:2 * c + 2]),
            "sp_in": np.ascontiguousarray(state_pool[2 * c:2 * c + 2]),
            "sc_in": np.ascontiguousarray(state_conv[2 * c:2 * c + 2]),
            "invcnt": invcnt.reshape(1, 64),
            "hvalid": np.full((128, 1), 0.0 if c == 0 else 1.0, np.float32),
        })
        in_maps.append(m)
    if _prep_only:
        return in_maps
    if "nc" not in _NC_CACHE:
        _NC_CACHE["nc"] = build()
    res = run_bass_kernel_spmd(_NC_CACHE["nc"], in_maps, core_ids=list(range(NCORE)))
    R = res.results
    cat = lambda k: np.concatenate([np.asarray(R[c][k]) for c in range(NCORE)], axis=0)
    y_p = cat("y_p").reshape(1, 16384, D)
    y_s = cat("y_s").reshape(16, 16, D)
    k_p = cat("k_p").reshape(1, 1, 16384, 8, 128)
    v_p = cat("v_p").reshape(1, 1, 16384, 8, 128)
    pool_p = np.asarray(R[NCORE - 1]["pool_p"]).reshape(1, 1, 15, 1024)
    conv_p = np.asarray(R[NCORE - 1]["conv_p"]).reshape(1, 1, 2, DFF)
    k_s = cat("k_s").reshape(1, 16, 16, 8, 128)
    v_s = cat("v_s").reshape(1, 16, 16, 8, 128)
    pool_s = cat("pool_s").reshape(1, 16, 15, 1024)
    conv_s = cat("conv_s").reshape(1, 16, 2, DFF)
    return tuple(np.ascontiguousarray(a, dtype=np.float32) for a in
                 (y_p, y_s, k_p, v_p, pool_p, conv_p, k_s, v_s, pool_s, conv_s))
```
